# Optimizing a Trainium2 kernel written in Bass

```python
import jax, jax.numpy as jnp
from jax import lax
import numpy as np

D_MODEL = 1024
BATCH = 8
SEQ = 2048
DEPTH = 4
DEC_BATCH = 128
DEC_SEQ = 8
PAST_LEN = 16384
PAGE_SIZE = 128

N_EVEN = (DEPTH + 1) // 2
N_ODD = DEPTH // 2
CHUNK = 128
A_HEADS = 4
A_WIDTH = D_MODEL // 2
A_HEAD_DIM = A_WIDTH // A_HEADS
B_GROUPS = 4
B_WIDTH = D_MODEL // 2
B_CONV = 3
C_WIDTH = D_MODEL // 2
C_CONV = 31
D_WIDTH = D_MODEL // 2
POOL_WINDOWS = (2, 4, 8, 16)
D_GROUPS = len(POOL_WINDOWS)
D_GROUP_DIM = D_WIDTH // D_GROUPS
POOL_BUF = max(POOL_WINDOWS) - 1
D_FF = ((8 * D_MODEL // 3 + 127) // 128) * 128
FFN_CONV = 3
EVEN_IN = 2 * A_WIDTH + 3 * B_WIDTH
ODD_IN = 2 * C_WIDTH + D_WIDTH
EVEN_OUT = A_WIDTH + B_WIDTH
ODD_OUT = C_WIDTH + D_WIDTH
EPS = 1e-6

kernel_name = "hybrid_chunkmlp_shortconv_conformer_pool_decoder_step"


def rms_norm(x, g):
    xf = x.astype(jnp.float32)
    y = xf * lax.rsqrt(jnp.mean(xf * xf, axis=-1, keepdims=True) + EPS)
    return (y * g.astype(jnp.float32)).astype(x.dtype)


def layer_norm(x, g, b=None):
    xf = x.astype(jnp.float32)
    mu = jnp.mean(xf, axis=-1, keepdims=True)
    xc = xf - mu
    y = xc * lax.rsqrt(jnp.mean(xc * xc, axis=-1, keepdims=True) + EPS) * g.astype(jnp.float32)
    if b is not None:
        y = y + b.astype(jnp.float32)
    return y.astype(x.dtype)


def causal_dwconv(x, buf, w, b=None):
    k = w.shape[0]
    xp = jnp.concatenate([buf.astype(x.dtype), x], axis=1)
    y = lax.conv_general_dilated(
        xp, w[:, None, :].astype(x.dtype), window_strides=(1,), padding='VALID',
        dimension_numbers=('NWC', 'WIO', 'NWC'), feature_group_count=x.shape[-1])
    if b is not None:
        y = y + b.astype(y.dtype)
    return y, xp[:, xp.shape[1] - (k - 1):]


def chunk_spatial_gate(u, v, ws, bs):
    n, l, h, p = v.shape
    lc = min(l, CHUNK)
    nc = l // lc
    w = jnp.tril(ws[:, :lc, :lc]).astype(v.dtype)
    vc = v.reshape(n, nc, lc, h, p)
    mixed = jnp.einsum('hts,ncshp->ncthp', w, vc)
    mixed = mixed + bs[:, :lc].T.astype(v.dtype)[None, None, :, :, None]
    return u * mixed.reshape(n, l, h, p)


def multi_scale_pool(p, buf, start_pos, proj, scale):
    n, l, _ = p.shape
    xp = jnp.concatenate([buf.astype(p.dtype), p], axis=1)
    cs = jnp.cumsum(xp.astype(jnp.float32), axis=1)
    cs = jnp.concatenate([jnp.zeros_like(cs[:, :1]), cs], axis=1)
    pos = start_pos + jnp.arange(l, dtype=jnp.int32)
    hi = cs[:, POOL_BUF + 1:]
    outs = []
    for gi, win in enumerate(POOL_WINDOWS):
        sl = slice(gi * D_GROUP_DIM, (gi + 1) * D_GROUP_DIM)
        lo = cs[:, POOL_BUF + 1 - win:POOL_BUF + 1 - win + l, sl]
        cnt = jnp.minimum(pos + 1, win).astype(jnp.float32)[None, :, None]
        outs.append((hi[:, :, sl] - lo) / cnt)
    pooled = jnp.stack(outs, axis=2).astype(p.dtype)
    diff = pooled - p.reshape(n, l, D_GROUPS, D_GROUP_DIM)
    y = jnp.einsum('nlgc,gcd->nlgd', diff, proj).reshape(n, l, D_WIDTH) * scale
    return y, xp[:, xp.shape[1] - POOL_BUF:]


def even_mixer(xn, buf_b, w_in, w_out, a_ln_g, a_ws, a_bs, b_conv_w):
    n, l, _ = xn.shape
    z = xn @ w_in
    u, v, bg, cg, h = jnp.split(
        z, [A_WIDTH, 2 * A_WIDTH, 2 * A_WIDTH + B_WIDTH, 2 * A_WIDTH + 2 * B_WIDTH], axis=-1)
    u = jax.nn.gelu(u)
    v = layer_norm(jax.nn.gelu(v), a_ln_g)
    ya = chunk_spatial_gate(u.reshape(n, l, A_HEADS, A_HEAD_DIM),
                            v.reshape(n, l, A_HEADS, A_HEAD_DIM), a_ws, a_bs).reshape(n, l, A_WIDTH)
    conv_out, new_b = causal_dwconv(cg * h, buf_b, b_conv_w)
    yb = bg * conv_out
    y = jnp.concatenate([ya, yb], axis=-1) @ w_out
    return y, v, new_b


def odd_mixer(xn, buf_c, buf_d, start_pos, w_in, w_out, c_conv_w, c_conv_b, c_ln_g, c_ln_b,
              d_proj, d_scale):
    z = xn @ w_in
    ca, cb, p = jnp.split(z, [C_WIDTH, 2 * C_WIDTH], axis=-1)
    g = ca * jax.nn.sigmoid(cb)
    cc, new_c = causal_dwconv(g, buf_c, c_conv_w, c_conv_b)
    yc = jax.nn.silu(layer_norm(cc, c_ln_g, c_ln_b))
    yd, new_d = multi_scale_pool(p, buf_d, start_pos, d_proj, d_scale)
    y = jnp.concatenate([yc, yd], axis=-1) @ w_out
    return y, new_c, new_d


def conv_ffn(xn, buf, w_in, conv_w, w_out):
    z = xn @ w_in
    gate, up = jnp.split(z, [D_FF], axis=-1)
    gc, new_buf = causal_dwconv(gate, buf, conv_w)
    return (jax.nn.gelu(gc) * up) @ w_out, new_buf


def trunk(x, buf_b, buf_c, buf_d, buf_f, start_pos, prm):
    new_a, new_b, new_c, new_d, new_f = [], [], [], [], []
    for layer in range(DEPTH):
        i = layer // 2
        xn = rms_norm(x, prm['norm_mix_g'][layer])
        if layer % 2 == 0:
            y, v, nb = even_mixer(xn, buf_b[i], prm['w_in_even'][i], prm['w_out_even'][i],
                                  prm['a_ln_g'][i], prm['a_ws'][i], prm['a_bs'][i], prm['b_conv_w'][i])
            new_a.append(v)
            new_b.append(nb)
        else:
            y, nc, nd = odd_mixer(xn, buf_c[i], buf_d[i], start_pos, prm['w_in_odd'][i],
                                  prm['w_out_odd'][i], prm['c_conv_w'][i], prm['c_conv_b'][i],
                                  prm['c_ln_g'][i], prm['c_ln_b'][i], prm['d_proj'][i], prm['d_scale'][i])
            new_c.append(nc)
            new_d.append(nd)
        x = x + y
        xn = rms_norm(x, prm['norm_ffn_g'][layer])
        y, nf = conv_ffn(xn, buf_f[layer], prm['w_ffn_in'][layer], prm['ffn_conv_w'][layer],
                         prm['w_ffn_out'][layer])
        new_f.append(nf)
        x = x + y
    out = rms_norm(x, prm['norm_final_g'])
    return out, jnp.stack(new_a), jnp.stack(new_b), jnp.stack(new_c), jnp.stack(new_d), jnp.stack(new_f)


def setup_inputs(seed: int = 0) -> dict:
    key = jax.random.key(seed)
    ks = iter(jax.random.split(key, 32))

    def nrm(shape, scale):
        return jax.random.normal(next(ks), shape, jnp.float32) * scale

    inp = {}
    inp['x_prompt'] = nrm((BATCH, SEQ, D_MODEL), 1.0)
    inp['x_sample'] = nrm((DEC_BATCH, DEC_SEQ, D_MODEL), 1.0)
    inp['state_b_conv'] = nrm((N_EVEN, DEC_BATCH, B_CONV - 1, B_WIDTH), 1.0)
    inp['state_c_conv'] = nrm((N_ODD, DEC_BATCH, C_CONV - 1, C_WIDTH), 0.5)
    inp['state_d_pool'] = nrm((N_ODD, DEC_BATCH, POOL_BUF, D_WIDTH), 1.0)
    inp['state_ffn_conv'] = nrm((DEPTH, DEC_BATCH, FFN_CONV - 1, D_FF), 1.0)
    inp['norm_mix_g'] = 1.0 + nrm((DEPTH, D_MODEL), 0.05)
    inp['norm_ffn_g'] = 1.0 + nrm((DEPTH, D_MODEL), 0.05)
    inp['norm_final_g'] = 1.0 + nrm((D_MODEL,), 0.05)
    inp['w_in_even'] = nrm((N_EVEN, D_MODEL, EVEN_IN), D_MODEL ** -0.5)
    inp['w_out_even'] = nrm((N_EVEN, EVEN_OUT, D_MODEL), EVEN_OUT ** -0.5)
    inp['a_ln_g'] = 1.0 + nrm((N_EVEN, A_WIDTH), 0.05)
    inp['a_ws'] = nrm((N_EVEN, A_HEADS, CHUNK, CHUNK), CHUNK ** -0.5)
    inp['a_bs'] = 1.0 + nrm((N_EVEN, A_HEADS, CHUNK), 0.1)
    inp['b_conv_w'] = nrm((N_EVEN, B_CONV, B_WIDTH), B_CONV ** -0.5)
    inp['w_in_odd'] = nrm((N_ODD, D_MODEL, ODD_IN), D_MODEL ** -0.5)
    inp['w_out_odd'] = nrm((N_ODD, ODD_OUT, D_MODEL), ODD_OUT ** -0.5)
    inp['c_conv_w'] = nrm((N_ODD, C_CONV, C_WIDTH), C_CONV ** -0.5)
    inp['c_conv_b'] = nrm((N_ODD, C_WIDTH), 0.02)
    inp['c_ln_g'] = 1.0 + nrm((N_ODD, C_WIDTH), 0.05)
    inp['c_ln_b'] = nrm((N_ODD, C_WIDTH), 0.02)
    inp['d_proj'] = nrm((N_ODD, D_GROUPS, D_GROUP_DIM, D_GROUP_DIM), D_GROUP_DIM ** -0.5)
    inp['d_scale'] = 1.0 + nrm((N_ODD, D_WIDTH), 0.1)
    inp['w_ffn_in'] = nrm((DEPTH, D_MODEL, 2 * D_FF), D_MODEL ** -0.5)
    inp['ffn_conv_w'] = nrm((DEPTH, FFN_CONV, D_FF), FFN_CONV ** -0.5)
    inp['w_ffn_out'] = nrm((DEPTH, D_FF, D_MODEL), D_FF ** -0.5)
    return inp


def reference(x_prompt, x_sample, state_b_conv, state_c_conv, state_d_pool, state_ffn_conv,
              norm_mix_g, norm_ffn_g, norm_final_g, w_in_even, w_out_even, a_ln_g, a_ws, a_bs,
              b_conv_w, w_in_odd, w_out_odd, c_conv_w, c_conv_b, c_ln_g, c_ln_b, d_proj, d_scale,
              w_ffn_in, ffn_conv_w, w_ffn_out):
    prm = dict(norm_mix_g=norm_mix_g, norm_ffn_g=norm_ffn_g, norm_final_g=norm_final_g,
               w_in_even=w_in_even, w_out_even=w_out_even, a_ln_g=a_ln_g, a_ws=a_ws, a_bs=a_bs,
               b_conv_w=b_conv_w, w_in_odd=w_in_odd, w_out_odd=w_out_odd, c_conv_w=c_conv_w,
               c_conv_b=c_conv_b, c_ln_g=c_ln_g, c_ln_b=c_ln_b, d_proj=d_proj, d_scale=d_scale,
               w_ffn_in=w_ffn_in, ffn_conv_w=ffn_conv_w, w_ffn_out=w_ffn_out)
    dt = x_prompt.dtype
    zb = jnp.zeros((N_EVEN, BATCH, B_CONV - 1, B_WIDTH), dt)
    zc = jnp.zeros((N_ODD, BATCH, C_CONV - 1, C_WIDTH), dt)
    zd = jnp.zeros((N_ODD, BATCH, POOL_BUF, D_WIDTH), dt)
    zf = jnp.zeros((DEPTH, BATCH, FFN_CONV - 1, D_FF), dt)
    y_prompt, _, b_p, c_p, d_p, f_p = trunk(x_prompt, zb, zc, zd, zf, 0, prm)
    y_sample, a_s, b_s, c_s, d_s, f_s = trunk(x_sample, state_b_conv, state_c_conv, state_d_pool,
                                              state_ffn_conv, PAST_LEN, prm)
    return (y_prompt, y_sample, a_s, b_p, b_s, c_p, c_s, d_p, d_s, f_p, f_s)
```

```python
import numpy as np
from contextlib import ExitStack
import concourse.bass as bass
import concourse.mybir as mybir
from concourse.bass_utils import run_bass_kernel_spmd

F32 = mybir.dt.float32
BF16 = mybir.dt.bfloat16
AF = mybir.ActivationFunctionType
ALU = mybir.AluOpType

D = 1024
KC = 8
DEPTH = 4
DFF = 2816
NJ = 22
NPP = 1024
NTM = 1152
HO = 32
SH = 30
SW = 40
EPS = 1e-6
NSLOT = 14
NCORES = 8

V_GMIX = 0
V_GFFN = 32
V_GFIN = 64
V_BW = 72
V_CW = 96
V_CB = 344
V_CLG = 352
V_CLB = 360
V_DSC = 368
V_FW = 376
V_ALG = 640
NV = 648


class Buf:
    _n = 0

    def __init__(self, t, free_shape, cell, name):
        self.t = t
        self.shape = list(free_shape)
        self.cell = cell
        self.strides = []
        s = 1
        for d in reversed(self.shape):
            self.strides.insert(0, s)
            s *= d
        self.size = s
        self.id = Buf._n
        Buf._n += 1
        self.name = name
        if len(self.shape) == 1:
            self._flat = None
        else:
            names = " ".join(f"d{i}" for i in range(len(self.shape)))
            self._flat = f"p {names} -> p ({names})"

    def __call__(self, *idx, p=None):
        idx = list(idx) + [None] * (len(self.shape) - len(idx))
        sl = []
        lo = 0
        hi = 0
        for i, n, st in zip(idx, self.shape, self.strides):
            if i is None:
                a, b = 0, n
                sl.append(slice(None))
            elif isinstance(i, tuple):
                a, b = i
                assert 0 <= a < b <= n, (self.name, idx, self.shape)
                sl.append(slice(a, b))
            else:
                a, b = i, i + 1
                assert 0 <= a < n, (self.name, idx, self.shape)
                sl.append(i)
            lo += a * st
            hi += (b - 1) * st
        hi += 1
        ps = slice(None) if p is None else slice(p[0], p[1])
        return Acc(self, self.t[tuple([ps] + sl)], lo, hi)

    def flat(self, lo, hi, p=None):
        ps = slice(None) if p is None else slice(p[0], p[1])
        full = self.t[tuple([slice(None)] * (len(self.shape) + 1))]
        if self._flat is not None:
            full = full.rearrange(self._flat)
        return Acc(self, full[ps, lo:hi], lo, hi)


class Acc:
    def __init__(self, buf, ap, lo, hi):
        self.buf = buf
        self.ap = ap
        self.lo = lo
        self.hi = hi

    def cells(self):
        c = self.buf.cell
        return range(self.lo // c, (self.hi - 1) // c + 1)

    def r(self, pattern, **kw):
        return Acc(self.buf, self.ap.rearrange(pattern, **kw), self.lo, self.hi)

    def s3(self):
        return self.r("p (s t) -> p s t", s=16)


class Op:
    __slots__ = ("eng", "fn", "kind", "chan", "chan_idx", "seq", "waits", "signal", "sigcount", "id", "kn")


class Sched:
    ENGS = ("pe", "act", "dve", "pool", "sp")

    def __init__(self):
        self.ops = []
        self.state = {}
        self.eng_ops = {e: [] for e in self.ENGS}
        self.waited = {e: {} for e in self.ENGS}
        self.waited_dma = {e: {} for e in self.ENGS}
        self.chan_count = {}

    def _add_dep(self, op, d):
        if d is None or d is op:
            return
        E = op.eng
        if d.kind == "dma":
            if self.waited_dma[E].get(d.chan, 0) >= d.chan_idx:
                return
            cur = op.waits.get(("dma", d.chan))
            if cur is None or cur.chan_idx < d.chan_idx:
                op.waits[("dma", d.chan)] = d
        else:
            if self.waited[E].get(d.eng, -1) >= d.seq:
                return
            cur = op.waits.get(("eng", d.eng))
            if cur is None or cur.seq < d.seq:
                op.waits[("eng", d.eng)] = d

    def op(self, eng, fn, reads=(), writes=(), dma=False, chan=None):
        o = Op()
        o.eng = eng
        o.fn = fn
        o.kind = "dma" if dma else "cmp"
        o.chan = chan
        o.waits = {}
        o.signal = False
        o.id = len(self.ops)
        o.seq = len(self.eng_ops[eng])
        if dma:
            n = self.chan_count.get(chan, 0) + 1
            self.chan_count[chan] = n
            o.chan_idx = n
            o.signal = True
        for a in reads:
            for c in a.cells():
                st = self.state.get((a.buf.id, c))
                if st is not None and st[0] is not None:
                    self._add_dep(o, st[0])
        for a in writes:
            for c in a.cells():
                st = self.state.get((a.buf.id, c))
                if st is None:
                    continue
                d = st[0]
                if d is not None and (d.kind == "dma" or dma or d.eng != eng):
                    self._add_dep(o, d)
                for r in st[1].values():
                    if r.kind == "dma" or dma or r.eng != eng:
                        self._add_dep(o, r)
        if dma and o.chan_idx > 1:
            cur = o.waits.get(("dma", chan))
            if self.waited_dma[eng].get(chan, 0) < o.chan_idx - 1 and (cur is None or cur.chan_idx < o.chan_idx - 1):
                prev = Op()
                prev.kind = "dma"
                prev.chan = chan
                prev.chan_idx = o.chan_idx - 1
                prev.id = -1
                prev.kn = None
                o.waits[("dma", chan)] = prev
        ke = self.waited[eng]
        kd = self.waited_dma[eng]
        kept = {}
        for key, d in sorted(o.waits.items(), key=lambda kv: -getattr(kv[1], "id", -1)):
            if key[0] == "dma":
                if kd.get(d.chan, 0) >= d.chan_idx:
                    continue
                kd[d.chan] = d.chan_idx
            else:
                if ke.get(d.eng, -1) >= d.seq:
                    continue
                d.signal = True
                ke[d.eng] = d.seq
            kept[key] = d
            kn = getattr(d, "kn", None)
            if kn is not None:
                for k2, v2 in kn[0].items():
                    if ke.get(k2, -1) < v2:
                        ke[k2] = v2
                for k2, v2 in kn[1].items():
                    if kd.get(k2, 0) < v2:
                        kd[k2] = v2
        o.waits = kept
        o.kn = (dict(ke), dict(kd))
        for a in writes:
            for c in a.cells():
                self.state[(a.buf.id, c)] = [o, {}]
        for a in reads:
            for c in a.cells():
                st = self.state.get((a.buf.id, c))
                if st is None:
                    st = [None, {}]
                    self.state[(a.buf.id, c)] = st
                st[1][eng if not dma else ("dma", o.id)] = o
        self.ops.append(o)
        self.eng_ops[eng].append(o)
        return o

    def emit(self, nc, sems, chan_sems):
        for e, lst in self.eng_ops.items():
            n = 0
            for o in lst:
                if o.kind == "cmp" and o.signal:
                    n += 1
                o.sigcount = n
        finals = list(self.chan_count.items())

        class _First:
            def __init__(self, e):
                self._e = e
                self.first = None

            def __getattr__(self, name):
                attr = getattr(self._e, name)
                if not callable(attr):
                    return attr

                def w(*a, **k):
                    r = attr(*a, **k)
                    if self.first is None and r is not None:
                        self.first = r
                    return r
                return w

        def run(engobj, ename):
            proxy = _First(engobj)
            for o in self.eng_ops[ename]:
                waits = list(o.waits.items())
                fuse = None
                if o.kind == "cmp" and waits:
                    fuse = waits.pop()
                for key, d in waits:
                    if key[0] == "dma":
                        engobj.wait_ge(chan_sems[d.chan], 16 * d.chan_idx)
                    else:
                        engobj.wait_ge(sems[d.eng], d.sigcount)
                proxy.first = None
                ins = o.fn(proxy)
                if fuse is not None:
                    key, d = fuse
                    if key[0] == "dma":
                        proxy.first._wait_ge(chan_sems[d.chan], 16 * d.chan_idx)
                    else:
                        proxy.first._wait_ge(sems[d.eng], d.sigcount)
                if o.kind == "dma":
                    ins.then_inc(chan_sems[o.chan], 16)
                elif o.signal:
                    ins.then_inc(sems[o.eng], 1)
            if ename == "sp":
                for c, n in finals:
                    engobj.wait_ge(chan_sems[c], 16 * n)

        with nc.Block() as block:
            @block.tensor
            def _(t):
                run(t, "pe")

            @block.scalar
            def _(a):
                run(a, "act")

            @block.vector
            def _(v):
                run(v, "dve")

            @block.gpsimd
            def _(g):
                run(g, "pool")

            @block.sync
            def _(s):
                run(s, "sp")


class Blk:
    def __init__(self, c0, n, kind):
        self.c0 = c0
        self.n = n
        self.kind = kind
        self.cols = (c0, c0 + n)


class Builder:
    def __init__(self, depth=DEPTH, dbg=None):
        self.depth = depth
        self.dbg = dbg
        self.nc = bass.Bass("TRN2", target_bir_lowering=False)
        self.S = Sched()
        self.es = ExitStack()
        self.dry = False
        self.nchan = 0
        self.gen_chans = {}
        self.rot = {}

    def dram(self, name, shape, kind="ExternalInput", dt=F32):
        return self.nc.dram_tensor(name, list(shape), dt, kind=kind).ap()

    def sb(self, name, shape, dt=F32, cell=128):
        t = self.es.enter_context(self.nc.sbuf_tensor(name, [128] + list(shape), dt))
        return Buf(t, shape, cell, name)

    def psb(self, name):
        t = self.es.enter_context(self.nc.psum_tensor(name, [128, 512], F32))
        return Buf(t, [512], 512, name)

    def rotate(self, key, lst):
        i = self.rot.get(key, 0)
        self.rot[key] = i + 1
        return lst[i % len(lst)]

    def op(self, eng, fn, reads=(), writes=(), **kw):
        if self.dry:
            return None
        return self.S.op(eng, fn, reads=reads, writes=writes, **kw)

    def dma(self, eng, out, in_, reads=(), writes=(), chan=None):
        if self.dry:
            return
        if chan is None:
            chan = self.rotate("genchan_" + eng, self.gen_chans[eng])
        o_ap = out.ap if isinstance(out, Acc) else out
        i_ap = in_.ap if isinstance(in_, Acc) else in_
        rd = list(reads) + ([in_] if isinstance(in_, Acc) else [])
        wr = list(writes) + ([out] if isinstance(out, Acc) else [])
        self.S.op(eng, lambda e: e.dma_start(out=o_ap, in_=i_ap), reads=rd, writes=wr, dma=True, chan=chan)

    def ACT(self, func, out, in_, scale=None, bias=None):
        kw = {}
        rd = [in_]
        if scale is not None:
            if isinstance(scale, Acc):
                kw["scale"] = scale.ap
                rd.append(scale)
            else:
                kw["scale"] = float(scale)
        if bias is not None:
            if isinstance(bias, Acc):
                kw["bias"] = bias.ap
                rd.append(bias)
            else:
                kw["bias"] = float(bias)
        self.op("act", lambda e: e.activation(out=out.ap, in_=in_.ap, func=func, **kw), reads=rd, writes=[out])

    def TT(self, eng, out, a, b, op):
        self.op(eng, lambda e: e.tensor_tensor(out=out.ap, in0=a.ap, in1=b.ap, op=op), reads=[a, b], writes=[out])

    def TS(self, eng, out, a, s1, op0, s2=None, op1=None):
        rd = [a]
        v1 = s1
        v2 = s2
        if isinstance(s1, Acc):
            rd.append(s1)
            v1 = s1.ap
        if isinstance(s2, Acc):
            rd.append(s2)
            v2 = s2.ap
        if op1 is None:
            self.op(eng, lambda e: e.tensor_scalar(out=out.ap, in0=a.ap, scalar1=v1, scalar2=None, op0=op0), reads=rd, writes=[out])
        else:
            self.op(eng, lambda e: e.tensor_scalar(out=out.ap, in0=a.ap, scalar1=v1, scalar2=v2, op0=op0, op1=op1), reads=rd, writes=[out])

    def STT(self, eng, out, a, s, b, op0, op1):
        rd = [a, b]
        v = s
        if isinstance(s, Acc):
            rd.append(s)
            v = s.ap
        self.op(eng, lambda e: e.scalar_tensor_tensor(out=out.ap, in0=a.ap, scalar=v, in1=b.ap, op0=op0, op1=op1), reads=rd, writes=[out])

    def MM(self, out, pairs, extra_reads=()):
        rd = []
        for l, r in pairs:
            rd.append(l)
            rd.append(r)
        rd += list(extra_reads)
        n = len(pairs)

        def fn(e):
            ins = None
            for i, (l, r) in enumerate(pairs):
                ins = e.matmul(out.ap, lhsT=l.ap, rhs=r.ap, start=(i == 0), stop=(i == n - 1))
            return ins
        self.op("pe", fn, reads=rd, writes=[out])

    def MMS(self, items):
        rd = []
        wr = []
        for o, l, r in items:
            rd += [l, r]
            wr.append(o)

        def fn(e):
            ins = None
            for o, l, r in items:
                ins = e.matmul(o.ap, lhsT=l.ap, rhs=r.ap, start=True, stop=True)
            return ins
        self.op("pe", fn, reads=rd, writes=wr)

    def wreset(self):
        self.w_idx = 0
        self.w_issued = 0
        self.w_released = set()
        self.w_oldest = 0

    def wget(self, spec):
        i = self.w_idx
        self.w_idx += 1
        if self.dry:
            self.plan.append(spec)
            return i % NSLOT
        assert i < len(self.plan)
        while self.w_issued <= i:
            assert self.w_issued - self.w_oldest < NSLOT, "ring overflow"
            self._wissue()
        return i % NSLOT

    def _wissue(self):
        i = self.w_issued
        kind, src = self.plan[i]
        slot = i % NSLOT
        if kind == "in":
            dst = self.RING(slot).r("p (k f) -> p k f", k=KC)
        else:
            dst = self.RING(slot)
        self.dma("pool", dst, src, chan=("ring", slot))
        self.w_issued += 1

    def wrel(self, n=1):
        if self.dry:
            return
        self.w_oldest += n
        self.wprefetch()

    def wprefetch(self):
        if self.dry:
            return
        while self.w_issued < len(self.plan) and self.w_issued - self.w_oldest < NSLOT:
            self._wissue()

    def w_in(self, wd, f0):
        return ("in", wd[:, f0:f0 + 128].rearrange("(k p) f -> p k f", p=128))

    def wk(self, slot, k):
        return self.RING(slot, (k * 128, (k + 1) * 128))

    def build(self):
        nc = self.nc
        dram = self.dram
        self.xT = dram("xT", [D, 2176])
        self.yT = dram("yT", [D, 2176], "ExternalOutput")
        self.st_b = dram("st_b", [2, 128, 128])
        self.st_c = dram("st_c", [2, 128, 4, 480])
        self.st_d = dram("st_d", [2, 128, 4, 240])
        self.st_f = dram("st_f", [4, 128, NJ * 32])
        self.w_in_even = dram("w_in_even", [2, D, 2560])
        self.w_out_even = dram("w_out_even", [2, D, D])
        self.w_in_odd = dram("w_in_odd", [2, D, 1536])
        self.w_out_odd = dram("w_out_odd", [2, D, D])
        self.w_ffn_in = dram("w_ffn_in", [4, D, 2 * DFF])
        self.w_ffn_out = dram("w_ffn_out", [4, DFF, D])
        self.d_proj = dram("d_proj", [2, 4, 128, 128])
        self.vecs = dram("vecs", [128, NV])
        self.lng = dram("lng", [2, 512])
        self.bsb = dram("bsb", [2, 4, 256])
        self.wst = dram("wst", [2, 4, 2, 128, 128])
        self.masks = dram("masks", [2, 128, 128])
        self.ident = dram("ident", [128, 128])
        self.invc = dram("invc", [4, 16])
        self.o_av = dram("o_av", [2, 128, 512], "ExternalOutput")
        self.o_bp = dram("o_bp", [2, 128, 8], "ExternalOutput")
        self.o_bs = dram("o_bs", [2, 128, 128], "ExternalOutput")
        self.o_cp = dram("o_cp", [2, 128, 120], "ExternalOutput")
        self.o_cs = dram("o_cs", [2, 128, 4, 480], "ExternalOutput")
        self.o_dp = dram("o_dp", [2, 128, 60], "ExternalOutput")
        self.o_ds = dram("o_ds", [2, 128, 4, 240], "ExternalOutput")
        self.o_fp = dram("o_fp", [4, 128, NJ * 2], "ExternalOutput")
        self.o_fs = dram("o_fs", [4, 128, NJ * 32], "ExternalOutput")
        if self.dbg is not None:
            self.o_dbg = dram("o_dbg", [D, 2176], "ExternalOutput")

        sb = self.sb
        self.X = sb("X", [KC, NTM])
        self.XN = sb("XN", [KC, NTM], BF16)
        self.YH = sb("YH", [KC, NTM], BF16)
        self.R = [sb(f"R{i}", [HO + NPP]) for i in range(3)]
        self.RB = [sb(f"RB{i}", [HO + NPP], BF16) for i in range(2)]
        self.SR = [sb(f"SR{i}", [16, SW], cell=16 * SW) for i in range(3)]
        self.SRB = [sb(f"SRB{i}", [16, SW], BF16, cell=16 * SW) for i in range(2)]
        self.TA = [sb(f"TA{i}", [512], cell=512) for i in range(4)]
        self.TD = [sb(f"TD{i}", [512], cell=512) for i in range(3)]
        self.SQ = [sb(f"SQ{i}", [512], BF16, cell=512) for i in range(2)]
        self.SQ2 = [sb(f"SQ2_{i}", [512], BF16, cell=512) for i in range(2)]
        self.RING = sb("RING", [NSLOT, 1024], BF16, cell=1024)
        self.DIAG = [sb(f"DIAG{i}", [31, 128], BF16, cell=31 * 128) for i in range(2)]
        self.CC = sb("CC", [4, NTM])
        self.VEC = sb("VEC", [NV], cell=8)
        self.LNG = sb("LNG", [512], cell=512)
        self.BSB = sb("BSB", [4, 256], cell=256)
        self.WST = sb("WST", [2, 4, 2, 128], BF16, cell=128)
        self.MSK = sb("MSK", [2, 128], cell=128)
        self.IDENT = sb("IDENT", [128], cell=128)
        self.ONES = sb("ONES", [128], BF16, cell=128)
        self.NEGH = sb("NEGH", [8], cell=8)
        self.ONESF = sb("ONESF", [128], cell=128)
        self.R4 = [sb(f"R4_{i}", [8], cell=8) for i in range(8)]
        self.INVC = sb("INVC", [4, 16], cell=64)
        self.DPROJ = sb("DPROJ", [2, 4, 128], BF16, cell=128)
        self.CB = sb("CB", [2, 4, 2], cell=2)
        self.CC2 = sb("CC2", [2, 4, 30], BF16, cell=30)
        self.CD = sb("CD", [2, 4, 15], cell=15)
        self.CF = sb("CF", [4, NJ, 2], cell=2)
        self.GT = sb("GT", [2, 4, 30], cell=30)
        self.SI = sb("SI", [480], cell=480)
        self.SO = sb("SO", [480], cell=480)
        self.FSI = sb("FSI", [NJ, 32], cell=32)
        self.FSO = sb("FSO", [NJ, 32], cell=32)
        self.STT6 = [sb(f"ST6_{i}", [8], cell=8) for i in range(3)]
        self.MV = [sb(f"MV{i}", [4], cell=4) for i in range(3)]
        self.PS = [self.psb(f"PS{i}") for i in range(8)]
        self.PSIN = self.PS[0:5]
        self.PSAUX = self.PS[5:8]

        self.gen_chans = {"sp": [("g", i) for i in range(24)], "pool": [("q", i) for i in range(12)]}

        self.plan = []
        self.dry = True
        self.wreset()
        self.program()
        self.dry = False
        self.rot = {}
        self.wreset()
        self.program()

        sems = {e: self.es.enter_context(nc.semaphore("s_" + e)) for e in ("pe", "act", "dve", "pool")}
        chans = {}
        for c in list(self.S.chan_count.keys()):
            chans[c] = self.es.enter_context(nc.semaphore("c_%s_%s" % c))
        self.S.emit(nc, sems, chans)
        self.es.close()
        return nc

    def vec(self, c):
        return self.VEC((c, c + 1))

    def program(self):
        self.setup()
        for ps in range(2):
            self.run_pass(ps)

    def setup(self):
        dma = self.dma
        dma("sp", self.VEC(), self.vecs[:, :])
        dma("sp", self.MSK(), self.masks.rearrange("m p f -> p m f"))
        dma("sp", self.IDENT(), self.ident[:, :])
        dma("sp", self.INVC(), self.invc.partition_broadcast(128))
        dma("pool", self.DPROJ(), self.d_proj.rearrange("i g c d -> c i g d"))
        self.op("dve", lambda e: e.memset(self.ONES().ap, 1.0), writes=[self.ONES()])
        self.op("dve", lambda e: e.memset(self.NEGH().ap, -0.5), writes=[self.NEGH()])
        self.op("dve", lambda e: e.memset(self.ONESF().ap, 1.0), writes=[self.ONESF()])
        for b in (self.CB, self.CD, self.CF, self.CC2):
            self.op("dve", lambda e, b=b: e.memset(b().ap, 0.0), writes=[b()])
        g = self.VEC((V_GMIX, V_GFIN + 8))
        self.TS("dve", g, g, 32.0, ALU.mult)
        h = self.VEC((V_CLG, V_CLB + 8))
        self.TS("dve", h, h, 0.5, ALU.mult)
        stg = self.CC.flat(0, 2048)
        dma("sp", Acc(self.CC, stg.ap.rearrange("p (a t) -> p a t", a=16), 0, 2048), self.wst.rearrange("i h k s t -> s (i h k) t"))
        v5 = stg.ap.rearrange("p (i h k t) -> p i h k t", i=2, h=4, k=2)
        for i in range(2):
            for kd in range(2):
                src = Acc(self.CC, v5[:, i, :, kd, :], 0, 2048)
                m0 = self.MSK(kd)
                mb = Acc(self.MSK, m0.ap.rearrange("p (o t) -> p o t", o=1).broadcast_to([128, 4, 128]), m0.lo, m0.hi)
                self.TT("dve", self.WST(i, None, kd), src, mb, ALU.mult)

    def blocks(self, ps):
        b = [Blk(0, 512, "p"), Blk(512, 512, "p")]
        if ps == 1:
            b.append(Blk(1024, 128, "s"))
        return b

    def run_pass(self, ps):
        self.ps = ps
        blks = self.blocks(ps)
        self.blks = blks
        nt = blks[-1].c0 + blks[-1].n
        self.nt = nt
        for b in blks:
            for k in range(KC):
                if b.kind == "p":
                    g0 = ps * NPP + b.c0
                else:
                    g0 = 2048
                self.dma("sp" if ps == 0 else "pool", self.X(k, b.cols), self.xT[k * 128:(k + 1) * 128, g0:g0 + b.n])
        self.pending = {}
        self.lookahead = 1
        for bi, b in enumerate(blks):
            b.index = bi
            self.pending[bi] = (lambda b=b: self.norm_block(V_GMIX + 0, b))
        for l in range(self.depth):
            if l % 2 == 0:
                self.even_mixer(l)
            else:
                self.odd_mixer(l)
            self.ffn(l)
        if self.dbg is not None:
            for k in range(KC):
                self.dma("sp", self.o_dbg[k * 128:(k + 1) * 128, ps * NPP:(ps + 1) * NPP], self.X(k, (0, NPP)))
                if ps == 1:
                    self.dma("sp", self.o_dbg[k * 128:(k + 1) * 128, 2048:2176], self.X(k, (NPP, NPP + 128)))

    def bv(self, acc, b):
        return acc.s3() if b.kind == "s" else acc

    def tile(self, pool, key, b):
        t = self.rotate(key, pool)((0, b.n))
        return self.bv(t, b)

    def pst(self, lst, key, b):
        t = self.rotate(key, lst)((0, b.n))
        return self.bv(t, b)

    def xcols(self, buf, k, b):
        return self.bv(buf(k, b.cols), b)

    def row(self, R, SRb, b, shift):
        if b.kind == "p":
            return R((HO + b.c0 - shift, HO + b.c0 - shift + b.n))
        return SRb(None, (SH - shift, SH + 8 - shift))

    def bcast_rows(self, r4, ntt, n):
        rep = self.rotate("ta", self.TA)((0, n))
        for tt in range(ntt):
            self.ACT(AF.Identity, th_slice(rep, tt * 128, (tt + 1) * 128), self.ONESF(), scale=th_slice(r4, tt, tt + 1))
        pr = self.rotate("aux", self.PSAUX)((0, n))

        def fnt(e, pr=pr, rep=rep, ntt=ntt):
            ins = None
            for tt in range(ntt):
                ins = e.transpose(pr.ap[:, tt * 128:(tt + 1) * 128], rep.ap[:, tt * 128:(tt + 1) * 128], self.IDENT().ap)
            return ins
        self.op("pe", fnt, reads=[rep, self.IDENT()], writes=[pr])
        return pr

    def norm_block(self, gcol, b, final=False):
        n = b.n
        ntt = n // 128
        pbank = self.rotate("aux", self.PSAUX)
        pss = pbank((0, ntt))
        for k in range(KC):
            sq = self.rotate("sq", self.SQ)((0, n))
            self.ACT(AF.Square, sq, self.X(k, b.cols))

            def fn(e, k=k, sq=sq, pbank=pbank, ntt=ntt):
                ins = None
                for tt in range(ntt):
                    ins = e.matmul(pbank((tt, tt + 1)).ap, lhsT=sq.ap[:, tt * 128:(tt + 1) * 128], rhs=self.ONES((0, 1)).ap,
                                   start=(k == 0 and tt == 0), stop=(k == KC - 1), skip_group_check=True)
                return ins
            self.op("pe", fn, reads=[sq, self.ONES()], writes=[pss])
        r4 = self.rotate("r4", self.R4)((0, ntt))
        self.TS("dve", r4, pss, 1024.0 * EPS, ALU.add)
        self.TT("pool", r4, r4, self.NEGH((0, ntt)), ALU.pow)
        rs = self.bcast_rows(r4, ntt, n)
        for k in range(KC):
            if not final:
                self.STT("dve", self.XN(k, b.cols), self.X(k, b.cols), self.vec(gcol + k), rs, ALU.mult, ALU.mult)
            else:
                o = self.rotate("ta", self.TA)((0, n))
                self.STT("dve", o, self.X(k, b.cols), self.vec(gcol + k), rs, ALU.mult, ALU.mult)
                if b.kind == "p":
                    c = self.ps * NPP + b.c0
                else:
                    c = 2048
                self.dma("sp", self.yT[k * 128:(k + 1) * 128, c:c + n], o)

    def rotate_last(self, key, lst):
        i = self.rot.get(key, 0)
        return lst[(i - 1) % len(lst)]

    def need(self, b):
        for bi in sorted(self.pending.keys()):
            if bi <= b.index + self.lookahead:
                fn = self.pending.pop(bi)
                fn()

    def need_all(self):
        for bi in sorted(self.pending.keys()):
            self.pending.pop(bi)()

    def next_norm(self, l, b):
        if l == self.depth - 1:
            self.norm_block(V_GFIN, b, final=True)
        else:
            self.norm_block(V_GMIX + 8 * (l + 1), b)

    def out_proj(self, wd, after):
        slots = [self.wget(self.w_in(wd, m * 128)) for m in range(KC)]
        prev = None
        for b in self.blks:
            for m in range(KC):
                po = self.pst(self.PSAUX, "aux", b)
                self.MM(po, [(self.wk(slots[m], k), self.xcols(self.YH, k, b)) for k in range(KC)])
                xm = self.xcols(self.X, m, b)
                self.TT("dve", xm, xm, po, ALU.add)
            if prev is not None:
                after(prev)
            prev = b
        after(prev)
        self.wrel(KC)

    def even_mixer(self, l):
        i = l // 2
        ps = self.ps
        blks = self.blks
        wd = self.w_in_even[i]
        self.dma("pool", self.BSB(), self.bsb[i].partition_broadcast(128))
        self.dma("pool", self.LNG(), self.lng[i].partition_broadcast(128))
        if ps == 1:
            self.dma("pool", self.SI((0, 128)), self.st_b[i])
            self.dma("pool", self.FSI(), self.st_f[l].rearrange("p (j f) -> p j f", j=NJ))
        self.wprefetch()
        sv = [self.wget(self.w_in(wd, 512 + q * 128)) for q in range(4)]
        ntile = self.nt // 128
        VOFF = 4 * NTM

        def vacc(tt, c0, c1, p=None):
            return self.YH.flat(VOFF + tt * 512 + c0, VOFF + tt * 512 + c1)
        for tt in range(ntile):
            self.need(blks[min(tt // 4, len(blks) - 1)])
            pv = self.rotate("pin", self.PSIN)
            runs = []
            q = 0
            while q < 4:
                L = 1
                while q + L < 4 and sv[q + L] == sv[q + L - 1] + 1:
                    L += 1
                runs.append((q, L))
                q += L
            for (q0, L) in runs:
                outv = pv((q0 * 128, (q0 + L) * 128)).r("p (s f) -> p s f", s=L)
                self.MM(outv, [(self.XN(k, (tt * 128, (tt + 1) * 128)),
                                self.RING((sv[q0], sv[q0] + L), (k * 128, (k + 1) * 128))) for k in range(KC)])
            gv = self.rotate("ta", self.TA)()
            self.ACT(AF.Gelu_apprx_tanh, gv, pv())
            st6 = self.rotate("st6", self.STT6)
            mv = self.rotate("mv", self.MV)
            self.op("dve", lambda e, st6=st6, gv=gv: e.bn_stats(out=st6((0, 6)).ap, in_=gv.ap), reads=[gv], writes=[st6()])
            self.op("dve", lambda e, st6=st6, mv=mv: e.bn_aggr(out=mv((0, 2)).ap, in_=st6((0, 6)).ap), reads=[st6()], writes=[mv()])
            self.TS("dve", mv((2, 3)), mv((1, 2)), EPS, ALU.add)
            self.TT("pool", mv((3, 4)), mv((2, 3)), self.NEGH((0, 1)), ALU.pow)
            is_s = (ps == 1 and tt == ntile - 1)
            if is_s:
                vn = self.rotate("td", self.TD)()
                self.TS("dve", vn, gv, mv((0, 1)), ALU.subtract, mv((3, 4)), ALU.mult)
                vf = self.rotate("td", self.TD)()
                self.TT("dve", vf, vn, self.LNG(), ALU.mult)
                self.dma("sp", self.o_av[i], vf)
                self.ACT(AF.Copy, vacc(tt, 0, 512), vn)
            else:
                self.TS("dve", vacc(tt, 0, 512), gv, mv((0, 1)), ALU.subtract, mv((3, 4)), ALU.mult)
        self.wrel(4)
        self.lookahead = 0
        for h in range(4):
            su = self.wget(self.w_in(wd, h * 128))
            for b in blks:
                pu = self.pst(self.PSIN, "pin", b)
                self.MM(pu, [(self.wk(su, k), self.xcols(self.XN, k, b)) for k in range(KC)])
                u = self.tile(self.TA, "ta", b)
                self.ACT(AF.Gelu_apprx_tanh, u, pu)
                pmb = self.rotate("aux", self.PSAUX)
                kd = 1 if b.kind == "s" else 0
                items = []
                for q in range(b.n // 128):
                    tt = b.c0 // 128 + q
                    items.append((pmb((q * 128, (q + 1) * 128)), vacc(tt, h * 128, (h + 1) * 128), self.WST(i, h, kd)))
                self.MMS(items)
                pm = self.bv(pmb((0, b.n)), b)
                tmp = self.tile(self.TD, "td", b)
                if b.kind == "p":
                    b0 = self.BSB(h, (0, 128))
                    bs = Acc(self.BSB, b0.ap.rearrange("p (o t) -> p o t", o=1).broadcast_to([128, 4, 128]), b0.lo, b0.hi)
                    self.STT("dve", tmp.r("p (q t) -> p q t", q=4), pm.r("p (q t) -> p q t", q=4), self.vec(V_ALG + i * 4 + h), bs, ALU.mult, ALU.add)
                else:
                    bs = self.BSB(h, (128, 256)).s3()
                    self.STT("dve", tmp, pm, self.vec(V_ALG + i * 4 + h), bs, ALU.mult, ALU.add)
                self.TT("dve", self.xcols(self.YH, h, b), tmp, u, ALU.mult)
            self.wrel(1)
        for c in range(4):
            scg = self.wget(self.w_in(wd, 1536 + c * 128))
            sh = self.wget(self.w_in(wd, 2048 + c * 128))
            sbg = self.wget(self.w_in(wd, 1024 + c * 128))
            R = self.rotate("r", self.R)
            SRb = self.rotate("sr", self.SR)
            self.ACT(AF.Copy, R((HO - 2, HO)), self.CB(i, c))
            if ps == 1:
                self.ACT(AF.Copy, SRb(None, (SH - 2, SH)), self.SI((c * 32, (c + 1) * 32)).r("p (s r) -> p s r", s=16))
            w0 = self.vec(V_BW + (i * 3 + 0) * 4 + c)
            w1 = self.vec(V_BW + (i * 3 + 1) * 4 + c)
            w2 = self.vec(V_BW + (i * 3 + 2) * 4 + c)
            for b in blks:
                pcg = self.pst(self.PSIN, "pin", b)
                self.MM(pcg, [(self.wk(scg, k), self.xcols(self.XN, k, b)) for k in range(KC)])
                ph = self.pst(self.PSIN, "pin", b)
                self.MM(ph, [(self.wk(sh, k), self.xcols(self.XN, k, b)) for k in range(KC)])
                pbg = self.pst(self.PSIN, "pin", b)
                self.MM(pbg, [(self.wk(sbg, k), self.xcols(self.XN, k, b)) for k in range(KC)])
                cg = self.tile(self.TA, "ta", b)
                self.ACT(AF.Copy, cg, pcg)
                self.TT("dve", self.row(R, SRb, b, 0), cg, ph, ALU.mult)
                t0 = self.tile(self.TA, "ta", b)
                self.ACT(AF.Identity, t0, self.row(R, SRb, b, 0), scale=w2)
                t1 = self.tile(self.TD, "td", b)
                self.STT("dve", t1, self.row(R, SRb, b, 1), w1, t0, ALU.mult, ALU.add)
                self.STT("dve", t1, self.row(R, SRb, b, 2), w0, t1, ALU.mult, ALU.add)
                self.TT("dve", self.xcols(self.YH, 4 + c, b), t1, pbg, ALU.mult)
            self.ACT(AF.Copy, self.CB(i, c), R((HO + NPP - 2, HO + NPP)))
            if ps == 1:
                self.ACT(AF.Copy, self.SO((c * 32, (c + 1) * 32)).r("p (s r) -> p s r", s=16), SRb(None, (SH + 6, SH + 8)))
            self.wrel(3)
        if ps == 1:
            self.dma("sp", self.o_bp[i], self.CB(i).r("p c r -> p (c r)"))
            self.dma("sp", self.o_bs[i], self.SO((0, 128)))
        self.out_proj(self.w_out_even[i], lambda b: self.norm_block(V_GFFN + 8 * l, b))

    def odd_mixer(self, l):
        i = l // 2
        ps = self.ps
        blks = self.blks
        wd = self.w_in_odd[i]
        if ps == 1:
            self.dma("pool", self.FSI(), self.st_f[l].rearrange("p (j f) -> p j f", j=NJ))
        self.wprefetch()
        def d_in(g):
            sp_ = self.wget(self.w_in(wd, 1024 + g * 128))
            P = self.R[0]
            SP = self.SR[0]
            self.ACT(AF.Copy, P((HO - 15, HO)), self.CD(i, g))
            if ps == 1:
                self.dma("pool", self.SI((0, 240)), self.st_d[i, :, g])
                self.ACT(AF.Copy, SP(None, (SH - 15, SH)), self.SI((0, 240)).r("p (s r) -> p s r", s=16))
            for b in blks:
                pp = self.pst(self.PSIN, "pin", b)
                self.MM(pp, [(self.wk(sp_, k), self.xcols(self.XN, k, b)) for k in range(KC)])
                self.ACT(AF.Copy, self.row(P, SP, b, 0), pp)

        def d_rest(g):
            win = 2 << g
            P = self.R[0]
            Q1 = self.R[1]
            Q2 = self.R[2]
            SP = self.SR[0]
            SQ1 = self.SR[1]
            lo = HO - 15
            hi = HO + NPP
            src = P
            dsts = [Q1, Q2]
            sh = 1
            for m in range(g + 1):
                lo += sh
                dst = dsts[m % 2]
                self.TT("dve", dst((lo, hi)), src((lo, hi)), src((lo - sh, hi - sh)), ALU.add)
                src = dst
                sh *= 2
            RBr = self.rotate("rb", self.RB)
            SRBr = None
            self.STT("dve", RBr((HO, hi)), src((HO, hi)), 1.0 / win, P((HO, hi)), ALU.mult, ALU.subtract)
            if ps == 0:
                tmp = self.rotate("td", self.TD)((0, 16))
                self.TT("dve", tmp, src((HO, HO + 16)), self.INVC(g), ALU.mult)
                self.TT("dve", RBr((HO, HO + 16)), tmp, P((HO, HO + 16)), ALU.subtract)
            else:
                SRBr = self.rotate("srb", self.SRB)
                lo = SH - 15
                ssrc = SP
                sdsts = [SQ1, self.SR[2]]
                sh = 1
                for m in range(g + 1):
                    lo += sh
                    dst = sdsts[m % 2]
                    self.TT("dve", dst(None, (lo, SH + 8)), ssrc(None, (lo, SH + 8)), ssrc(None, (lo - sh, SH + 8 - sh)), ALU.add)
                    ssrc = dst
                    sh *= 2
                self.STT("dve", SRBr(None, (SH, SH + 8)), ssrc(None, (SH, SH + 8)), 1.0 / win, SP(None, (SH, SH + 8)), ALU.mult, ALU.subtract)
            for b in blks:
                pd = self.pst(self.PSAUX, "aux", b)
                rhs = RBr((HO + b.c0, HO + b.c0 + b.n)) if b.kind == "p" else SRBr(None, (SH, SH + 8))
                self.MM(pd, [(self.DPROJ(i, g), rhs)])
                self.ACT(AF.Identity, self.xcols(self.YH, 4 + g, b), pd, scale=self.vec(V_DSC + i * 4 + g))
            self.ACT(AF.Copy, self.CD(i, g), P((HO + NPP - 15, HO + NPP)))
            if ps == 1:
                self.ACT(AF.Copy, self.SO((0, 240)).r("p (s r) -> p s r", s=16), SP(None, (SH + 8 - 15, SH + 8)))
                self.dma("sp", self.o_ds[i, :, g], self.SO((0, 240)))
                if g == 3:
                    self.dma("sp", self.o_dp[i], self.CD(i).r("p c r -> p (c r)"))

        for c in range(4):
            sa = self.wget(self.w_in(wd, c * 128))
            sb_ = self.wget(self.w_in(wd, 512 + c * 128))
            DG = self.rotate("diag", self.DIAG)
            c0_ = V_CW + i * 124 + c
            cw = Acc(self.VEC, self.VEC.t[:, c0_:c0_ + 121:4].rearrange("p (k o) -> p k o", o=1).broadcast_to([128, 31, 128]), c0_, c0_ + 121)
            idb = Acc(self.IDENT, self.IDENT().ap.rearrange("p (o q) -> p o q", o=1).broadcast_to([128, 31, 128]), 0, 128)
            self.TT("dve", DG(), idb, cw, ALU.mult)
            RBr = self.rotate("rb", self.RB)
            SRf = self.rotate("sr", self.SR)
            SRBr = self.rotate("srb", self.SRB)
            self.ACT(AF.Copy, RBr((HO - 30, HO)), self.CC2(i, c))
            if ps == 1:
                self.dma("pool", self.SI((0, 480)), self.st_c[i, :, c])
                si3 = self.SI((0, 480)).r("p (s r) -> p s r", s=16)
                self.ACT(AF.Copy, SRf(None, (0, 30)), si3)
                self.ACT(AF.Identity, SRBr(None, (0, 30)), si3, scale=2.0)
            pend = None
            for bi, b in enumerate(blks):
                self.need(b)
                pa = self.pst(self.PSIN, "pin", b)
                self.MM(pa, [(self.wk(sa, k), self.xcols(self.XN, k, b)) for k in range(KC)])
                pb = self.pst(self.PSIN, "pin", b)
                self.MM(pb, [(self.wk(sb_, k), self.xcols(self.XN, k, b)) for k in range(KC)])
                th = self.tile(self.TA, "ta", b)
                self.ACT(AF.Tanh, th, pb, scale=0.5)
                self.STT("dve", self.row(RBr, SRBr, b, 0), th, 1.0, pa, ALU.add, ALU.mult)
                if b.kind == "s":
                    t2 = self.tile(self.TD, "td", b)
                    self.STT("dve", t2, th, 1.0, pa, ALU.add, ALU.mult)
                    self.ACT(AF.Identity, SRf(None, (SH, SH + 8)), t2, scale=0.5)
                elif ps == 1 and bi == 1:
                    t2 = self.rotate("td", self.TD)((0, 30))
                    self.STT("dve", t2, th_slice(th, 482, 512), 1.0, ps_slice(pa, 482, 512), ALU.add, ALU.mult)
                    self.ACT(AF.Identity, self.GT(i, c), t2, scale=0.5)
                if pend is not None:
                    pend()

                def conv(b=b, RBr=RBr, SRBr=SRBr, DG=DG, c=c):
                    pcv = self.pst(self.PSAUX, "aux", b)
                    pairs = []
                    for k in range(31):
                        if b.kind == "p":
                            rhs = RBr((HO + b.c0 - 30 + k, HO + b.c0 - 30 + k + b.n))
                        else:
                            rhs = SRBr(None, (k, k + 8))
                        pairs.append((DG(k), rhs))
                    self.MM(pcv, pairs)
                    self.ACT(AF.Identity, self.xcols(self.CC, c, b), pcv, scale=0.5, bias=self.vec(V_CB + i * 4 + c))
                pend = conv
            self.ACT(AF.Copy, self.CC2(i, c), RBr((HO + NPP - 30, HO + NPP)))
            if ps == 1:
                self.ACT(AF.Copy, self.SO((0, 480)).r("p (s r) -> p s r", s=16), SRf(None, (8, 38)))
                self.dma("sp", self.o_cs[i, :, c], self.SO((0, 480)))
                if c == 3:
                    self.dma("sp", self.o_cp[i], self.GT(i).r("p c r -> p (c r)"))
            d_in(c)
            pend()
            d_rest(c)
            self.wrel(3)
        ln_state = []
        for b in blks:
            n = b.n
            ntt = n // 128
            pbank = self.rotate("aux", self.PSAUX)
            pst_ = pbank((0, 8))
            for c in range(4):
                ccb = self.rotate("sq", self.SQ)((0, n))
                self.op("dve", lambda e, ccb=ccb, c=c, b=b: e.tensor_copy(out=ccb.ap, in_=self.CC(c, b.cols).ap), reads=[self.CC(c, b.cols)], writes=[ccb])
                sqb = self.rotate("sq2", self.SQ2)((0, n))
                self.ACT(AF.Square, sqb, self.CC(c, b.cols))

                def fn(e, c=c, ccb=ccb, sqb=sqb, pbank=pbank, ntt=ntt):
                    ins = None
                    for tt in range(ntt):
                        ins = e.matmul(pbank((tt, tt + 1)).ap, lhsT=ccb.ap[:, tt * 128:(tt + 1) * 128], rhs=self.ONES((0, 1)).ap,
                                       start=(c == 0 and tt == 0), stop=(c == 3), skip_group_check=True)
                    for tt in range(ntt):
                        ins = e.matmul(pbank((4 + tt, 5 + tt)).ap, lhsT=sqb.ap[:, tt * 128:(tt + 1) * 128], rhs=self.ONES((0, 1)).ap,
                                       start=False, stop=(c == 3), skip_group_check=True)
                    return ins
                self.op("pe", fn, reads=[ccb, sqb, self.ONES()], writes=[pst_])
            m4 = self.rotate("r4", self.R4)((0, ntt))
            self.TS("dve", m4, pbank((0, ntt)), 1.0 / 512, ALU.mult)
            v4 = self.rotate("r4", self.R4)((0, ntt))
            self.TT("dve", v4, m4, m4, ALU.mult)
            self.TS("dve", v4, v4, EPS, ALU.subtract)
            self.STT("dve", v4, pbank((4, 4 + ntt)), 1.0 / 512, v4, ALU.mult, ALU.subtract)
            self.TT("pool", v4, v4, self.NEGH((0, ntt)), ALU.pow)
            mr4 = self.rotate("r4", self.R4)((0, ntt))
            self.TT("dve", mr4, m4, v4, ALU.mult)
            ln_state.append((b, n, ntt, v4, mr4))
        for (b, n, ntt, v4, mr4) in ln_state:
            prR = self.bcast_rows(v4, ntt, n)
            prM = self.bcast_rows(mr4, ntt, n)
            pend = None
            for c in range(4):
                tn = self.rotate("td", self.TD)((0, n))
                self.TT("dve", tn, self.CC(c, b.cols), prR, ALU.mult)
                self.TT("dve", tn, tn, prM, ALU.subtract)
                zh = self.rotate("ta", self.TA)((0, n))
                self.ACT(AF.Identity, zh, tn, scale=self.vec(V_CLG + i * 4 + c), bias=self.vec(V_CLB + i * 4 + c))
                th = self.rotate("ta", self.TA)((0, n))
                self.ACT(AF.Tanh, th, tn, scale=self.vec(V_CLG + i * 4 + c), bias=self.vec(V_CLB + i * 4 + c))
                if pend is not None:
                    pend()
                pend = (lambda c=c, th=th, zh=zh, b=b: self.STT("dve", self.YH(c, b.cols), th, 1.0, zh, ALU.add, ALU.mult))
            pend()
        self.out_proj(self.w_out_odd[i], lambda b: self.norm_block(V_GFFN + 8 * l, b))

    def ffn(self, l):
        ps = self.ps
        blks = self.blks
        wd = self.w_ffn_in[l]
        wo = self.w_ffn_out[l]
        parts = [list(range(0, 4)), list(range(4, 8)), list(range(8, 12)), list(range(12, 16)),
                 list(range(16, 19)), list(range(19, 22))]
        self.wprefetch()

        def ffn_in(pi):
            part = parts[pi]
            pend = None
            for jj, j in enumerate(part):
                hs = (pi % 2) * 4 + jj
                sg = self.wget(self.w_in(wd, j * 128))
                su = self.wget(self.w_in(wd, DFF + j * 128))
                R = self.rotate("r", self.R)
                SRb = self.rotate("sr", self.SR)
                self.ACT(AF.Copy, R((HO - 2, HO)), self.CF(l, j))
                if ps == 1:
                    self.ACT(AF.Copy, SRb(None, (SH - 2, SH)), self.FSI(j).r("p (s r) -> p s r", s=16))
                w0 = self.vec(V_FW + (l * 3 + 0) * NJ + j)
                w1 = self.vec(V_FW + (l * 3 + 1) * NJ + j)
                w2 = self.vec(V_FW + (l * 3 + 2) * NJ + j)
                for b in blks:
                    self.need(b)
                    pg = self.pst(self.PSIN, "pin", b)
                    self.MM(pg, [(self.wk(sg, k), self.xcols(self.XN, k, b)) for k in range(KC)])
                    pu = self.pst(self.PSIN, "pin", b)
                    self.MM(pu, [(self.wk(su, k), self.xcols(self.XN, k, b)) for k in range(KC)])
                    self.ACT(AF.Copy, self.row(R, SRb, b, 0), pg)
                    t0 = self.tile(self.TA, "ta", b)
                    self.ACT(AF.Identity, t0, pg, scale=w2)
                    t1 = self.tile(self.TD, "td", b)
                    self.STT("dve", t1, self.row(R, SRb, b, 1), w1, t0, ALU.mult, ALU.add)
                    self.STT("dve", t1, self.row(R, SRb, b, 2), w0, t1, ALU.mult, ALU.add)
                    if pend is not None:
                        pend()

                    def fin(t1=t1, pu=pu, b=b, hs=hs):
                        gl = self.tile(self.TA, "ta", b)
                        self.ACT(AF.Gelu_apprx_tanh, gl, t1)
                        self.TT("dve", self.xcols(self.YH, hs, b), gl, pu, ALU.mult)
                    pend = fin
                self.ACT(AF.Copy, self.CF(l, j), R((HO + NPP - 2, HO + NPP)))
                if ps == 1:
                    self.ACT(AF.Copy, self.FSO(j).r("p (s r) -> p s r", s=16), SRb(None, (SH + 6, SH + 8)))
                self.wrel(2)
            if pend is not None:
                pend()

        def ffn_out(pi, last):
            part = parts[pi]
            slots = [self.wget(("out", wo[j * 128:(j + 1) * 128, :])) for j in part]
            prev = None
            for b in blks:
                for m in range(KC):
                    po = self.pst(self.PSAUX, "aux", b)
                    self.MM(po, [(self.RING(slots[jj], (m * 128, (m + 1) * 128)), self.xcols(self.YH, (pi % 2) * 4 + jj, b))
                                 for jj in range(len(part))])
                    xm = self.xcols(self.X, m, b)
                    self.TT("dve", xm, xm, po, ALU.add)
                if last and prev is not None:
                    self.next_norm(l, prev)
                prev = b
            if last:
                if l == self.depth - 1 or (l + 1) % 2 == 1:
                    self.next_norm(l, prev)
                else:
                    self.pending[prev.index] = (lambda prev=prev: self.next_norm(l, prev))
            self.wrel(len(part))

        ffn_in(0)
        for pi in range(len(parts)):
            if pi + 1 < len(parts):
                ffn_in(pi + 1)
            else:
                if ps == 1:
                    self.dma("sp", self.o_fp[l], self.CF(l).r("p c r -> p (c r)"))
                    self.dma("sp", self.o_fs[l], self.FSO().r("p j f -> p (j f)"))
            ffn_out(pi, pi == len(parts) - 1)


def th_slice(acc, a, b):
    return Acc(acc.buf, acc.ap[:, a:b], acc.lo + a, acc.lo + b)


def ps_slice(acc, a, b):
    return Acc(acc.buf, acc.ap[:, a:b], acc.lo + a, acc.lo + b)


def _consts():
    s = np.arange(128)
    maskp = (s[:, None] <= s[None, :]).astype(np.float32)
    masks = ((s[:, None] // 8 == s[None, :] // 8) & (s[:, None] % 8 <= s[None, :] % 8)).astype(np.float32)
    ident = np.eye(128, dtype=np.float32)
    invc = np.zeros((4, 16), np.float32)
    for g, win in enumerate((2, 4, 8, 16)):
        invc[g] = 1.0 / np.minimum(np.arange(16) + 1, win)
    return np.stack([maskp, masks]), ident, invc


def _pack_vecs(inp):
    def pk(a):
        a = np.asarray(a, np.float32)
        lead = a.shape[:-1]
        n = a.shape[-1] // 128
        a = a.reshape(lead + (n, 128))
        a = np.moveaxis(a, -1, 0)
        return a.reshape(128, -1)
    cols = [pk(inp["norm_mix_g"]), pk(inp["norm_ffn_g"]), pk(inp["norm_final_g"]),
            pk(inp["b_conv_w"]), pk(inp["c_conv_w"]), pk(inp["c_conv_b"]), pk(inp["c_ln_g"]),
            pk(inp["c_ln_b"]), pk(inp["d_scale"]), pk(inp["ffn_conv_w"]), pk(inp["a_ln_g"])]
    v = np.concatenate(cols, axis=1)
    assert v.shape == (128, NV), v.shape
    return np.ascontiguousarray(v)


_CACHE = {}


def kernel(**inp):
    inp = {k: np.asarray(v) for k, v in inp.items()}
    key = "nc"
    if key not in _CACHE:
        _CACHE[key] = Builder().build()
    nc = _CACHE[key]
    masks, ident, invc = _consts()
    vecs = _pack_vecs(inp)
    ws = inp["a_ws"].astype(np.float32)
    wst = np.zeros((2, 4, 2, 128, 128), np.float32)
    wst[:, :, 0] = np.swapaxes(ws, -1, -2)
    wst[:, :, 1] = np.tile(np.swapaxes(ws[:, :, :8, :8], -1, -2), (1, 1, 16, 16))
    bs = inp["a_bs"].astype(np.float32)
    bsb = np.concatenate([bs, np.tile(bs[:, :, :8], (1, 1, 16))], axis=-1)
    shared = dict(
        w_in_even=inp["w_in_even"], w_out_even=inp["w_out_even"], w_in_odd=inp["w_in_odd"],
        w_out_odd=inp["w_out_odd"], w_ffn_in=inp["w_ffn_in"], w_ffn_out=inp["w_ffn_out"],
        d_proj=inp["d_proj"], vecs=vecs, lng=inp["a_ln_g"], bsb=np.ascontiguousarray(bsb), wst=wst,
        masks=masks, ident=ident, invc=invc)
    shared = {k: np.ascontiguousarray(v, dtype=np.float32) for k, v in shared.items()}
    in_maps = []
    for c in range(NCORES):
        sl = slice(16 * c, 16 * c + 16)
        xs = inp["x_sample"][sl].reshape(128, D)
        xT = np.ascontiguousarray(np.concatenate([inp["x_prompt"][c], xs], axis=0).T, dtype=np.float32)
        m = dict(shared)
        m["xT"] = xT
        def st(a):
            n, _, r, ch = a.shape
            a = a.reshape(n, 16, r, ch // 128, 128)
            return np.ascontiguousarray(np.transpose(a, (0, 4, 3, 1, 2)), dtype=np.float32)
        m["st_b"] = st(inp["state_b_conv"][:, sl]).reshape(2, 128, 128)
        m["st_c"] = st(inp["state_c_conv"][:, sl]).reshape(2, 128, 4, 480)
        m["st_d"] = st(inp["state_d_pool"][:, sl]).reshape(2, 128, 4, 240)
        m["st_f"] = st(inp["state_ffn_conv"][:, sl]).reshape(4, 128, NJ * 32)
        in_maps.append(m)
    res = run_bass_kernel_spmd(nc, in_maps, core_ids=list(range(NCORES)))
    R = res.results
    y_prompt = np.stack([R[c]["yT"][:, :2048].T for c in range(NCORES)])
    y_sample = np.concatenate([R[c]["yT"][:, 2048:].T.reshape(16, 8, D) for c in range(NCORES)])
    a_s = np.concatenate([R[c]["o_av"].reshape(2, 16, 8, 512) for c in range(NCORES)], axis=1)

    def pr(name, C, r):
        outs_ = []
        for c in range(NCORES):
            a = R[c][name]
            n = a.shape[0]
            a = a.reshape(n, 128, C, r)
            outs_.append(np.transpose(a, (0, 3, 2, 1)).reshape(n, r, C * 128))
        return np.stack(outs_, axis=1)

    def sm(name, C, r):
        outs_ = []
        for c in range(NCORES):
            a = R[c][name]
            n = a.shape[0]
            a = a.reshape(n, 128, C, 16, r)
            outs_.append(np.transpose(a, (0, 3, 4, 2, 1)).reshape(n, 16, r, C * 128))
        return np.concatenate(outs_, axis=1)
    outs = (y_prompt, y_sample, a_s, pr("o_bp", 4, 2), sm("o_bs", 4, 2), pr("o_cp", 4, 30), sm("o_cs", 4, 30),
            pr("o_dp", 4, 15), sm("o_ds", 4, 15), pr("o_fp", NJ, 2), sm("o_fs", NJ, 2))
    return tuple(np.ascontiguousarray(o, dtype=np.float32) for o in outs)
```

```python
import numpy as np
from contextlib import ExitStack
import concourse.bass as bass
import concourse.mybir as mybir
from concourse.bass_utils import run_bass_kernel_spmd

F32 = mybir.dt.float32
BF16 = mybir.dt.bfloat16
AF = mybir.ActivationFunctionType
ALU = mybir.AluOpType

D = 1024
KC = 8
DEPTH = 4
DFF = 2816
NJ = 22
NPP = 1024
NTM = 1152
HO = 32
SH = 30
SW = 40
EPS = 1e-6
NSLOT = 14
NCORES = 8

V_GMIX = 0
V_GFFN = 32
V_GFIN = 64
V_BW = 72
V_CW = 96
V_CB = 344
V_CLG = 352
V_CLB = 360
V_DSC = 368
V_FW = 376
V_ALG = 640
NV = 648


class Buf:
    _n = 0

    def __init__(self, t, free_shape, cell, name):
        self.t = t
        self.shape = list(free_shape)
        self.cell = cell
        self.strides = []
        s = 1
        for d in reversed(self.shape):
            self.strides.insert(0, s)
            s *= d
        self.size = s
        self.id = Buf._n
        Buf._n += 1
        self.name = name
        if len(self.shape) == 1:
            self._flat = None
        else:
            names = " ".join(f"d{i}" for i in range(len(self.shape)))
            self._flat = f"p {names} -> p ({names})"

    def __call__(self, *idx, p=None):
        idx = list(idx) + [None] * (len(self.shape) - len(idx))
        sl = []
        lo = 0
        hi = 0
        for i, n, st in zip(idx, self.shape, self.strides):
            if i is None:
                a, b = 0, n
                sl.append(slice(None))
            elif isinstance(i, tuple):
                a, b = i
                assert 0 <= a < b <= n, (self.name, idx, self.shape)
                sl.append(slice(a, b))
            else:
                a, b = i, i + 1
                assert 0 <= a < n, (self.name, idx, self.shape)
                sl.append(i)
            lo += a * st
            hi += (b - 1) * st
        hi += 1
        ps = slice(None) if p is None else slice(p[0], p[1])
        return Acc(self, self.t[tuple([ps] + sl)], lo, hi)

    def flat(self, lo, hi, p=None):
        ps = slice(None) if p is None else slice(p[0], p[1])
        full = self.t[tuple([slice(None)] * (len(self.shape) + 1))]
        if self._flat is not None:
            full = full.rearrange(self._flat)
        return Acc(self, full[ps, lo:hi], lo, hi)


class Acc:
    def __init__(self, buf, ap, lo, hi):
        self.buf = buf
        self.ap = ap
        self.lo = lo
        self.hi = hi

    def cells(self):
        c = self.buf.cell
        return range(self.lo // c, (self.hi - 1) // c + 1)

    def r(self, pattern, **kw):
        return Acc(self.buf, self.ap.rearrange(pattern, **kw), self.lo, self.hi)

    def s3(self):
        return self.r("p (s t) -> p s t", s=16)


class Op:
    __slots__ = ("eng", "fn", "kind", "chan", "chan_idx", "seq", "waits", "signal", "sigcount", "id", "kn")


class Sched:
    ENGS = ("pe", "act", "dve", "pool", "sp")

    def __init__(self):
        self.ops = []
        self.state = {}
        self.eng_ops = {e: [] for e in self.ENGS}
        self.waited = {e: {} for e in self.ENGS}
        self.waited_dma = {e: {} for e in self.ENGS}
        self.chan_count = {}

    def _add_dep(self, op, d):
        if d is None or d is op:
            return
        E = op.eng
        if d.kind == "dma":
            if self.waited_dma[E].get(d.chan, 0) >= d.chan_idx:
                return
            cur = op.waits.get(("dma", d.chan))
            if cur is None or cur.chan_idx < d.chan_idx:
                op.waits[("dma", d.chan)] = d
        else:
            if self.waited[E].get(d.eng, -1) >= d.seq:
                return
            cur = op.waits.get(("eng", d.eng))
            if cur is None or cur.seq < d.seq:
                op.waits[("eng", d.eng)] = d

    def op(self, eng, fn, reads=(), writes=(), dma=False, chan=None):
        o = Op()
        o.eng = eng
        o.fn = fn
        o.kind = "dma" if dma else "cmp"
        o.chan = chan
        o.waits = {}
        o.signal = False
        o.id = len(self.ops)
        o.seq = len(self.eng_ops[eng])
        if dma:
            n = self.chan_count.get(chan, 0) + 1
            self.chan_count[chan] = n
            o.chan_idx = n
            o.signal = True
        for a in reads:
            for c in a.cells():
                st = self.state.get((a.buf.id, c))
                if st is not None and st[0] is not None:
                    self._add_dep(o, st[0])
        for a in writes:
            for c in a.cells():
                st = self.state.get((a.buf.id, c))
                if st is None:
                    continue
                d = st[0]
                if d is not None and (d.kind == "dma" or dma or d.eng != eng):
                    self._add_dep(o, d)
                for r in st[1].values():
                    if r.kind == "dma" or dma or r.eng != eng:
                        self._add_dep(o, r)
        if dma and o.chan_idx > 1:
            cur = o.waits.get(("dma", chan))
            if self.waited_dma[eng].get(chan, 0) < o.chan_idx - 1 and (cur is None or cur.chan_idx < o.chan_idx - 1):
                prev = Op()
                prev.kind = "dma"
                prev.chan = chan
                prev.chan_idx = o.chan_idx - 1
                prev.id = -1
                prev.kn = None
                o.waits[("dma", chan)] = prev
        ke = self.waited[eng]
        kd = self.waited_dma[eng]
        kept = {}
        for key, d in sorted(o.waits.items(), key=lambda kv: -getattr(kv[1], "id", -1)):
            if key[0] == "dma":
                if kd.get(d.chan, 0) >= d.chan_idx:
                    continue
                kd[d.chan] = d.chan_idx
            else:
                if ke.get(d.eng, -1) >= d.seq:
                    continue
                d.signal = True
                ke[d.eng] = d.seq
            kept[key] = d
            kn = getattr(d, "kn", None)
            if kn is not None:
                for k2, v2 in kn[0].items():
                    if ke.get(k2, -1) < v2:
                        ke[k2] = v2
                for k2, v2 in kn[1].items():
                    if kd.get(k2, 0) < v2:
                        kd[k2] = v2
        o.waits = kept
        o.kn = (dict(ke), dict(kd))
        for a in writes:
            for c in a.cells():
                self.state[(a.buf.id, c)] = [o, {}]
        for a in reads:
            for c in a.cells():
                st = self.state.get((a.buf.id, c))
                if st is None:
                    st = [None, {}]
                    self.state[(a.buf.id, c)] = st
                st[1][eng if not dma else ("dma", o.id)] = o
        self.ops.append(o)
        self.eng_ops[eng].append(o)
        return o

    def emit(self, nc, sems, chan_sems):
        for e, lst in self.eng_ops.items():
            n = 0
            for o in lst:
                if o.kind == "cmp" and o.signal:
                    n += 1
                o.sigcount = n
        finals = list(self.chan_count.items())

        class _First:
            def __init__(self, e):
                self._e = e
                self.first = None

            def __getattr__(self, name):
                attr = getattr(self._e, name)
                if not callable(attr):
                    return attr

                def w(*a, **k):
                    r = attr(*a, **k)
                    if self.first is None and r is not None:
                        self.first = r
                    return r
                return w

        def run(engobj, ename):
            proxy = _First(engobj)
            for o in self.eng_ops[ename]:
                waits = list(o.waits.items())
                fuse = None
                if o.kind == "cmp" and waits:
                    fuse = waits.pop()
                for key, d in waits:
                    if key[0] == "dma":
                        engobj.wait_ge(chan_sems[d.chan], 16 * d.chan_idx)
                    else:
                        engobj.wait_ge(sems[d.eng], d.sigcount)
                proxy.first = None
                ins = o.fn(proxy)
                if fuse is not None:
                    key, d = fuse
                    if key[0] == "dma":
                        proxy.first._wait_ge(chan_sems[d.chan], 16 * d.chan_idx)
                    else:
                        proxy.first._wait_ge(sems[d.eng], d.sigcount)
                if o.kind == "dma":
                    ins.then_inc(chan_sems[o.chan], 16)
                elif o.signal:
                    ins.then_inc(sems[o.eng], 1)
            if ename == "sp":
                for c, n in finals:
                    engobj.wait_ge(chan_sems[c], 16 * n)

        with nc.Block() as block:
            @block.tensor
            def _(t):
                run(t, "pe")

            @block.scalar
            def _(a):
                run(a, "act")

            @block.vector
            def _(v):
                run(v, "dve")

            @block.gpsimd
            def _(g):
                run(g, "pool")

            @block.sync
            def _(s):
                run(s, "sp")


class Blk:
    def __init__(self, c0, n, kind):
        self.c0 = c0
        self.n = n
        self.kind = kind
        self.cols = (c0, c0 + n)


class Builder:
    def __init__(self, depth=DEPTH, dbg=None):
        self.depth = depth
        self.dbg = dbg
        self.nc = bass.Bass("TRN2", target_bir_lowering=False)
        self.S = Sched()
        self.es = ExitStack()
        self.dry = False
        self.nchan = 0
        self.gen_chans = {}
        self.rot = {}

    def dram(self, name, shape, kind="ExternalInput", dt=F32):
        return self.nc.dram_tensor(name, list(shape), dt, kind=kind).ap()

    def sb(self, name, shape, dt=F32, cell=128):
        t = self.es.enter_context(self.nc.sbuf_tensor(name, [128] + list(shape), dt))
        return Buf(t, shape, cell, name)

    def psb(self, name):
        t = self.es.enter_context(self.nc.psum_tensor(name, [128, 512], F32))
        return Buf(t, [512], 512, name)

    def rotate(self, key, lst):
        i = self.rot.get(key, 0)
        self.rot[key] = i + 1
        return lst[i % len(lst)]

    def op(self, eng, fn, reads=(), writes=(), **kw):
        if self.dry:
            return None
        return self.S.op(eng, fn, reads=reads, writes=writes, **kw)

    def dma(self, eng, out, in_, reads=(), writes=(), chan=None):
        if self.dry:
            return
        if chan is None:
            chan = self.rotate("genchan_" + eng, self.gen_chans[eng])
        o_ap = out.ap if isinstance(out, Acc) else out
        i_ap = in_.ap if isinstance(in_, Acc) else in_
        rd = list(reads) + ([in_] if isinstance(in_, Acc) else [])
        wr = list(writes) + ([out] if isinstance(out, Acc) else [])
        self.S.op(eng, lambda e: e.dma_start(out=o_ap, in_=i_ap), reads=rd, writes=wr, dma=True, chan=chan)

    def ACT(self, func, out, in_, scale=None, bias=None):
        kw = {}
        rd = [in_]
        if scale is not None:
            if isinstance(scale, Acc):
                kw["scale"] = scale.ap
                rd.append(scale)
            else:
                kw["scale"] = float(scale)
        if bias is not None:
            if isinstance(bias, Acc):
                kw["bias"] = bias.ap
                rd.append(bias)
            else:
                kw["bias"] = float(bias)
        self.op("act", lambda e: e.activation(out=out.ap, in_=in_.ap, func=func, **kw), reads=rd, writes=[out])

    def TT(self, eng, out, a, b, op):
        self.op(eng, lambda e: e.tensor_tensor(out=out.ap, in0=a.ap, in1=b.ap, op=op), reads=[a, b], writes=[out])

    def TS(self, eng, out, a, s1, op0, s2=None, op1=None):
        rd = [a]
        v1 = s1
        v2 = s2
        if isinstance(s1, Acc):
            rd.append(s1)
            v1 = s1.ap
        if isinstance(s2, Acc):
            rd.append(s2)
            v2 = s2.ap
        if op1 is None:
            self.op(eng, lambda e: e.tensor_scalar(out=out.ap, in0=a.ap, scalar1=v1, scalar2=None, op0=op0), reads=rd, writes=[out])
        else:
            self.op(eng, lambda e: e.tensor_scalar(out=out.ap, in0=a.ap, scalar1=v1, scalar2=v2, op0=op0, op1=op1), reads=rd, writes=[out])

    def STT(self, eng, out, a, s, b, op0, op1):
        rd = [a, b]
        v = s
        if isinstance(s, Acc):
            rd.append(s)
            v = s.ap
        self.op(eng, lambda e: e.scalar_tensor_tensor(out=out.ap, in0=a.ap, scalar=v, in1=b.ap, op0=op0, op1=op1), reads=rd, writes=[out])

    def MM(self, out, pairs, extra_reads=()):
        rd = []
        for l, r in pairs:
            rd.append(l)
            rd.append(r)
        rd += list(extra_reads)
        n = len(pairs)

        def fn(e):
            ins = None
            for i, (l, r) in enumerate(pairs):
                ins = e.matmul(out.ap, lhsT=l.ap, rhs=r.ap, start=(i == 0), stop=(i == n - 1))
            return ins
        self.op("pe", fn, reads=rd, writes=[out])

    def MMS(self, items):
        rd = []
        wr = []
        for o, l, r in items:
            rd += [l, r]
            wr.append(o)

        def fn(e):
            ins = None
            for o, l, r in items:
                ins = e.matmul(o.ap, lhsT=l.ap, rhs=r.ap, start=True, stop=True)
            return ins
        self.op("pe", fn, reads=rd, writes=wr)

    def wreset(self):
        self.w_idx = 0
        self.w_issued = 0
        self.w_released = set()
        self.w_oldest = 0

    def wget(self, spec):
        i = self.w_idx
        self.w_idx += 1
        if self.dry:
            self.plan.append(spec)
            return i % NSLOT
        assert i < len(self.plan)
        while self.w_issued <= i:
            assert self.w_issued - self.w_oldest < NSLOT, "ring overflow"
            self._wissue()
        return i % NSLOT

    def _wissue(self):
        i = self.w_issued
        kind, src = self.plan[i]
        slot = i % NSLOT
        if kind == "in":
            dst = self.RING(slot).r("p (k f) -> p k f", k=KC)
        else:
            dst = self.RING(slot)
        self.dma("pool", dst, src, chan=("ring", slot))
        self.w_issued += 1

    def wrel(self, n=1):
        if self.dry:
            return
        self.w_oldest += n
        self.wprefetch()

    def wprefetch(self):
        if self.dry:
            return
        while self.w_issued < len(self.plan) and self.w_issued - self.w_oldest < NSLOT:
            self._wissue()

    def w_in(self, wd, f0):
        return ("in", wd[:, f0:f0 + 128].rearrange("(k p) f -> p k f", p=128))

    def wk(self, slot, k):
        return self.RING(slot, (k * 128, (k + 1) * 128))

    def build(self):
        nc = self.nc
        dram = self.dram
        self.xT = dram("xT", [D, 2176])
        self.yT = dram("yT", [D, 2176], "ExternalOutput")
        self.st_b = dram("st_b", [2, 128, 128])
        self.st_c = dram("st_c", [2, 128, 4, 480])
        self.st_d = dram("st_d", [2, 128, 4, 240])
        self.st_f = dram("st_f", [4, 128, NJ * 32])
        self.w_in_even = dram("w_in_even", [2, D, 2560])
        self.w_out_even = dram("w_out_even", [2, D, D])
        self.w_in_odd = dram("w_in_odd", [2, D, 1536])
        self.w_out_odd = dram("w_out_odd", [2, D, D])
        self.w_ffn_in = dram("w_ffn_in", [4, D, 2 * DFF])
        self.w_ffn_out = dram("w_ffn_out", [4, DFF, D])
        self.d_proj = dram("d_proj", [2, 4, 128, 128])
        self.vecs = dram("vecs", [128, NV])
        self.lng = dram("lng", [2, 512])
        self.bsb = dram("bsb", [2, 4, 256])
        self.wst = dram("wst", [2, 4, 2, 128, 128])
        self.masks = dram("masks", [2, 128, 128])
        self.ident = dram("ident", [128, 128])
        self.invc = dram("invc", [4, 16])
        self.o_av = dram("o_av", [2, 128, 512], "ExternalOutput")
        self.o_bp = dram("o_bp", [2, 128, 8], "ExternalOutput")
        self.o_bs = dram("o_bs", [2, 128, 128], "ExternalOutput")
        self.o_cp = dram("o_cp", [2, 128, 120], "ExternalOutput")
        self.o_cs = dram("o_cs", [2, 128, 4, 480], "ExternalOutput")
        self.o_dp = dram("o_dp", [2, 128, 60], "ExternalOutput")
        self.o_ds = dram("o_ds", [2, 128, 4, 240], "ExternalOutput")
        self.o_fp = dram("o_fp", [4, 128, NJ * 2], "ExternalOutput")
        self.o_fs = dram("o_fs", [4, 128, NJ * 32], "ExternalOutput")
        if self.dbg is not None:
            self.o_dbg = dram("o_dbg", [D, 2176], "ExternalOutput")

        sb = self.sb
        self.X = sb("X", [KC, NTM])
        self.XN = sb("XN", [KC, NTM], BF16)
        self.YH = sb("YH", [KC, NTM], BF16)
        self.R = [sb(f"R{i}", [HO + NPP]) for i in range(3)]
        self.RB = [sb(f"RB{i}", [HO + NPP], BF16) for i in range(2)]
        self.SR = [sb(f"SR{i}", [16, SW], cell=16 * SW) for i in range(3)]
        self.SRB = [sb(f"SRB{i}", [16, SW], BF16, cell=16 * SW) for i in range(2)]
        self.TA = [sb(f"TA{i}", [512], cell=512) for i in range(4)]
        self.TD = [sb(f"TD{i}", [512], cell=512) for i in range(3)]
        self.SQ = [sb(f"SQ{i}", [512], BF16, cell=512) for i in range(2)]
        self.SQ2 = [sb(f"SQ2_{i}", [512], BF16, cell=512) for i in range(2)]
        self.RING = sb("RING", [NSLOT, 1024], BF16, cell=1024)
        self.DIAG = [sb(f"DIAG{i}", [31, 128], BF16, cell=31 * 128) for i in range(2)]
        self.CC = sb("CC", [4, NTM])
        self.VEC = sb("VEC", [NV], cell=8)
        self.LNG = sb("LNG", [512], cell=512)
        self.BSB = sb("BSB", [4, 256], cell=256)
        self.WST = sb("WST", [2, 4, 2, 128], BF16, cell=128)
        self.MSK = sb("MSK", [2, 128], cell=128)
        self.IDENT = sb("IDENT", [128], cell=128)
        self.ONES = sb("ONES", [128], BF16, cell=128)
        self.NEGH = sb("NEGH", [8], cell=8)
        self.ONESF = sb("ONESF", [128], cell=128)
        self.R4 = [sb(f"R4_{i}", [8], cell=8) for i in range(8)]
        self.INVC = sb("INVC", [4, 16], cell=64)
        self.DPROJ = sb("DPROJ", [2, 4, 128], BF16, cell=128)
        self.CB = sb("CB", [2, 4, 2], cell=2)
        self.CC2 = sb("CC2", [2, 4, 30], BF16, cell=30)
        self.CD = sb("CD", [2, 4, 15], cell=15)
        self.CF = sb("CF", [4, NJ, 2], cell=2)
        self.GT = sb("GT", [2, 4, 30], cell=30)
        self.SI = sb("SI", [480], cell=480)
        self.SO = sb("SO", [480], cell=480)
        self.FSI = sb("FSI", [NJ, 32], cell=32)
        self.FSO = sb("FSO", [NJ, 32], cell=32)
        self.STT6 = [sb(f"ST6_{i}", [8], cell=8) for i in range(3)]
        self.MV = [sb(f"MV{i}", [8], cell=8) for i in range(3)]
        self.PS = [self.psb(f"PS{i}") for i in range(8)]
        self.PSIN = self.PS[0:5]
        self.PSAUX = self.PS[5:8]

        self.gen_chans = {"sp": [("g", i) for i in range(24)], "pool": [("q", i) for i in range(12)]}

        self.plan = []
        self.dry = True
        self.wreset()
        self.program()
        self.dry = False
        self.rot = {}
        self.wreset()
        self.program()

        sems = {e: self.es.enter_context(nc.semaphore("s_" + e)) for e in ("pe", "act", "dve", "pool")}
        chans = {}
        for c in list(self.S.chan_count.keys()):
            chans[c] = self.es.enter_context(nc.semaphore("c_%s_%s" % c))
        self.S.emit(nc, sems, chans)
        self.es.close()
        return nc

    def vec(self, c):
        return self.VEC((c, c + 1))

    def program(self):
        self.setup()
        for ps in range(2):
            self.run_pass(ps)

    def setup(self):
        dma = self.dma
        dma("sp", self.VEC(), self.vecs[:, :])
        dma("sp", self.MSK(), self.masks.rearrange("m p f -> p m f"))
        dma("sp", self.IDENT(), self.ident[:, :])
        dma("sp", self.INVC(), self.invc.partition_broadcast(128))
        dma("pool", self.DPROJ(), self.d_proj.rearrange("i g c d -> c i g d"))
        self.op("dve", lambda e: e.memset(self.ONES().ap, 1.0), writes=[self.ONES()])
        self.op("dve", lambda e: e.memset(self.NEGH().ap, -0.5), writes=[self.NEGH()])
        self.op("dve", lambda e: e.memset(self.ONESF().ap, 1.0), writes=[self.ONESF()])
        for b in (self.CB, self.CD, self.CF, self.CC2):
            self.op("dve", lambda e, b=b: e.memset(b().ap, 0.0), writes=[b()])
        g = self.VEC((V_GMIX, V_GFIN + 8))
        self.TS("dve", g, g, 32.0, ALU.mult)
        h = self.VEC((V_CLG, V_CLB + 8))
        self.TS("dve", h, h, 0.5, ALU.mult)

    def setup_wst(self):
        dma = self.dma
        stg = self.CC.flat(0, 2048)
        dma("sp", Acc(self.CC, stg.ap.rearrange("p (a t) -> p a t", a=16), 0, 2048), self.wst.rearrange("i h k s t -> s (i h k) t"))
        v5 = stg.ap.rearrange("p (i h k t) -> p i h k t", i=2, h=4, k=2)
        for i in range(2):
            for kd in range(2):
                src = Acc(self.CC, v5[:, i, :, kd, :], 0, 2048)
                m0 = self.MSK(kd)
                mb = Acc(self.MSK, m0.ap.rearrange("p (o t) -> p o t", o=1).broadcast_to([128, 4, 128]), m0.lo, m0.hi)
                self.TT("dve", self.WST(i, None, kd), src, mb, ALU.mult)

    def blocks(self, ps):
        b = [Blk(0, 512, "p"), Blk(512, 512, "p")]
        if ps == 1:
            b.append(Blk(1024, 128, "s"))
        return b

    def run_pass(self, ps):
        self.ps = ps
        blks = self.blocks(ps)
        self.blks = blks
        nt = blks[-1].c0 + blks[-1].n
        self.nt = nt
        for b in blks:
            for k in range(KC):
                if b.kind == "p":
                    g0 = ps * NPP + b.c0
                else:
                    g0 = 2048
                self.dma("sp" if ps == 0 else "pool", self.X(k, b.cols), self.xT[k * 128:(k + 1) * 128, g0:g0 + b.n])
        self.pending = {}
        self.lookahead = 1
        for bi, b in enumerate(blks):
            b.index = bi
            self.pending[bi] = (lambda b=b: self.norm_block(V_GMIX + 0, b))
        for l in range(self.depth):
            if l % 2 == 0:
                self.even_mixer(l)
            else:
                self.odd_mixer(l)
            self.ffn(l)
        if self.dbg is not None:
            for k in range(KC):
                self.dma("sp", self.o_dbg[k * 128:(k + 1) * 128, ps * NPP:(ps + 1) * NPP], self.X(k, (0, NPP)))
                if ps == 1:
                    self.dma("sp", self.o_dbg[k * 128:(k + 1) * 128, 2048:2176], self.X(k, (NPP, NPP + 128)))

    def bv(self, acc, b):
        return acc.s3() if b.kind == "s" else acc

    def tile(self, pool, key, b):
        t = self.rotate(key, pool)((0, b.n))
        return self.bv(t, b)

    def pst(self, lst, key, b):
        t = self.rotate(key, lst)((0, b.n))
        return self.bv(t, b)

    def xcols(self, buf, k, b):
        return self.bv(buf(k, b.cols), b)

    def row(self, R, SRb, b, shift):
        if b.kind == "p":
            return R((HO + b.c0 - shift, HO + b.c0 - shift + b.n))
        return SRb(None, (SH - shift, SH + 8 - shift))

    def bcast_rows(self, r4, ntt, n):
        rep = self.rotate("ta", self.TA)((0, n))
        for tt in range(ntt):
            self.ACT(AF.Identity, th_slice(rep, tt * 128, (tt + 1) * 128), self.ONESF(), scale=th_slice(r4, tt, tt + 1))
        pr = self.rotate("aux", self.PSAUX)((0, n))

        def fnt(e, pr=pr, rep=rep, ntt=ntt):
            ins = None
            for tt in range(ntt):
                ins = e.transpose(pr.ap[:, tt * 128:(tt + 1) * 128], rep.ap[:, tt * 128:(tt + 1) * 128], self.IDENT().ap)
            return ins
        self.op("pe", fnt, reads=[rep, self.IDENT()], writes=[pr])
        return pr

    def norm_block(self, gcol, b, final=False):
        n = b.n
        ntt = n // 128
        pbank = self.rotate("aux", self.PSAUX)
        pss = pbank((0, ntt))
        for k in range(KC):
            sq = self.rotate("sq", self.SQ)((0, n))
            self.ACT(AF.Square, sq, self.X(k, b.cols))

            def fn(e, k=k, sq=sq, pbank=pbank, ntt=ntt):
                ins = None
                for tt in range(ntt):
                    ins = e.matmul(pbank((tt, tt + 1)).ap, lhsT=sq.ap[:, tt * 128:(tt + 1) * 128], rhs=self.ONES((0, 1)).ap,
                                   start=(k == 0 and tt == 0), stop=(k == KC - 1), skip_group_check=True)
                return ins
            self.op("pe", fn, reads=[sq, self.ONES()], writes=[pss])
        r4 = self.rotate("r4", self.R4)((0, ntt))
        self.TS("dve", r4, pss, 1024.0 * EPS, ALU.add)
        self.TT("pool", r4, r4, self.NEGH((0, ntt)), ALU.pow)
        rs = self.bcast_rows(r4, ntt, n)
        for k in range(KC):
            if not final:
                self.STT("dve", self.XN(k, b.cols), self.X(k, b.cols), self.vec(gcol + k), rs, ALU.mult, ALU.mult)
            else:
                o = self.rotate("ta", self.TA)((0, n))
                self.STT("dve", o, self.X(k, b.cols), self.vec(gcol + k), rs, ALU.mult, ALU.mult)
                if b.kind == "p":
                    c = self.ps * NPP + b.c0
                else:
                    c = 2048
                self.dma("sp", self.yT[k * 128:(k + 1) * 128, c:c + n], o)

    def rotate_last(self, key, lst):
        i = self.rot.get(key, 0)
        return lst[(i - 1) % len(lst)]

    def need(self, b):
        for bi in sorted(self.pending.keys()):
            if bi <= b.index + self.lookahead:
                fn = self.pending.pop(bi)
                fn()

    def need_all(self):
        for bi in sorted(self.pending.keys()):
            self.pending.pop(bi)()

    def next_norm(self, l, b):
        if l == self.depth - 1:
            self.norm_block(V_GFIN, b, final=True)
        else:
            self.norm_block(V_GMIX + 8 * (l + 1), b)

    def out_proj(self, wd, after):
        slots = [self.wget(self.w_in(wd, m * 128)) for m in range(KC)]
        prev = None
        for b in self.blks:
            for m in range(KC):
                po = self.pst(self.PSAUX, "aux", b)
                self.MM(po, [(self.wk(slots[m], k), self.xcols(self.YH, k, b)) for k in range(KC)])
                xm = self.xcols(self.X, m, b)
                self.TT("dve", xm, xm, po, ALU.add)
            if prev is not None:
                after(prev)
            prev = b
        after(prev)
        self.wrel(KC)

    def even_mixer(self, l):
        i = l // 2
        ps = self.ps
        blks = self.blks
        wd = self.w_in_even[i]
        self.dma("pool", self.BSB(), self.bsb[i].partition_broadcast(128))
        self.dma("pool", self.LNG(), self.lng[i].partition_broadcast(128))
        if ps == 1:
            self.dma("pool", self.SI((0, 128)), self.st_b[i])
            self.dma("pool", self.FSI(), self.st_f[l].rearrange("p (j f) -> p j f", j=NJ))
        first = (ps == 0 and l == 0)
        if not first:
            self.wprefetch()
        sv = [self.wget(self.w_in(wd, 512 + q * 128)) for q in range(4)]
        ntile = self.nt // 128
        VOFF = 4 * NTM

        def vacc(tt, c0, c1, p=None):
            return self.YH.flat(VOFF + tt * 512 + c0, VOFF + tt * 512 + c1)
        for tt in range(ntile):
            if first and tt == 2:
                self.setup_wst()
                self.wprefetch()
            self.need(blks[min(tt // 4, len(blks) - 1)])
            pv = self.rotate("pin", self.PSIN)
            runs = []
            q = 0
            while q < 4:
                L = 1
                while q + L < 4 and sv[q + L] == sv[q + L - 1] + 1:
                    L += 1
                runs.append((q, L))
                q += L
            for (q0, L) in runs:
                outv = pv((q0 * 128, (q0 + L) * 128)).r("p (s f) -> p s f", s=L)
                self.MM(outv, [(self.XN(k, (tt * 128, (tt + 1) * 128)),
                                self.RING((sv[q0], sv[q0] + L), (k * 128, (k + 1) * 128))) for k in range(KC)])
            gv = self.rotate("ta", self.TA)()
            self.ACT(AF.Gelu_apprx_tanh, gv, pv())
            st6 = self.rotate("st6", self.STT6)
            mv = self.rotate("mv", self.MV)
            self.op("dve", lambda e, st6=st6, gv=gv: e.bn_stats(out=st6((0, 6)).ap, in_=gv.ap), reads=[gv], writes=[st6()])
            self.op("dve", lambda e, st6=st6, mv=mv: e.bn_aggr(out=mv((0, 2)).ap, in_=st6((0, 6)).ap), reads=[st6()], writes=[mv()])
            self.TS("dve", mv((2, 3)), mv((1, 2)), EPS, ALU.add)
            self.TT("pool", mv((3, 4)), mv((2, 3)), self.NEGH((0, 1)), ALU.pow)
            is_s = (ps == 1 and tt == ntile - 1)
            if is_s:
                vn = self.rotate("td", self.TD)()
                self.TS("dve", vn, gv, mv((0, 1)), ALU.subtract, mv((3, 4)), ALU.mult)
                vf = self.rotate("td", self.TD)()
                self.TT("dve", vf, vn, self.LNG(), ALU.mult)
                self.dma("sp", self.o_av[i], vf)
                self.ACT(AF.Copy, vacc(tt, 0, 512), vn)
            else:
                self.STT("dve", mv((4, 5)), mv((0, 1)), -1.0, mv((3, 4)), ALU.mult, ALU.mult)
                self.ACT(AF.Identity, vacc(tt, 0, 512), gv, scale=mv((3, 4)), bias=mv((4, 5)))
        self.wrel(4)
        self.lookahead = 0
        for h in range(4):
            su = self.wget(self.w_in(wd, h * 128))
            for b in blks:
                pu = self.pst(self.PSIN, "pin", b)
                self.MM(pu, [(self.wk(su, k), self.xcols(self.XN, k, b)) for k in range(KC)])
                u = self.tile(self.TA, "ta", b)
                self.ACT(AF.Gelu_apprx_tanh, u, pu)
                pmb = self.rotate("aux", self.PSAUX)
                kd = 1 if b.kind == "s" else 0
                items = []
                for q in range(b.n // 128):
                    tt = b.c0 // 128 + q
                    items.append((pmb((q * 128, (q + 1) * 128)), vacc(tt, h * 128, (h + 1) * 128), self.WST(i, h, kd)))
                self.MMS(items)
                pm = self.bv(pmb((0, b.n)), b)
                tmp = self.tile(self.TD, "td", b)
                if b.kind == "p":
                    b0 = self.BSB(h, (0, 128))
                    bs = Acc(self.BSB, b0.ap.rearrange("p (o t) -> p o t", o=1).broadcast_to([128, 4, 128]), b0.lo, b0.hi)
                    self.STT("dve", tmp.r("p (q t) -> p q t", q=4), pm.r("p (q t) -> p q t", q=4), self.vec(V_ALG + i * 4 + h), bs, ALU.mult, ALU.add)
                else:
                    bs = self.BSB(h, (128, 256)).s3()
                    self.STT("dve", tmp, pm, self.vec(V_ALG + i * 4 + h), bs, ALU.mult, ALU.add)
                self.TT("dve", self.xcols(self.YH, h, b), tmp, u, ALU.mult)
            self.wrel(1)
        for c in range(4):
            scg = self.wget(self.w_in(wd, 1536 + c * 128))
            sh = self.wget(self.w_in(wd, 2048 + c * 128))
            sbg = self.wget(self.w_in(wd, 1024 + c * 128))
            R = self.rotate("r", self.R)
            SRb = self.rotate("sr", self.SR)
            self.ACT(AF.Copy, R((HO - 2, HO)), self.CB(i, c))
            if ps == 1:
                self.ACT(AF.Copy, SRb(None, (SH - 2, SH)), self.SI((c * 32, (c + 1) * 32)).r("p (s r) -> p s r", s=16))
            w0 = self.vec(V_BW + (i * 3 + 0) * 4 + c)
            w1 = self.vec(V_BW + (i * 3 + 1) * 4 + c)
            w2 = self.vec(V_BW + (i * 3 + 2) * 4 + c)
            for b in blks:
                pcg = self.pst(self.PSIN, "pin", b)
                self.MM(pcg, [(self.wk(scg, k), self.xcols(self.XN, k, b)) for k in range(KC)])
                ph = self.pst(self.PSIN, "pin", b)
                self.MM(ph, [(self.wk(sh, k), self.xcols(self.XN, k, b)) for k in range(KC)])
                pbg = self.pst(self.PSIN, "pin", b)
                self.MM(pbg, [(self.wk(sbg, k), self.xcols(self.XN, k, b)) for k in range(KC)])
                cg = self.tile(self.TA, "ta", b)
                self.ACT(AF.Copy, cg, pcg)
                self.TT("dve", self.row(R, SRb, b, 0), cg, ph, ALU.mult)
                t0 = self.tile(self.TA, "ta", b)
                self.ACT(AF.Identity, t0, self.row(R, SRb, b, 0), scale=w2)
                t1 = self.tile(self.TD, "td", b)
                self.STT("dve", t1, self.row(R, SRb, b, 1), w1, t0, ALU.mult, ALU.add)
                self.STT("dve", t1, self.row(R, SRb, b, 2), w0, t1, ALU.mult, ALU.add)
                self.TT("dve", self.xcols(self.YH, 4 + c, b), t1, pbg, ALU.mult)
            self.ACT(AF.Copy, self.CB(i, c), R((HO + NPP - 2, HO + NPP)))
            if ps == 1:
                self.ACT(AF.Copy, self.SO((c * 32, (c + 1) * 32)).r("p (s r) -> p s r", s=16), SRb(None, (SH + 6, SH + 8)))
            self.wrel(3)
        if ps == 1:
            self.dma("sp", self.o_bp[i], self.CB(i).r("p c r -> p (c r)"))
            self.dma("sp", self.o_bs[i], self.SO((0, 128)))
        self.out_proj(self.w_out_even[i], lambda b: self.norm_block(V_GFFN + 8 * l, b))

    def odd_mixer(self, l):
        i = l // 2
        ps = self.ps
        blks = self.blks
        wd = self.w_in_odd[i]
        if ps == 1:
            self.dma("pool", self.FSI(), self.st_f[l].rearrange("p (j f) -> p j f", j=NJ))
        self.wprefetch()
        def d_in(g):
            sp_ = self.wget(self.w_in(wd, 1024 + g * 128))
            P = self.R[0]
            SP = self.SR[0]
            self.ACT(AF.Copy, P((HO - 15, HO)), self.CD(i, g))
            if ps == 1:
                self.dma("pool", self.SI((0, 240)), self.st_d[i, :, g])
                self.ACT(AF.Copy, SP(None, (SH - 15, SH)), self.SI((0, 240)).r("p (s r) -> p s r", s=16))
            for b in blks:
                pp = self.pst(self.PSIN, "pin", b)
                self.MM(pp, [(self.wk(sp_, k), self.xcols(self.XN, k, b)) for k in range(KC)])
                self.ACT(AF.Copy, self.row(P, SP, b, 0), pp)

        def d_rest(g):
            win = 2 << g
            P = self.R[0]
            Q1 = self.R[1]
            Q2 = self.R[2]
            SP = self.SR[0]
            SQ1 = self.SR[1]
            lo = HO - 15
            hi = HO + NPP
            src = P
            dsts = [Q1, Q2]
            sh = 1
            for m in range(g + 1):
                lo += sh
                dst = dsts[m % 2]
                self.TT("dve", dst((lo, hi)), src((lo, hi)), src((lo - sh, hi - sh)), ALU.add)
                src = dst
                sh *= 2
            RBr = self.rotate("rb", self.RB)
            SRBr = None
            self.STT("dve", RBr((HO, hi)), src((HO, hi)), 1.0 / win, P((HO, hi)), ALU.mult, ALU.subtract)
            if ps == 0:
                tmp = self.rotate("td", self.TD)((0, 16))
                self.TT("dve", tmp, src((HO, HO + 16)), self.INVC(g), ALU.mult)
                self.TT("dve", RBr((HO, HO + 16)), tmp, P((HO, HO + 16)), ALU.subtract)
            else:
                SRBr = self.rotate("srb", self.SRB)
                lo = SH - 15
                ssrc = SP
                sdsts = [SQ1, self.SR[2]]
                sh = 1
                for m in range(g + 1):
                    lo += sh
                    dst = sdsts[m % 2]
                    self.TT("dve", dst(None, (lo, SH + 8)), ssrc(None, (lo, SH + 8)), ssrc(None, (lo - sh, SH + 8 - sh)), ALU.add)
                    ssrc = dst
                    sh *= 2
                self.STT("dve", SRBr(None, (SH, SH + 8)), ssrc(None, (SH, SH + 8)), 1.0 / win, SP(None, (SH, SH + 8)), ALU.mult, ALU.subtract)
            for b in blks:
                pd = self.pst(self.PSAUX, "aux", b)
                rhs = RBr((HO + b.c0, HO + b.c0 + b.n)) if b.kind == "p" else SRBr(None, (SH, SH + 8))
                self.MM(pd, [(self.DPROJ(i, g), rhs)])
                self.ACT(AF.Identity, self.xcols(self.YH, 4 + g, b), pd, scale=self.vec(V_DSC + i * 4 + g))
            self.ACT(AF.Copy, self.CD(i, g), P((HO + NPP - 15, HO + NPP)))
            if ps == 1:
                self.ACT(AF.Copy, self.SO((0, 240)).r("p (s r) -> p s r", s=16), SP(None, (SH + 8 - 15, SH + 8)))
                self.dma("sp", self.o_ds[i, :, g], self.SO((0, 240)))
                if g == 3:
                    self.dma("sp", self.o_dp[i], self.CD(i).r("p c r -> p (c r)"))

        for c in range(4):
            sa = self.wget(self.w_in(wd, c * 128))
            sb_ = self.wget(self.w_in(wd, 512 + c * 128))
            DG = self.rotate("diag", self.DIAG)
            c0_ = V_CW + i * 124 + c
            cw = Acc(self.VEC, self.VEC.t[:, c0_:c0_ + 121:4].rearrange("p (k o) -> p k o", o=1).broadcast_to([128, 31, 128]), c0_, c0_ + 121)
            idb = Acc(self.IDENT, self.IDENT().ap.rearrange("p (o q) -> p o q", o=1).broadcast_to([128, 31, 128]), 0, 128)
            self.TT("dve", DG(), idb, cw, ALU.mult)
            RBr = self.rotate("rb", self.RB)
            SRf = self.rotate("sr", self.SR)
            SRBr = self.rotate("srb", self.SRB)
            self.ACT(AF.Copy, RBr((HO - 30, HO)), self.CC2(i, c))
            if ps == 1:
                self.dma("pool", self.SI((0, 480)), self.st_c[i, :, c])
                si3 = self.SI((0, 480)).r("p (s r) -> p s r", s=16)
                self.ACT(AF.Copy, SRf(None, (0, 30)), si3)
                self.ACT(AF.Identity, SRBr(None, (0, 30)), si3, scale=2.0)
            pend = None
            for bi, b in enumerate(blks):
                self.need(b)
                pa = self.pst(self.PSIN, "pin", b)
                self.MM(pa, [(self.wk(sa, k), self.xcols(self.XN, k, b)) for k in range(KC)])
                pb = self.pst(self.PSIN, "pin", b)
                self.MM(pb, [(self.wk(sb_, k), self.xcols(self.XN, k, b)) for k in range(KC)])
                th = self.tile(self.TA, "ta", b)
                self.ACT(AF.Tanh, th, pb, scale=0.5)
                self.STT("dve", self.row(RBr, SRBr, b, 0), th, 1.0, pa, ALU.add, ALU.mult)
                if b.kind == "s":
                    t2 = self.tile(self.TD, "td", b)
                    self.STT("dve", t2, th, 1.0, pa, ALU.add, ALU.mult)
                    self.ACT(AF.Identity, SRf(None, (SH, SH + 8)), t2, scale=0.5)
                elif ps == 1 and bi == 1:
                    t2 = self.rotate("td", self.TD)((0, 30))
                    self.STT("dve", t2, th_slice(th, 482, 512), 1.0, ps_slice(pa, 482, 512), ALU.add, ALU.mult)
                    self.ACT(AF.Identity, self.GT(i, c), t2, scale=0.5)
                if pend is not None:
                    pend()

                def conv(b=b, RBr=RBr, SRBr=SRBr, DG=DG, c=c):
                    pcv = self.pst(self.PSAUX, "aux", b)
                    pairs = []
                    for k in range(31):
                        if b.kind == "p":
                            rhs = RBr((HO + b.c0 - 30 + k, HO + b.c0 - 30 + k + b.n))
                        else:
                            rhs = SRBr(None, (k, k + 8))
                        pairs.append((DG(k), rhs))
                    self.MM(pcv, pairs)
                    self.ACT(AF.Identity, self.xcols(self.CC, c, b), pcv, scale=0.5, bias=self.vec(V_CB + i * 4 + c))
                pend = conv
            self.ACT(AF.Copy, self.CC2(i, c), RBr((HO + NPP - 30, HO + NPP)))
            if ps == 1:
                self.ACT(AF.Copy, self.SO((0, 480)).r("p (s r) -> p s r", s=16), SRf(None, (8, 38)))
                self.dma("sp", self.o_cs[i, :, c], self.SO((0, 480)))
                if c == 3:
                    self.dma("sp", self.o_cp[i], self.GT(i).r("p c r -> p (c r)"))
            d_in(c)
            pend()
            d_rest(c)
            self.wrel(3)
        ln_state = []
        for b in blks:
            n = b.n
            ntt = n // 128
            pbank = self.rotate("aux", self.PSAUX)
            pst_ = pbank((0, 8))
            for c in range(4):
                ccb = self.rotate("sq", self.SQ)((0, n))
                self.op("dve", lambda e, ccb=ccb, c=c, b=b: e.tensor_copy(out=ccb.ap, in_=self.CC(c, b.cols).ap), reads=[self.CC(c, b.cols)], writes=[ccb])
                sqb = self.rotate("sq2", self.SQ2)((0, n))
                self.ACT(AF.Square, sqb, self.CC(c, b.cols))

                def fn(e, c=c, ccb=ccb, sqb=sqb, pbank=pbank, ntt=ntt):
                    ins = None
                    for tt in range(ntt):
                        ins = e.matmul(pbank((tt, tt + 1)).ap, lhsT=ccb.ap[:, tt * 128:(tt + 1) * 128], rhs=self.ONES((0, 1)).ap,
                                       start=(c == 0 and tt == 0), stop=(c == 3), skip_group_check=True)
                    for tt in range(ntt):
                        ins = e.matmul(pbank((4 + tt, 5 + tt)).ap, lhsT=sqb.ap[:, tt * 128:(tt + 1) * 128], rhs=self.ONES((0, 1)).ap,
                                       start=False, stop=(c == 3), skip_group_check=True)
                    return ins
                self.op("pe", fn, reads=[ccb, sqb, self.ONES()], writes=[pst_])
            m4 = self.rotate("r4", self.R4)((0, ntt))
            self.TS("dve", m4, pbank((0, ntt)), 1.0 / 512, ALU.mult)
            v4 = self.rotate("r4", self.R4)((0, ntt))
            self.TT("dve", v4, m4, m4, ALU.mult)
            self.TS("dve", v4, v4, EPS, ALU.subtract)
            self.STT("dve", v4, pbank((4, 4 + ntt)), 1.0 / 512, v4, ALU.mult, ALU.subtract)
            self.TT("pool", v4, v4, self.NEGH((0, ntt)), ALU.pow)
            mr4 = self.rotate("r4", self.R4)((0, ntt))
            self.TT("dve", mr4, m4, v4, ALU.mult)
            ln_state.append((b, n, ntt, v4, mr4))
        for (b, n, ntt, v4, mr4) in ln_state:
            prR = self.bcast_rows(v4, ntt, n)
            prM = self.bcast_rows(mr4, ntt, n)
            pend = None
            for c in range(4):
                tn = self.rotate("td", self.TD)((0, n))
                self.TT("dve", tn, self.CC(c, b.cols), prR, ALU.mult)
                self.TT("dve", tn, tn, prM, ALU.subtract)
                zh = self.rotate("ta", self.TA)((0, n))
                self.ACT(AF.Identity, zh, tn, scale=self.vec(V_CLG + i * 4 + c), bias=self.vec(V_CLB + i * 4 + c))
                th = self.rotate("ta", self.TA)((0, n))
                self.ACT(AF.Tanh, th, tn, scale=self.vec(V_CLG + i * 4 + c), bias=self.vec(V_CLB + i * 4 + c))
                if pend is not None:
                    pend()
                pend = (lambda c=c, th=th, zh=zh, b=b: self.STT("dve", self.YH(c, b.cols), th, 1.0, zh, ALU.add, ALU.mult))
            pend()
        self.out_proj(self.w_out_odd[i], lambda b: self.norm_block(V_GFFN + 8 * l, b))

    def ffn(self, l):
        ps = self.ps
        blks = self.blks
        wd = self.w_ffn_in[l]
        wo = self.w_ffn_out[l]
        parts = [list(range(0, 4)), list(range(4, 8)), list(range(8, 12)), list(range(12, 16)),
                 list(range(16, 19)), list(range(19, 22))]
        self.wprefetch()

        def ffn_in(pi):
            part = parts[pi]
            pend = None
            for jj, j in enumerate(part):
                hs = (pi % 2) * 4 + jj
                sg = self.wget(self.w_in(wd, j * 128))
                su = self.wget(self.w_in(wd, DFF + j * 128))
                R = self.rotate("r", self.R)
                SRb = self.rotate("sr", self.SR)
                self.ACT(AF.Copy, R((HO - 2, HO)), self.CF(l, j))
                if ps == 1:
                    self.ACT(AF.Copy, SRb(None, (SH - 2, SH)), self.FSI(j).r("p (s r) -> p s r", s=16))
                w0 = self.vec(V_FW + (l * 3 + 0) * NJ + j)
                w1 = self.vec(V_FW + (l * 3 + 1) * NJ + j)
                w2 = self.vec(V_FW + (l * 3 + 2) * NJ + j)
                for b in blks:
                    self.need(b)
                    pg = self.pst(self.PSIN, "pin", b)
                    self.MM(pg, [(self.wk(sg, k), self.xcols(self.XN, k, b)) for k in range(KC)])
                    pu = self.pst(self.PSIN, "pin", b)
                    self.MM(pu, [(self.wk(su, k), self.xcols(self.XN, k, b)) for k in range(KC)])
                    self.ACT(AF.Copy, self.row(R, SRb, b, 0), pg)
                    t0 = self.tile(self.TA, "ta", b)
                    self.ACT(AF.Identity, t0, pg, scale=w2)
                    t1 = self.tile(self.TD, "td", b)
                    self.STT("dve", t1, self.row(R, SRb, b, 1), w1, t0, ALU.mult, ALU.add)
                    self.STT("dve", t1, self.row(R, SRb, b, 2), w0, t1, ALU.mult, ALU.add)
                    if pend is not None:
                        pend()

                    def fin(t1=t1, pu=pu, b=b, hs=hs):
                        gl = self.tile(self.TA, "ta", b)
                        self.ACT(AF.Gelu_apprx_tanh, gl, t1)
                        self.TT("dve", self.xcols(self.YH, hs, b), gl, pu, ALU.mult)
                    pend = fin
                self.ACT(AF.Copy, self.CF(l, j), R((HO + NPP - 2, HO + NPP)))
                if ps == 1:
                    self.ACT(AF.Copy, self.FSO(j).r("p (s r) -> p s r", s=16), SRb(None, (SH + 6, SH + 8)))
                self.wrel(2)
            if pend is not None:
                pend()

        def ffn_out(pi, last):
            part = parts[pi]
            slots = [self.wget(("out", wo[j * 128:(j + 1) * 128, :])) for j in part]
            prev = None
            for b in blks:
                for m in range(KC):
                    po = self.pst(self.PSAUX, "aux", b)
                    self.MM(po, [(self.RING(slots[jj], (m * 128, (m + 1) * 128)), self.xcols(self.YH, (pi % 2) * 4 + jj, b))
                                 for jj in range(len(part))])
                    xm = self.xcols(self.X, m, b)
                    self.TT("dve", xm, xm, po, ALU.add)
                if last and prev is not None:
                    self.next_norm(l, prev)
                prev = b
            if last:
                if l == self.depth - 1 or (l + 1) % 2 == 1:
                    self.next_norm(l, prev)
                else:
                    self.pending[prev.index] = (lambda prev=prev: self.next_norm(l, prev))
            self.wrel(len(part))

        ffn_in(0)
        for pi in range(len(parts)):
            if pi + 1 < len(parts):
                ffn_in(pi + 1)
            else:
                if ps == 1:
                    self.dma("sp", self.o_fp[l], self.CF(l).r("p c r -> p (c r)"))
                    self.dma("sp", self.o_fs[l], self.FSO().r("p j f -> p (j f)"))
            ffn_out(pi, pi == len(parts) - 1)


def th_slice(acc, a, b):
    return Acc(acc.buf, acc.ap[:, a:b], acc.lo + a, acc.lo + b)


def ps_slice(acc, a, b):
    return Acc(acc.buf, acc.ap[:, a:b], acc.lo + a, acc.lo + b)


def _consts():
    s = np.arange(128)
    maskp = (s[:, None] <= s[None, :]).astype(np.float32)
    masks = ((s[:, None] // 8 == s[None, :] // 8) & (s[:, None] % 8 <= s[None, :] % 8)).astype(np.float32)
    ident = np.eye(128, dtype=np.float32)
    invc = np.zeros((4, 16), np.float32)
    for g, win in enumerate((2, 4, 8, 16)):
        invc[g] = 1.0 / np.minimum(np.arange(16) + 1, win)
    return np.stack([maskp, masks]), ident, invc


def _pack_vecs(inp):
    def pk(a):
        a = np.asarray(a, np.float32)
        lead = a.shape[:-1]
        n = a.shape[-1] // 128
        a = a.reshape(lead + (n, 128))
        a = np.moveaxis(a, -1, 0)
        return a.reshape(128, -1)
    cols = [pk(inp["norm_mix_g"]), pk(inp["norm_ffn_g"]), pk(inp["norm_final_g"]),
            pk(inp["b_conv_w"]), pk(inp["c_conv_w"]), pk(inp["c_conv_b"]), pk(inp["c_ln_g"]),
            pk(inp["c_ln_b"]), pk(inp["d_scale"]), pk(inp["ffn_conv_w"]), pk(inp["a_ln_g"])]
    v = np.concatenate(cols, axis=1)
    assert v.shape == (128, NV), v.shape
    return np.ascontiguousarray(v)


_CACHE = {}


def kernel(**inp):
    inp = {k: np.asarray(v) for k, v in inp.items()}
    key = "nc"
    if key not in _CACHE:
        _CACHE[key] = Builder().build()
    nc = _CACHE[key]
    masks, ident, invc = _consts()
    vecs = _pack_vecs(inp)
    ws = inp["a_ws"].astype(np.float32)
    wst = np.zeros((2, 4, 2, 128, 128), np.float32)
    wst[:, :, 0] = np.swapaxes(ws, -1, -2)
    wst[:, :, 1] = np.tile(np.swapaxes(ws[:, :, :8, :8], -1, -2), (1, 1, 16, 16))
    bs = inp["a_bs"].astype(np.float32)
    bsb = np.concatenate([bs, np.tile(bs[:, :, :8], (1, 1, 16))], axis=-1)
    shared = dict(
        w_in_even=inp["w_in_even"], w_out_even=inp["w_out_even"], w_in_odd=inp["w_in_odd"],
        w_out_odd=inp["w_out_odd"], w_ffn_in=inp["w_ffn_in"], w_ffn_out=inp["w_ffn_out"],
        d_proj=inp["d_proj"], vecs=vecs, lng=inp["a_ln_g"], bsb=np.ascontiguousarray(bsb), wst=wst,
        masks=masks, ident=ident, invc=invc)
    shared = {k: np.ascontiguousarray(v, dtype=np.float32) for k, v in shared.items()}
    in_maps = []
    for c in range(NCORES):
        sl = slice(16 * c, 16 * c + 16)
        xs = inp["x_sample"][sl].reshape(128, D)
        xT = np.ascontiguousarray(np.concatenate([inp["x_prompt"][c], xs], axis=0).T, dtype=np.float32)
        m = dict(shared)
        m["xT"] = xT
        def st(a):
            n, _, r, ch = a.shape
            a = a.reshape(n, 16, r, ch // 128, 128)
            return np.ascontiguousarray(np.transpose(a, (0, 4, 3, 1, 2)), dtype=np.float32)
        m["st_b"] = st(inp["state_b_conv"][:, sl]).reshape(2, 128, 128)
        m["st_c"] = st(inp["state_c_conv"][:, sl]).reshape(2, 128, 4, 480)
        m["st_d"] = st(inp["state_d_pool"][:, sl]).reshape(2, 128, 4, 240)
        m["st_f"] = st(inp["state_ffn_conv"][:, sl]).reshape(4, 128, NJ * 32)
        in_maps.append(m)
    res = run_bass_kernel_spmd(nc, in_maps, core_ids=list(range(NCORES)))
    R = res.results
    y_prompt = np.stack([R[c]["yT"][:, :2048].T for c in range(NCORES)])
    y_sample = np.concatenate([R[c]["yT"][:, 2048:].T.reshape(16, 8, D) for c in range(NCORES)])
    a_s = np.concatenate([R[c]["o_av"].reshape(2, 16, 8, 512) for c in range(NCORES)], axis=1)

    def pr(name, C, r):
        outs_ = []
        for c in range(NCORES):
            a = R[c][name]
            n = a.shape[0]
            a = a.reshape(n, 128, C, r)
            outs_.append(np.transpose(a, (0, 3, 2, 1)).reshape(n, r, C * 128))
        return np.stack(outs_, axis=1)

    def sm(name, C, r):
        outs_ = []
        for c in range(NCORES):
            a = R[c][name]
            n = a.shape[0]
            a = a.reshape(n, 128, C, 16, r)
            outs_.append(np.transpose(a, (0, 3, 4, 2, 1)).reshape(n, 16, r, C * 128))
        return np.concatenate(outs_, axis=1)
    outs = (y_prompt, y_sample, a_s, pr("o_bp", 4, 2), sm("o_bs", 4, 2), pr("o_cp", 4, 30), sm("o_cs", 4, 30),
            pr("o_dp", 4, 15), sm("o_ds", 4, 15), pr("o_fp", NJ, 2), sm("o_fs", NJ, 2))
    return tuple(np.ascontiguousarray(o, dtype=np.float32) for o in outs)
```

```python
import numpy as np
from contextlib import ExitStack
import concourse.bass as bass
import concourse.mybir as mybir
from concourse.bass_utils import run_bass_kernel_spmd

F32 = mybir.dt.float32
BF16 = mybir.dt.bfloat16
AF = mybir.ActivationFunctionType
ALU = mybir.AluOpType

D = 1024
KC = 8
DEPTH = 4
DFF = 2816
NJ = 22
NPP = 1024
NTM = 1152
HO = 32
SH = 30
SW = 40
EPS = 1e-6
NSLOT = 14
NCORES = 8

V_GMIX = 0
V_GFFN = 32
V_GFIN = 64
V_BW = 72
V_CW = 96
V_CB = 344
V_CLG = 352
V_CLB = 360
V_DSC = 368
V_FW = 376
V_ALG = 640
NV = 648


class Buf:
    _n = 0

    def __init__(self, t, free_shape, cell, name):
        self.t = t
        self.shape = list(free_shape)
        self.cell = cell
        self.strides = []
        s = 1
        for d in reversed(self.shape):
            self.strides.insert(0, s)
            s *= d
        self.size = s
        self.id = Buf._n
        Buf._n += 1
        self.name = name
        if len(self.shape) == 1:
            self._flat = None
        else:
            names = " ".join(f"d{i}" for i in range(len(self.shape)))
            self._flat = f"p {names} -> p ({names})"

    def __call__(self, *idx, p=None):
        idx = list(idx) + [None] * (len(self.shape) - len(idx))
        sl = []
        lo = 0
        hi = 0
        for i, n, st in zip(idx, self.shape, self.strides):
            if i is None:
                a, b = 0, n
                sl.append(slice(None))
            elif isinstance(i, tuple):
                a, b = i
                assert 0 <= a < b <= n, (self.name, idx, self.shape)
                sl.append(slice(a, b))
            else:
                a, b = i, i + 1
                assert 0 <= a < n, (self.name, idx, self.shape)
                sl.append(i)
            lo += a * st
            hi += (b - 1) * st
        hi += 1
        ps = slice(None) if p is None else slice(p[0], p[1])
        return Acc(self, self.t[tuple([ps] + sl)], lo, hi)

    def flat(self, lo, hi, p=None):
        ps = slice(None) if p is None else slice(p[0], p[1])
        full = self.t[tuple([slice(None)] * (len(self.shape) + 1))]
        if self._flat is not None:
            full = full.rearrange(self._flat)
        return Acc(self, full[ps, lo:hi], lo, hi)


class Acc:
    def __init__(self, buf, ap, lo, hi):
        self.buf = buf
        self.ap = ap
        self.lo = lo
        self.hi = hi

    def cells(self):
        c = self.buf.cell
        return range(self.lo // c, (self.hi - 1) // c + 1)

    def r(self, pattern, **kw):
        return Acc(self.buf, self.ap.rearrange(pattern, **kw), self.lo, self.hi)

    def s3(self):
        return self.r("p (s t) -> p s t", s=16)


class Op:
    __slots__ = ("eng", "fn", "kind", "chan", "chan_idx", "seq", "waits", "signal", "sigcount", "id")


class Sched:
    ENGS = ("pe", "act", "dve", "pool", "sp")

    def __init__(self):
        self.ops = []
        self.state = {}
        self.eng_ops = {e: [] for e in self.ENGS}
        self.waited = {e: {} for e in self.ENGS}
        self.waited_dma = {e: {} for e in self.ENGS}
        self.chan_count = {}

    def _add_dep(self, op, d):
        if d is None or d is op:
            return
        E = op.eng
        if d.kind == "dma":
            if self.waited_dma[E].get(d.chan, 0) >= d.chan_idx:
                return
            cur = op.waits.get(("dma", d.chan))
            if cur is None or cur.chan_idx < d.chan_idx:
                op.waits[("dma", d.chan)] = d
        else:
            if self.waited[E].get(d.eng, -1) >= d.seq:
                return
            cur = op.waits.get(("eng", d.eng))
            if cur is None or cur.seq < d.seq:
                op.waits[("eng", d.eng)] = d

    def op(self, eng, fn, reads=(), writes=(), dma=False, chan=None):
        o = Op()
        o.eng = eng
        o.fn = fn
        o.kind = "dma" if dma else "cmp"
        o.chan = chan
        o.waits = {}
        o.signal = False
        o.id = len(self.ops)
        o.seq = len(self.eng_ops[eng])
        if dma:
            n = self.chan_count.get(chan, 0) + 1
            self.chan_count[chan] = n
            o.chan_idx = n
            o.signal = True
        for a in reads:
            for c in a.cells():
                st = self.state.get((a.buf.id, c))
                if st is not None and st[0] is not None:
                    self._add_dep(o, st[0])
        for a in writes:
            for c in a.cells():
                st = self.state.get((a.buf.id, c))
                if st is None:
                    continue
                d = st[0]
                if d is not None and (d.kind == "dma" or dma or d.eng != eng):
                    self._add_dep(o, d)
                for r in st[1].values():
                    if r.kind == "dma" or dma or r.eng != eng:
                        self._add_dep(o, r)
        if dma and o.chan_idx > 1:
            cur = o.waits.get(("dma", chan))
            if self.waited_dma[eng].get(chan, 0) < o.chan_idx - 1 and (cur is None or cur.chan_idx < o.chan_idx - 1):
                prev = Op()
                prev.kind = "dma"
                prev.chan = chan
                prev.chan_idx = o.chan_idx - 1
                o.waits[("dma", chan)] = prev
        for key, d in o.waits.items():
            if key[0] == "dma":
                self.waited_dma[eng][d.chan] = max(self.waited_dma[eng].get(d.chan, 0), d.chan_idx)
            else:
                d.signal = True
                self.waited[eng][d.eng] = max(self.waited[eng].get(d.eng, -1), d.seq)
        for a in writes:
            for c in a.cells():
                self.state[(a.buf.id, c)] = [o, {}]
        for a in reads:
            for c in a.cells():
                st = self.state.get((a.buf.id, c))
                if st is None:
                    st = [None, {}]
                    self.state[(a.buf.id, c)] = st
                st[1][eng if not dma else ("dma", o.id)] = o
        self.ops.append(o)
        self.eng_ops[eng].append(o)
        return o

    def emit(self, nc, sems, chan_sems):
        for e, lst in self.eng_ops.items():
            n = 0
            for o in lst:
                if o.kind == "cmp" and o.signal:
                    n += 1
                o.sigcount = n
        finals = list(self.chan_count.items())

        class _First:
            def __init__(self, e):
                self._e = e
                self.first = None

            def __getattr__(self, name):
                attr = getattr(self._e, name)
                if not callable(attr):
                    return attr

                def w(*a, **k):
                    r = attr(*a, **k)
                    if self.first is None and r is not None:
                        self.first = r
                    return r
                return w

        def run(engobj, ename):
            proxy = _First(engobj)
            for o in self.eng_ops[ename]:
                waits = list(o.waits.items())
                fuse = None
                if o.kind == "cmp" and waits:
                    fuse = waits.pop()
                for key, d in waits:
                    if key[0] == "dma":
                        engobj.wait_ge(chan_sems[d.chan], 16 * d.chan_idx)
                    else:
                        engobj.wait_ge(sems[d.eng], d.sigcount)
                proxy.first = None
                ins = o.fn(proxy)
                if fuse is not None:
                    key, d = fuse
                    if key[0] == "dma":
                        proxy.first._wait_ge(chan_sems[d.chan], 16 * d.chan_idx)
                    else:
                        proxy.first._wait_ge(sems[d.eng], d.sigcount)
                if o.kind == "dma":
                    ins.then_inc(chan_sems[o.chan], 16)
                elif o.signal:
                    ins.then_inc(sems[o.eng], 1)
            if ename == "sp":
                for c, n in finals:
                    engobj.wait_ge(chan_sems[c], 16 * n)

        with nc.Block() as block:
            @block.tensor
            def _(t):
                run(t, "pe")

            @block.scalar
            def _(a):
                run(a, "act")

            @block.vector
            def _(v):
                run(v, "dve")

            @block.gpsimd
            def _(g):
                run(g, "pool")

            @block.sync
            def _(s):
                run(s, "sp")


class Blk:
    def __init__(self, c0, n, kind):
        self.c0 = c0
        self.n = n
        self.kind = kind
        self.cols = (c0, c0 + n)


class Builder:
    def __init__(self, depth=DEPTH, dbg=None):
        self.depth = depth
        self.dbg = dbg
        self.nc = bass.Bass("TRN2", target_bir_lowering=False)
        self.S = Sched()
        self.es = ExitStack()
        self.dry = False
        self.nchan = 0
        self.gen_chans = {}
        self.rot = {}

    def dram(self, name, shape, kind="ExternalInput", dt=F32):
        return self.nc.dram_tensor(name, list(shape), dt, kind=kind).ap()

    def sb(self, name, shape, dt=F32, cell=128):
        t = self.es.enter_context(self.nc.sbuf_tensor(name, [128] + list(shape), dt))
        return Buf(t, shape, cell, name)

    def psb(self, name):
        t = self.es.enter_context(self.nc.psum_tensor(name, [128, 512], F32))
        return Buf(t, [512], 512, name)

    def rotate(self, key, lst):
        i = self.rot.get(key, 0)
        self.rot[key] = i + 1
        return lst[i % len(lst)]

    def op(self, eng, fn, reads=(), writes=(), **kw):
        if self.dry:
            return None
        return self.S.op(eng, fn, reads=reads, writes=writes, **kw)

    def dma(self, eng, out, in_, reads=(), writes=(), chan=None):
        if self.dry:
            return
        if chan is None:
            chan = self.rotate("genchan_" + eng, self.gen_chans[eng])
        o_ap = out.ap if isinstance(out, Acc) else out
        i_ap = in_.ap if isinstance(in_, Acc) else in_
        rd = list(reads) + ([in_] if isinstance(in_, Acc) else [])
        wr = list(writes) + ([out] if isinstance(out, Acc) else [])
        self.S.op(eng, lambda e: e.dma_start(out=o_ap, in_=i_ap), reads=rd, writes=wr, dma=True, chan=chan)

    def ACT(self, func, out, in_, scale=None, bias=None):
        kw = {}
        rd = [in_]
        if scale is not None:
            if isinstance(scale, Acc):
                kw["scale"] = scale.ap
                rd.append(scale)
            else:
                kw["scale"] = float(scale)
        if bias is not None:
            if isinstance(bias, Acc):
                kw["bias"] = bias.ap
                rd.append(bias)
            else:
                kw["bias"] = float(bias)
        self.op("act", lambda e: e.activation(out=out.ap, in_=in_.ap, func=func, **kw), reads=rd, writes=[out])

    def TT(self, eng, out, a, b, op):
        self.op(eng, lambda e: e.tensor_tensor(out=out.ap, in0=a.ap, in1=b.ap, op=op), reads=[a, b], writes=[out])

    def TS(self, eng, out, a, s1, op0, s2=None, op1=None):
        rd = [a]
        v1 = s1
        v2 = s2
        if isinstance(s1, Acc):
            rd.append(s1)
            v1 = s1.ap
        if isinstance(s2, Acc):
            rd.append(s2)
            v2 = s2.ap
        if op1 is None:
            self.op(eng, lambda e: e.tensor_scalar(out=out.ap, in0=a.ap, scalar1=v1, scalar2=None, op0=op0), reads=rd, writes=[out])
        else:
            self.op(eng, lambda e: e.tensor_scalar(out=out.ap, in0=a.ap, scalar1=v1, scalar2=v2, op0=op0, op1=op1), reads=rd, writes=[out])

    def STT(self, eng, out, a, s, b, op0, op1):
        rd = [a, b]
        v = s
        if isinstance(s, Acc):
            rd.append(s)
            v = s.ap
        self.op(eng, lambda e: e.scalar_tensor_tensor(out=out.ap, in0=a.ap, scalar=v, in1=b.ap, op0=op0, op1=op1), reads=rd, writes=[out])

    def MM(self, out, pairs, extra_reads=()):
        rd = []
        for l, r in pairs:
            rd.append(l)
            rd.append(r)
        rd += list(extra_reads)
        n = len(pairs)

        def fn(e):
            ins = None
            for i, (l, r) in enumerate(pairs):
                ins = e.matmul(out.ap, lhsT=l.ap, rhs=r.ap, start=(i == 0), stop=(i == n - 1))
            return ins
        self.op("pe", fn, reads=rd, writes=[out])

    def MMS(self, items):
        rd = []
        wr = []
        for o, l, r in items:
            rd += [l, r]
            wr.append(o)

        def fn(e):
            ins = None
            for o, l, r in items:
                ins = e.matmul(o.ap, lhsT=l.ap, rhs=r.ap, start=True, stop=True)
            return ins
        self.op("pe", fn, reads=rd, writes=wr)

    def wreset(self):
        self.w_idx = 0
        self.w_issued = 0
        self.w_released = set()
        self.w_oldest = 0

    def wget(self, spec):
        i = self.w_idx
        self.w_idx += 1
        if self.dry:
            self.plan.append(spec)
            return i % NSLOT
        assert i < len(self.plan)
        while self.w_issued <= i:
            assert self.w_issued - self.w_oldest < NSLOT, "ring overflow"
            self._wissue()
        return i % NSLOT

    def _wissue(self):
        i = self.w_issued
        kind, src = self.plan[i]
        slot = i % NSLOT
        if kind == "in":
            dst = self.RING(slot).r("p (k f) -> p k f", k=KC)
        else:
            dst = self.RING(slot)
        self.dma("pool", dst, src, chan=("ring", slot))
        self.w_issued += 1

    def wrel(self, n=1):
        if self.dry:
            return
        self.w_oldest += n
        self.wprefetch()

    def wprefetch(self):
        if self.dry:
            return
        while self.w_issued < len(self.plan) and self.w_issued - self.w_oldest < NSLOT:
            self._wissue()

    def w_in(self, wd, f0):
        return ("in", wd[:, f0:f0 + 128].rearrange("(k p) f -> p k f", p=128))

    def wk(self, slot, k):
        return self.RING(slot, (k * 128, (k + 1) * 128))

    def build(self):
        nc = self.nc
        dram = self.dram
        self.xT = dram("xT", [D, 2176])
        self.yT = dram("yT", [D, 2176], "ExternalOutput")
        self.st_b = dram("st_b", [2, 128, 128])
        self.st_c = dram("st_c", [2, 128, 4, 480])
        self.st_d = dram("st_d", [2, 128, 4, 240])
        self.st_f = dram("st_f", [4, 128, NJ * 32])
        self.w_in_even = dram("w_in_even", [2, D, 2560])
        self.w_out_even = dram("w_out_even", [2, D, D])
        self.w_in_odd = dram("w_in_odd", [2, D, 1536])
        self.w_out_odd = dram("w_out_odd", [2, D, D])
        self.w_ffn_in = dram("w_ffn_in", [4, D, 2 * DFF])
        self.w_ffn_out = dram("w_ffn_out", [4, DFF, D])
        self.d_proj = dram("d_proj", [2, 4, 128, 128])
        self.vecs = dram("vecs", [128, NV])
        self.lng = dram("lng", [2, 512])
        self.bsb = dram("bsb", [2, 4, 256])
        self.wst = dram("wst", [2, 4, 2, 128, 128])
        self.masks = dram("masks", [2, 128, 128])
        self.ident = dram("ident", [128, 128])
        self.invc = dram("invc", [4, 16])
        self.o_av = dram("o_av", [2, 128, 512], "ExternalOutput")
        self.o_bp = dram("o_bp", [2, 128, 8], "ExternalOutput")
        self.o_bs = dram("o_bs", [2, 128, 128], "ExternalOutput")
        self.o_cp = dram("o_cp", [2, 128, 120], "ExternalOutput")
        self.o_cs = dram("o_cs", [2, 128, 4, 480], "ExternalOutput")
        self.o_dp = dram("o_dp", [2, 128, 60], "ExternalOutput")
        self.o_ds = dram("o_ds", [2, 128, 4, 240], "ExternalOutput")
        self.o_fp = dram("o_fp", [4, 128, NJ * 2], "ExternalOutput")
        self.o_fs = dram("o_fs", [4, 128, NJ * 32], "ExternalOutput")
        if self.dbg is not None:
            self.o_dbg = dram("o_dbg", [D, 2176], "ExternalOutput")

        sb = self.sb
        self.X = sb("X", [KC, NTM])
        self.XN = sb("XN", [KC, NTM], BF16)
        self.YH = sb("YH", [KC, NTM], BF16)
        self.R = [sb(f"R{i}", [HO + NPP]) for i in range(3)]
        self.RB = [sb(f"RB{i}", [HO + NPP], BF16) for i in range(2)]
        self.SR = [sb(f"SR{i}", [16, SW], cell=16 * SW) for i in range(3)]
        self.SRB = [sb(f"SRB{i}", [16, SW], BF16, cell=16 * SW) for i in range(2)]
        self.TA = [sb(f"TA{i}", [512], cell=512) for i in range(4)]
        self.TD = [sb(f"TD{i}", [512], cell=512) for i in range(3)]
        self.SQ = [sb(f"SQ{i}", [512], BF16, cell=512) for i in range(2)]
        self.SQ2 = [sb(f"SQ2_{i}", [512], BF16, cell=512) for i in range(2)]
        self.RING = sb("RING", [NSLOT, 1024], BF16, cell=1024)
        self.DIAG = [sb(f"DIAG{i}", [31, 128], BF16, cell=31 * 128) for i in range(2)]
        self.CC = sb("CC", [4, NTM])
        self.VEC = sb("VEC", [NV], cell=8)
        self.LNG = sb("LNG", [512], cell=512)
        self.BSB = sb("BSB", [4, 256], cell=256)
        self.WST = sb("WST", [2, 4, 2, 128], BF16, cell=128)
        self.MSK = sb("MSK", [2, 128], cell=128)
        self.IDENT = sb("IDENT", [128], cell=128)
        self.ONES = sb("ONES", [128], BF16, cell=128)
        self.NEGH = sb("NEGH", [8], cell=8)
        self.ONESF = sb("ONESF", [128], cell=128)
        self.R4 = [sb(f"R4_{i}", [8], cell=8) for i in range(8)]
        self.INVC = sb("INVC", [4, 16], cell=64)
        self.DPROJ = sb("DPROJ", [2, 4, 128], BF16, cell=128)
        self.CB = sb("CB", [2, 4, 2], cell=2)
        self.CC2 = sb("CC2", [2, 4, 30], BF16, cell=30)
        self.CD = sb("CD", [2, 4, 15], cell=15)
        self.CF = sb("CF", [4, NJ, 2], cell=2)
        self.GT = sb("GT", [2, 4, 30], cell=30)
        self.SI = sb("SI", [480], cell=480)
        self.SO = sb("SO", [480], cell=480)
        self.FSI = sb("FSI", [NJ, 32], cell=32)
        self.FSO = sb("FSO", [NJ, 32], cell=32)
        self.STT6 = [sb(f"ST6_{i}", [8], cell=8) for i in range(3)]
        self.MV = [sb(f"MV{i}", [4], cell=4) for i in range(3)]
        self.PS = [self.psb(f"PS{i}") for i in range(8)]
        self.PSIN = self.PS[0:5]
        self.PSAUX = self.PS[5:8]

        self.gen_chans = {"sp": [("g", i) for i in range(24)], "pool": [("q", i) for i in range(12)]}

        self.plan = []
        self.dry = True
        self.wreset()
        self.program()
        self.dry = False
        self.rot = {}
        self.wreset()
        self.program()

        sems = {e: self.es.enter_context(nc.semaphore("s_" + e)) for e in ("pe", "act", "dve", "pool")}
        chans = {}
        for c in list(self.S.chan_count.keys()):
            chans[c] = self.es.enter_context(nc.semaphore("c_%s_%s" % c))
        self.S.emit(nc, sems, chans)
        self.es.close()
        return nc

    def vec(self, c):
        return self.VEC((c, c + 1))

    def program(self):
        self.setup()
        for ps in range(2):
            self.run_pass(ps)

    def setup(self):
        dma = self.dma
        dma("sp", self.VEC(), self.vecs[:, :])
        dma("sp", self.MSK(), self.masks.rearrange("m p f -> p m f"))
        dma("sp", self.IDENT(), self.ident[:, :])
        dma("sp", self.INVC(), self.invc.partition_broadcast(128))
        dma("pool", self.DPROJ(), self.d_proj.rearrange("i g c d -> c i g d"))
        self.op("dve", lambda e: e.memset(self.ONES().ap, 1.0), writes=[self.ONES()])
        self.op("dve", lambda e: e.memset(self.NEGH().ap, -0.5), writes=[self.NEGH()])
        self.op("dve", lambda e: e.memset(self.ONESF().ap, 1.0), writes=[self.ONESF()])
        for b in (self.CB, self.CD, self.CF, self.CC2):
            self.op("dve", lambda e, b=b: e.memset(b().ap, 0.0), writes=[b()])
        g = self.VEC((V_GMIX, V_GFIN + 8))
        self.TS("dve", g, g, 32.0, ALU.mult)
        h = self.VEC((V_CLG, V_CLB + 8))
        self.TS("dve", h, h, 0.5, ALU.mult)
        stg = self.CC.flat(0, 2048)
        dma("sp", Acc(self.CC, stg.ap.rearrange("p (a t) -> p a t", a=16), 0, 2048), self.wst.rearrange("i h k s t -> s (i h k) t"))
        v5 = stg.ap.rearrange("p (i h k t) -> p i h k t", i=2, h=4, k=2)
        for i in range(2):
            for kd in range(2):
                src = Acc(self.CC, v5[:, i, :, kd, :], 0, 2048)
                m0 = self.MSK(kd)
                mb = Acc(self.MSK, m0.ap.rearrange("p (o t) -> p o t", o=1).broadcast_to([128, 4, 128]), m0.lo, m0.hi)
                self.TT("dve", self.WST(i, None, kd), src, mb, ALU.mult)

    def blocks(self, ps):
        b = [Blk(0, 512, "p"), Blk(512, 512, "p")]
        if ps == 1:
            b.append(Blk(1024, 128, "s"))
        return b

    def run_pass(self, ps):
        self.ps = ps
        blks = self.blocks(ps)
        self.blks = blks
        nt = blks[-1].c0 + blks[-1].n
        self.nt = nt
        for b in blks:
            for k in range(KC):
                if b.kind == "p":
                    g0 = ps * NPP + b.c0
                else:
                    g0 = 2048
                self.dma("sp" if ps == 0 else "pool", self.X(k, b.cols), self.xT[k * 128:(k + 1) * 128, g0:g0 + b.n])
        self.pending = {}
        self.lookahead = 1
        for bi, b in enumerate(blks):
            b.index = bi
            self.pending[bi] = (lambda b=b: self.norm_block(V_GMIX + 0, b))
        for l in range(self.depth):
            if l % 2 == 0:
                self.even_mixer(l)
            else:
                self.odd_mixer(l)
            self.ffn(l)
        if self.dbg is not None:
            for k in range(KC):
                self.dma("sp", self.o_dbg[k * 128:(k + 1) * 128, ps * NPP:(ps + 1) * NPP], self.X(k, (0, NPP)))
                if ps == 1:
                    self.dma("sp", self.o_dbg[k * 128:(k + 1) * 128, 2048:2176], self.X(k, (NPP, NPP + 128)))

    def bv(self, acc, b):
        return acc.s3() if b.kind == "s" else acc

    def tile(self, pool, key, b):
        t = self.rotate(key, pool)((0, b.n))
        return self.bv(t, b)

    def pst(self, lst, key, b):
        t = self.rotate(key, lst)((0, b.n))
        return self.bv(t, b)

    def xcols(self, buf, k, b):
        return self.bv(buf(k, b.cols), b)

    def row(self, R, SRb, b, shift):
        if b.kind == "p":
            return R((HO + b.c0 - shift, HO + b.c0 - shift + b.n))
        return SRb(None, (SH - shift, SH + 8 - shift))

    def bcast_rows(self, r4, ntt, n):
        rep = self.rotate("ta", self.TA)((0, n))
        for tt in range(ntt):
            self.ACT(AF.Identity, th_slice(rep, tt * 128, (tt + 1) * 128), self.ONESF(), scale=th_slice(r4, tt, tt + 1))
        pr = self.rotate("aux", self.PSAUX)((0, n))

        def fnt(e, pr=pr, rep=rep, ntt=ntt):
            ins = None
            for tt in range(ntt):
                ins = e.transpose(pr.ap[:, tt * 128:(tt + 1) * 128], rep.ap[:, tt * 128:(tt + 1) * 128], self.IDENT().ap)
            return ins
        self.op("pe", fnt, reads=[rep, self.IDENT()], writes=[pr])
        return pr

    def norm_block(self, gcol, b, final=False):
        n = b.n
        ntt = n // 128
        pbank = self.rotate("aux", self.PSAUX)
        pss = pbank((0, ntt))
        for k in range(KC):
            sq = self.rotate("sq", self.SQ)((0, n))
            self.ACT(AF.Square, sq, self.X(k, b.cols))

            def fn(e, k=k, sq=sq, pbank=pbank, ntt=ntt):
                ins = None
                for tt in range(ntt):
                    ins = e.matmul(pbank((tt, tt + 1)).ap, lhsT=sq.ap[:, tt * 128:(tt + 1) * 128], rhs=self.ONES((0, 1)).ap,
                                   start=(k == 0 and tt == 0), stop=(k == KC - 1), skip_group_check=True)
                return ins
            self.op("pe", fn, reads=[sq, self.ONES()], writes=[pss])
        r4 = self.rotate("r4", self.R4)((0, ntt))
        self.TS("dve", r4, pss, 1024.0 * EPS, ALU.add)
        self.TT("pool", r4, r4, self.NEGH((0, ntt)), ALU.pow)
        rs = self.bcast_rows(r4, ntt, n)
        for k in range(KC):
            if not final:
                self.STT("dve", self.XN(k, b.cols), self.X(k, b.cols), self.vec(gcol + k), rs, ALU.mult, ALU.mult)
            else:
                o = self.rotate("ta", self.TA)((0, n))
                self.STT("dve", o, self.X(k, b.cols), self.vec(gcol + k), rs, ALU.mult, ALU.mult)
                if b.kind == "p":
                    c = self.ps * NPP + b.c0
                else:
                    c = 2048
                self.dma("sp", self.yT[k * 128:(k + 1) * 128, c:c + n], o)

    def rotate_last(self, key, lst):
        i = self.rot.get(key, 0)
        return lst[(i - 1) % len(lst)]

    def need(self, b):
        for bi in sorted(self.pending.keys()):
            if bi <= b.index + self.lookahead:
                fn = self.pending.pop(bi)
                fn()

    def need_all(self):
        for bi in sorted(self.pending.keys()):
            self.pending.pop(bi)()

    def next_norm(self, l, b):
        if l == self.depth - 1:
            self.norm_block(V_GFIN, b, final=True)
        else:
            self.norm_block(V_GMIX + 8 * (l + 1), b)

    def out_proj(self, wd, after):
        slots = [self.wget(self.w_in(wd, m * 128)) for m in range(KC)]
        prev = None
        for b in self.blks:
            for m in range(KC):
                po = self.pst(self.PSAUX, "aux", b)
                self.MM(po, [(self.wk(slots[m], k), self.xcols(self.YH, k, b)) for k in range(KC)])
                xm = self.xcols(self.X, m, b)
                self.TT("dve", xm, xm, po, ALU.add)
            if prev is not None:
                after(prev)
            prev = b
        after(prev)
        self.wrel(KC)

    def even_mixer(self, l):
        i = l // 2
        ps = self.ps
        blks = self.blks
        wd = self.w_in_even[i]
        self.dma("pool", self.BSB(), self.bsb[i].partition_broadcast(128))
        self.dma("pool", self.LNG(), self.lng[i].partition_broadcast(128))
        if ps == 1:
            self.dma("pool", self.SI((0, 128)), self.st_b[i])
            self.dma("pool", self.FSI(), self.st_f[l].rearrange("p (j f) -> p j f", j=NJ))
        self.wprefetch()
        sv = [self.wget(self.w_in(wd, 512 + q * 128)) for q in range(4)]
        ntile = self.nt // 128
        VOFF = 4 * NTM

        def vacc(tt, c0, c1, p=None):
            lo = tt * 256 + c0 // 2
            hi = tt * 256 + c1 // 2
            a_ = self.CC.flat(lo, hi)
            return Acc(self.CC, a_.ap.bitcast(BF16), lo, hi)

        b_steps = []
        st_ = {"vrel": False, "defer": 0}

        def rel3():
            if st_["vrel"]:
                self.wrel(3)
            else:
                st_["defer"] += 3

        def mk_c(c):
            cx = {}

            def setup():
                cx["scg"] = self.wget(self.w_in(wd, 1536 + c * 128))
                cx["sh"] = self.wget(self.w_in(wd, 2048 + c * 128))
                cx["sbg"] = self.wget(self.w_in(wd, 1024 + c * 128))
                cx["R"] = self.rotate("r", self.R)
                cx["SRb"] = self.rotate("sr", self.SR)
                self.ACT(AF.Copy, cx["R"]((HO - 2, HO)), self.CB(i, c))
                if ps == 1:
                    self.ACT(AF.Copy, cx["SRb"](None, (SH - 2, SH)), self.SI((c * 32, (c + 1) * 32)).r("p (s r) -> p s r", s=16))

            def item(b):
                self.need(b)
                R, SRb = cx["R"], cx["SRb"]
                w0 = self.vec(V_BW + (i * 3 + 0) * 4 + c)
                w1 = self.vec(V_BW + (i * 3 + 1) * 4 + c)
                w2 = self.vec(V_BW + (i * 3 + 2) * 4 + c)
                pcg = self.pst(self.PSIN, "pin", b)
                self.MM(pcg, [(self.wk(cx["scg"], k), self.xcols(self.XN, k, b)) for k in range(KC)])
                ph = self.pst(self.PSIN, "pin", b)
                self.MM(ph, [(self.wk(cx["sh"], k), self.xcols(self.XN, k, b)) for k in range(KC)])
                pbg = self.pst(self.PSIN, "pin", b)
                self.MM(pbg, [(self.wk(cx["sbg"], k), self.xcols(self.XN, k, b)) for k in range(KC)])
                cg = self.tile(self.TA, "ta", b)
                self.ACT(AF.Copy, cg, pcg)
                self.TT("dve", self.row(R, SRb, b, 0), cg, ph, ALU.mult)
                t0 = self.tile(self.TA, "ta", b)
                self.ACT(AF.Identity, t0, self.row(R, SRb, b, 0), scale=w2)
                t1 = self.tile(self.TD, "td", b)
                self.STT("dve", t1, self.row(R, SRb, b, 1), w1, t0, ALU.mult, ALU.add)
                self.STT("dve", t1, self.row(R, SRb, b, 2), w0, t1, ALU.mult, ALU.add)
                self.TT("dve", self.xcols(self.YH, 4 + c, b), t1, pbg, ALU.mult)

            def finish():
                R, SRb = cx["R"], cx["SRb"]
                self.ACT(AF.Copy, self.CB(i, c), R((HO + NPP - 2, HO + NPP)))
                if ps == 1:
                    self.ACT(AF.Copy, self.SO((c * 32, (c + 1) * 32)).r("p (s r) -> p s r", s=16), SRb(None, (SH + 6, SH + 8)))
                rel3()
            b_steps.append(("setup", c, setup))
            for b in blks:
                b_steps.append(("item", c, (lambda b=b: item(b))))
            b_steps.append(("finish", c, finish))
        for c in range(4):
            mk_c(c)

        def b_advance():
            while b_steps:
                kind, c, fn = b_steps[0]
                if kind == "setup" and c == 3 and not st_["vrel"]:
                    return
                b_steps.pop(0)
                fn()
                if kind == "item":
                    return
        for tt in range(ntile):
            self.need(blks[min(tt // 4, len(blks) - 1)])
            pv = self.rotate("pin", self.PSIN)
            runs = []
            q = 0
            while q < 4:
                L = 1
                while q + L < 4 and sv[q + L] == sv[q + L - 1] + 1:
                    L += 1
                runs.append((q, L))
                q += L
            for (q0, L) in runs:
                outv = pv((q0 * 128, (q0 + L) * 128)).r("p (s f) -> p s f", s=L)
                self.MM(outv, [(self.XN(k, (tt * 128, (tt + 1) * 128)),
                                self.RING((sv[q0], sv[q0] + L), (k * 128, (k + 1) * 128))) for k in range(KC)])
            gv = self.rotate("ta", self.TA)()
            self.ACT(AF.Gelu_apprx_tanh, gv, pv())
            st6 = self.rotate("st6", self.STT6)
            mv = self.rotate("mv", self.MV)
            self.op("dve", lambda e, st6=st6, gv=gv: e.bn_stats(out=st6((0, 6)).ap, in_=gv.ap), reads=[gv], writes=[st6()])
            self.op("dve", lambda e, st6=st6, mv=mv: e.bn_aggr(out=mv((0, 2)).ap, in_=st6((0, 6)).ap), reads=[st6()], writes=[mv()])
            self.TS("dve", mv((2, 3)), mv((1, 2)), EPS, ALU.add)
            self.TT("pool", mv((3, 4)), mv((2, 3)), self.NEGH((0, 1)), ALU.pow)
            is_s = (ps == 1 and tt == ntile - 1)
            if is_s:
                vn = self.rotate("td", self.TD)()
                self.TS("dve", vn, gv, mv((0, 1)), ALU.subtract, mv((3, 4)), ALU.mult)
                vf = self.rotate("td", self.TD)()
                self.TT("dve", vf, vn, self.LNG(), ALU.mult)
                self.dma("sp", self.o_av[i], vf)
                self.ACT(AF.Copy, vacc(tt, 0, 512), vn)
            else:
                self.TS("dve", vacc(tt, 0, 512), gv, mv((0, 1)), ALU.subtract, mv((3, 4)), ALU.mult)
            b_advance()
        self.wrel(4)
        st_["vrel"] = True
        if st_["defer"]:
            self.wrel(st_["defer"])
            st_["defer"] = 0
        self.lookahead = 0
        while b_steps:
            b_steps.pop(0)[2]()
        for h in range(4):
            su = self.wget(self.w_in(wd, h * 128))
            for b in blks:
                pu = self.pst(self.PSIN, "pin", b)
                self.MM(pu, [(self.wk(su, k), self.xcols(self.XN, k, b)) for k in range(KC)])
                u = self.tile(self.TA, "ta", b)
                self.ACT(AF.Gelu_apprx_tanh, u, pu)
                pmb = self.rotate("aux", self.PSAUX)
                kd = 1 if b.kind == "s" else 0
                items = []
                for q in range(b.n // 128):
                    tt = b.c0 // 128 + q
                    items.append((pmb((q * 128, (q + 1) * 128)), vacc(tt, h * 128, (h + 1) * 128), self.WST(i, h, kd)))
                self.MMS(items)
                pm = self.bv(pmb((0, b.n)), b)
                tmp = self.tile(self.TD, "td", b)
                if b.kind == "p":
                    b0 = self.BSB(h, (0, 128))
                    bs = Acc(self.BSB, b0.ap.rearrange("p (o t) -> p o t", o=1).broadcast_to([128, 4, 128]), b0.lo, b0.hi)
                    self.STT("dve", tmp.r("p (q t) -> p q t", q=4), pm.r("p (q t) -> p q t", q=4), self.vec(V_ALG + i * 4 + h), bs, ALU.mult, ALU.add)
                else:
                    bs = self.BSB(h, (128, 256)).s3()
                    self.STT("dve", tmp, pm, self.vec(V_ALG + i * 4 + h), bs, ALU.mult, ALU.add)
                self.TT("dve", self.xcols(self.YH, h, b), tmp, u, ALU.mult)
            self.wrel(1)
        if ps == 1:
            self.dma("sp", self.o_bp[i], self.CB(i).r("p c r -> p (c r)"))
            self.dma("sp", self.o_bs[i], self.SO((0, 128)))
        self.out_proj(self.w_out_even[i], lambda b: self.norm_block(V_GFFN + 8 * l, b))

    def odd_mixer(self, l):
        i = l // 2
        ps = self.ps
        blks = self.blks
        wd = self.w_in_odd[i]
        if ps == 1:
            self.dma("pool", self.FSI(), self.st_f[l].rearrange("p (j f) -> p j f", j=NJ))
        self.wprefetch()
        def d_in(g):
            sp_ = self.wget(self.w_in(wd, 1024 + g * 128))
            P = self.R[0]
            SP = self.SR[0]
            self.ACT(AF.Copy, P((HO - 15, HO)), self.CD(i, g))
            if ps == 1:
                self.dma("pool", self.SI((0, 240)), self.st_d[i, :, g])
                self.ACT(AF.Copy, SP(None, (SH - 15, SH)), self.SI((0, 240)).r("p (s r) -> p s r", s=16))
            for b in blks:
                pp = self.pst(self.PSIN, "pin", b)
                self.MM(pp, [(self.wk(sp_, k), self.xcols(self.XN, k, b)) for k in range(KC)])
                self.ACT(AF.Copy, self.row(P, SP, b, 0), pp)

        def d_rest(g):
            win = 2 << g
            P = self.R[0]
            Q1 = self.R[1]
            Q2 = self.R[2]
            SP = self.SR[0]
            SQ1 = self.SR[1]
            lo = HO - 15
            hi = HO + NPP
            src = P
            dsts = [Q1, Q2]
            sh = 1
            for m in range(g + 1):
                lo += sh
                dst = dsts[m % 2]
                self.TT("dve", dst((lo, hi)), src((lo, hi)), src((lo - sh, hi - sh)), ALU.add)
                src = dst
                sh *= 2
            RBr = self.rotate("rb", self.RB)
            SRBr = None
            self.STT("dve", RBr((HO, hi)), src((HO, hi)), 1.0 / win, P((HO, hi)), ALU.mult, ALU.subtract)
            if ps == 0:
                tmp = self.rotate("td", self.TD)((0, 16))
                self.TT("dve", tmp, src((HO, HO + 16)), self.INVC(g), ALU.mult)
                self.TT("dve", RBr((HO, HO + 16)), tmp, P((HO, HO + 16)), ALU.subtract)
            else:
                SRBr = self.rotate("srb", self.SRB)
                lo = SH - 15
                ssrc = SP
                sdsts = [SQ1, self.SR[2]]
                sh = 1
                for m in range(g + 1):
                    lo += sh
                    dst = sdsts[m % 2]
                    self.TT("dve", dst(None, (lo, SH + 8)), ssrc(None, (lo, SH + 8)), ssrc(None, (lo - sh, SH + 8 - sh)), ALU.add)
                    ssrc = dst
                    sh *= 2
                self.STT("dve", SRBr(None, (SH, SH + 8)), ssrc(None, (SH, SH + 8)), 1.0 / win, SP(None, (SH, SH + 8)), ALU.mult, ALU.subtract)
            for b in blks:
                pd = self.pst(self.PSAUX, "aux", b)
                rhs = RBr((HO + b.c0, HO + b.c0 + b.n)) if b.kind == "p" else SRBr(None, (SH, SH + 8))
                self.MM(pd, [(self.DPROJ(i, g), rhs)])
                self.ACT(AF.Identity, self.xcols(self.YH, 4 + g, b), pd, scale=self.vec(V_DSC + i * 4 + g))
            self.ACT(AF.Copy, self.CD(i, g), P((HO + NPP - 15, HO + NPP)))
            if ps == 1:
                self.ACT(AF.Copy, self.SO((0, 240)).r("p (s r) -> p s r", s=16), SP(None, (SH + 8 - 15, SH + 8)))
                self.dma("sp", self.o_ds[i, :, g], self.SO((0, 240)))
                if g == 3:
                    self.dma("sp", self.o_dp[i], self.CD(i).r("p c r -> p (c r)"))

        for c in range(4):
            sa = self.wget(self.w_in(wd, c * 128))
            sb_ = self.wget(self.w_in(wd, 512 + c * 128))
            DG = self.rotate("diag", self.DIAG)
            c0_ = V_CW + i * 124 + c
            cw = Acc(self.VEC, self.VEC.t[:, c0_:c0_ + 121:4].rearrange("p (k o) -> p k o", o=1).broadcast_to([128, 31, 128]), c0_, c0_ + 121)
            idb = Acc(self.IDENT, self.IDENT().ap.rearrange("p (o q) -> p o q", o=1).broadcast_to([128, 31, 128]), 0, 128)
            self.TT("dve", DG(), idb, cw, ALU.mult)
            RBr = self.rotate("rb", self.RB)
            SRf = self.rotate("sr", self.SR)
            SRBr = self.rotate("srb", self.SRB)
            self.ACT(AF.Copy, RBr((HO - 30, HO)), self.CC2(i, c))
            if ps == 1:
                self.dma("pool", self.SI((0, 480)), self.st_c[i, :, c])
                si3 = self.SI((0, 480)).r("p (s r) -> p s r", s=16)
                self.ACT(AF.Copy, SRf(None, (0, 30)), si3)
                self.ACT(AF.Identity, SRBr(None, (0, 30)), si3, scale=2.0)
            pend = None
            for bi, b in enumerate(blks):
                self.need(b)
                pa = self.pst(self.PSIN, "pin", b)
                self.MM(pa, [(self.wk(sa, k), self.xcols(self.XN, k, b)) for k in range(KC)])
                pb = self.pst(self.PSIN, "pin", b)
                self.MM(pb, [(self.wk(sb_, k), self.xcols(self.XN, k, b)) for k in range(KC)])
                th = self.tile(self.TA, "ta", b)
                self.ACT(AF.Tanh, th, pb, scale=0.5)
                self.STT("dve", self.row(RBr, SRBr, b, 0), th, 1.0, pa, ALU.add, ALU.mult)
                if b.kind == "s":
                    t2 = self.tile(self.TD, "td", b)
                    self.STT("dve", t2, th, 1.0, pa, ALU.add, ALU.mult)
                    self.ACT(AF.Identity, SRf(None, (SH, SH + 8)), t2, scale=0.5)
                elif ps == 1 and bi == 1:
                    t2 = self.rotate("td", self.TD)((0, 30))
                    self.STT("dve", t2, th_slice(th, 482, 512), 1.0, ps_slice(pa, 482, 512), ALU.add, ALU.mult)
                    self.ACT(AF.Identity, self.GT(i, c), t2, scale=0.5)
                if pend is not None:
                    pend()

                def conv(b=b, RBr=RBr, SRBr=SRBr, DG=DG, c=c):
                    pcv = self.pst(self.PSAUX, "aux", b)
                    pairs = []
                    for k in range(31):
                        if b.kind == "p":
                            rhs = RBr((HO + b.c0 - 30 + k, HO + b.c0 - 30 + k + b.n))
                        else:
                            rhs = SRBr(None, (k, k + 8))
                        pairs.append((DG(k), rhs))
                    self.MM(pcv, pairs)
                    self.ACT(AF.Identity, self.xcols(self.CC, c, b), pcv, scale=0.5, bias=self.vec(V_CB + i * 4 + c))
                pend = conv
            self.ACT(AF.Copy, self.CC2(i, c), RBr((HO + NPP - 30, HO + NPP)))
            if ps == 1:
                self.ACT(AF.Copy, self.SO((0, 480)).r("p (s r) -> p s r", s=16), SRf(None, (8, 38)))
                self.dma("sp", self.o_cs[i, :, c], self.SO((0, 480)))
                if c == 3:
                    self.dma("sp", self.o_cp[i], self.GT(i).r("p c r -> p (c r)"))
            d_in(c)
            pend()
            d_rest(c)
            self.wrel(3)
        ln_state = []
        for b in blks:
            n = b.n
            ntt = n // 128
            pbank = self.rotate("aux", self.PSAUX)
            pst_ = pbank((0, 8))
            for c in range(4):
                ccb = self.rotate("sq", self.SQ)((0, n))
                self.op("dve", lambda e, ccb=ccb, c=c, b=b: e.tensor_copy(out=ccb.ap, in_=self.CC(c, b.cols).ap), reads=[self.CC(c, b.cols)], writes=[ccb])
                sqb = self.rotate("sq2", self.SQ2)((0, n))
                self.ACT(AF.Square, sqb, self.CC(c, b.cols))

                def fn(e, c=c, ccb=ccb, sqb=sqb, pbank=pbank, ntt=ntt):
                    ins = None
                    for tt in range(ntt):
                        ins = e.matmul(pbank((tt, tt + 1)).ap, lhsT=ccb.ap[:, tt * 128:(tt + 1) * 128], rhs=self.ONES((0, 1)).ap,
                                       start=(c == 0 and tt == 0), stop=(c == 3), skip_group_check=True)
                    for tt in range(ntt):
                        ins = e.matmul(pbank((4 + tt, 5 + tt)).ap, lhsT=sqb.ap[:, tt * 128:(tt + 1) * 128], rhs=self.ONES((0, 1)).ap,
                                       start=False, stop=(c == 3), skip_group_check=True)
                    return ins
                self.op("pe", fn, reads=[ccb, sqb, self.ONES()], writes=[pst_])
            m4 = self.rotate("r4", self.R4)((0, ntt))
            self.TS("dve", m4, pbank((0, ntt)), 1.0 / 512, ALU.mult)
            v4 = self.rotate("r4", self.R4)((0, ntt))
            self.TT("dve", v4, m4, m4, ALU.mult)
            self.TS("dve", v4, v4, EPS, ALU.subtract)
            self.STT("dve", v4, pbank((4, 4 + ntt)), 1.0 / 512, v4, ALU.mult, ALU.subtract)
            self.TT("pool", v4, v4, self.NEGH((0, ntt)), ALU.pow)
            mr4 = self.rotate("r4", self.R4)((0, ntt))
            self.TT("dve", mr4, m4, v4, ALU.mult)
            ln_state.append((b, n, ntt, v4, mr4))
        for (b, n, ntt, v4, mr4) in ln_state:
            prR = self.bcast_rows(v4, ntt, n)
            prM = self.bcast_rows(mr4, ntt, n)
            pend = None
            for c in range(4):
                tn = self.rotate("td", self.TD)((0, n))
                self.TT("dve", tn, self.CC(c, b.cols), prR, ALU.mult)
                self.TT("dve", tn, tn, prM, ALU.subtract)
                zh = self.rotate("ta", self.TA)((0, n))
                self.ACT(AF.Identity, zh, tn, scale=self.vec(V_CLG + i * 4 + c), bias=self.vec(V_CLB + i * 4 + c))
                th = self.rotate("ta", self.TA)((0, n))
                self.ACT(AF.Tanh, th, tn, scale=self.vec(V_CLG + i * 4 + c), bias=self.vec(V_CLB + i * 4 + c))
                if pend is not None:
                    pend()
                pend = (lambda c=c, th=th, zh=zh, b=b: self.STT("dve", self.YH(c, b.cols), th, 1.0, zh, ALU.add, ALU.mult))
            pend()
        self.out_proj(self.w_out_odd[i], lambda b: self.norm_block(V_GFFN + 8 * l, b))

    def ffn(self, l):
        ps = self.ps
        blks = self.blks
        wd = self.w_ffn_in[l]
        wo = self.w_ffn_out[l]
        parts = [list(range(0, 4)), list(range(4, 8)), list(range(8, 12)), list(range(12, 16)),
                 list(range(16, 19)), list(range(19, 22))]
        self.wprefetch()

        def ffn_in(pi):
            part = parts[pi]
            pend = None
            for jj, j in enumerate(part):
                hs = (pi % 2) * 4 + jj
                sg = self.wget(self.w_in(wd, j * 128))
                su = self.wget(self.w_in(wd, DFF + j * 128))
                R = self.rotate("r", self.R)
                SRb = self.rotate("sr", self.SR)
                self.ACT(AF.Copy, R((HO - 2, HO)), self.CF(l, j))
                if ps == 1:
                    self.ACT(AF.Copy, SRb(None, (SH - 2, SH)), self.FSI(j).r("p (s r) -> p s r", s=16))
                w0 = self.vec(V_FW + (l * 3 + 0) * NJ + j)
                w1 = self.vec(V_FW + (l * 3 + 1) * NJ + j)
                w2 = self.vec(V_FW + (l * 3 + 2) * NJ + j)
                for b in blks:
                    self.need(b)
                    pg = self.pst(self.PSIN, "pin", b)
                    self.MM(pg, [(self.wk(sg, k), self.xcols(self.XN, k, b)) for k in range(KC)])
                    pu = self.pst(self.PSIN, "pin", b)
                    self.MM(pu, [(self.wk(su, k), self.xcols(self.XN, k, b)) for k in range(KC)])
                    self.ACT(AF.Copy, self.row(R, SRb, b, 0), pg)
                    t0 = self.tile(self.TA, "ta", b)
                    self.ACT(AF.Identity, t0, pg, scale=w2)
                    t1 = self.tile(self.TD, "td", b)
                    self.STT("dve", t1, self.row(R, SRb, b, 1), w1, t0, ALU.mult, ALU.add)
                    self.STT("dve", t1, self.row(R, SRb, b, 2), w0, t1, ALU.mult, ALU.add)
                    if pend is not None:
                        pend()

                    def fin(t1=t1, pu=pu, b=b, hs=hs):
                        gl = self.tile(self.TA, "ta", b)
                        self.ACT(AF.Gelu_apprx_tanh, gl, t1)
                        self.TT("dve", self.xcols(self.YH, hs, b), gl, pu, ALU.mult)
                    pend = fin
                self.ACT(AF.Copy, self.CF(l, j), R((HO + NPP - 2, HO + NPP)))
                if ps == 1:
                    self.ACT(AF.Copy, self.FSO(j).r("p (s r) -> p s r", s=16), SRb(None, (SH + 6, SH + 8)))
                self.wrel(2)
            if pend is not None:
                pend()

        def ffn_out(pi, last):
            part = parts[pi]
            slots = [self.wget(("out", wo[j * 128:(j + 1) * 128, :])) for j in part]
            prev = None
            for b in blks:
                for m in range(KC):
                    po = self.pst(self.PSAUX, "aux", b)
                    self.MM(po, [(self.RING(slots[jj], (m * 128, (m + 1) * 128)), self.xcols(self.YH, (pi % 2) * 4 + jj, b))
                                 for jj in range(len(part))])
                    xm = self.xcols(self.X, m, b)
                    self.TT("dve", xm, xm, po, ALU.add)
                if last and prev is not None:
                    self.next_norm(l, prev)
                prev = b
            if last:
                if l == self.depth - 1 or (l + 1) % 2 == 1:
                    self.next_norm(l, prev)
                else:
                    self.pending[prev.index] = (lambda prev=prev: self.next_norm(l, prev))
            self.wrel(len(part))

        ffn_in(0)
        for pi in range(len(parts)):
            if pi + 1 < len(parts):
                ffn_in(pi + 1)
            else:
                if ps == 1:
                    self.dma("sp", self.o_fp[l], self.CF(l).r("p c r -> p (c r)"))
                    self.dma("sp", self.o_fs[l], self.FSO().r("p j f -> p (j f)"))
            ffn_out(pi, pi == len(parts) - 1)


def th_slice(acc, a, b):
    return Acc(acc.buf, acc.ap[:, a:b], acc.lo + a, acc.lo + b)


def ps_slice(acc, a, b):
    return Acc(acc.buf, acc.ap[:, a:b], acc.lo + a, acc.lo + b)


def _consts():
    s = np.arange(128)
    maskp = (s[:, None] <= s[None, :]).astype(np.float32)
    masks = ((s[:, None] // 8 == s[None, :] // 8) & (s[:, None] % 8 <= s[None, :] % 8)).astype(np.float32)
    ident = np.eye(128, dtype=np.float32)
    invc = np.zeros((4, 16), np.float32)
    for g, win in enumerate((2, 4, 8, 16)):
        invc[g] = 1.0 / np.minimum(np.arange(16) + 1, win)
    return np.stack([maskp, masks]), ident, invc


def _pack_vecs(inp):
    def pk(a):
        a = np.asarray(a, np.float32)
        lead = a.shape[:-1]
        n = a.shape[-1] // 128
        a = a.reshape(lead + (n, 128))
        a = np.moveaxis(a, -1, 0)
        return a.reshape(128, -1)
    cols = [pk(inp["norm_mix_g"]), pk(inp["norm_ffn_g"]), pk(inp["norm_final_g"]),
            pk(inp["b_conv_w"]), pk(inp["c_conv_w"]), pk(inp["c_conv_b"]), pk(inp["c_ln_g"]),
            pk(inp["c_ln_b"]), pk(inp["d_scale"]), pk(inp["ffn_conv_w"]), pk(inp["a_ln_g"])]
    v = np.concatenate(cols, axis=1)
    assert v.shape == (128, NV), v.shape
    return np.ascontiguousarray(v)


_CACHE = {}


def kernel(**inp):
    inp = {k: np.asarray(v) for k, v in inp.items()}
    key = "nc"
    if key not in _CACHE:
        _CACHE[key] = Builder().build()
    nc = _CACHE[key]
    masks, ident, invc = _consts()
    vecs = _pack_vecs(inp)
    ws = inp["a_ws"].astype(np.float32)
    wst = np.zeros((2, 4, 2, 128, 128), np.float32)
    wst[:, :, 0] = np.swapaxes(ws, -1, -2)
    wst[:, :, 1] = np.tile(np.swapaxes(ws[:, :, :8, :8], -1, -2), (1, 1, 16, 16))
    bs = inp["a_bs"].astype(np.float32)
    bsb = np.concatenate([bs, np.tile(bs[:, :, :8], (1, 1, 16))], axis=-1)
    shared = dict(
        w_in_even=inp["w_in_even"], w_out_even=inp["w_out_even"], w_in_odd=inp["w_in_odd"],
        w_out_odd=inp["w_out_odd"], w_ffn_in=inp["w_ffn_in"], w_ffn_out=inp["w_ffn_out"],
        d_proj=inp["d_proj"], vecs=vecs, lng=inp["a_ln_g"], bsb=np.ascontiguousarray(bsb), wst=wst,
        masks=masks, ident=ident, invc=invc)
    shared = {k: np.ascontiguousarray(v, dtype=np.float32) for k, v in shared.items()}
    in_maps = []
    for c in range(NCORES):
        sl = slice(16 * c, 16 * c + 16)
        xs = inp["x_sample"][sl].reshape(128, D)
        xT = np.ascontiguousarray(np.concatenate([inp["x_prompt"][c], xs], axis=0).T, dtype=np.float32)
        m = dict(shared)
        m["xT"] = xT
        def st(a):
            n, _, r, ch = a.shape
            a = a.reshape(n, 16, r, ch // 128, 128)
            return np.ascontiguousarray(np.transpose(a, (0, 4, 3, 1, 2)), dtype=np.float32)
        m["st_b"] = st(inp["state_b_conv"][:, sl]).reshape(2, 128, 128)
        m["st_c"] = st(inp["state_c_conv"][:, sl]).reshape(2, 128, 4, 480)
        m["st_d"] = st(inp["state_d_pool"][:, sl]).reshape(2, 128, 4, 240)
        m["st_f"] = st(inp["state_ffn_conv"][:, sl]).reshape(4, 128, NJ * 32)
        in_maps.append(m)
    res = run_bass_kernel_spmd(nc, in_maps, core_ids=list(range(NCORES)))
    R = res.results
    y_prompt = np.stack([R[c]["yT"][:, :2048].T for c in range(NCORES)])
    y_sample = np.concatenate([R[c]["yT"][:, 2048:].T.reshape(16, 8, D) for c in range(NCORES)])
    a_s = np.concatenate([R[c]["o_av"].reshape(2, 16, 8, 512) for c in range(NCORES)], axis=1)

    def pr(name, C, r):
        outs_ = []
        for c in range(NCORES):
            a = R[c][name]
            n = a.shape[0]
            a = a.reshape(n, 128, C, r)
            outs_.append(np.transpose(a, (0, 3, 2, 1)).reshape(n, r, C * 128))
        return np.stack(outs_, axis=1)

    def sm(name, C, r):
        outs_ = []
        for c in range(NCORES):
            a = R[c][name]
            n = a.shape[0]
            a = a.reshape(n, 128, C, 16, r)
            outs_.append(np.transpose(a, (0, 3, 4, 2, 1)).reshape(n, 16, r, C * 128))
        return np.concatenate(outs_, axis=1)
    outs = (y_prompt, y_sample, a_s, pr("o_bp", 4, 2), sm("o_bs", 4, 2), pr("o_cp", 4, 30), sm("o_cs", 4, 30),
            pr("o_dp", 4, 15), sm("o_ds", 4, 15), pr("o_fp", NJ, 2), sm("o_fs", NJ, 2))
    return tuple(np.ascontiguousarray(o, dtype=np.float32) for o in outs)
```

```python
import numpy as np
from contextlib import ExitStack
import concourse.bass as bass
import concourse.mybir as mybir
from concourse.bass_utils import run_bass_kernel_spmd

F32 = mybir.dt.float32
BF16 = mybir.dt.bfloat16
AF = mybir.ActivationFunctionType
ALU = mybir.AluOpType

D = 1024
KC = 8
DEPTH = 4
DFF = 2816
NJ = 22
NPP = 1024
NTM = 1152
HO = 32
SH = 30
SW = 40
EPS = 1e-6
NSLOT = 14
NCORES = 8

V_GMIX = 0
V_GFFN = 32
V_GFIN = 64
V_BW = 72
V_CW = 96
V_CB = 344
V_CLG = 352
V_CLB = 360
V_DSC = 368
V_FW = 376
V_ALG = 640
NV = 648


class Buf:
    _n = 0

    def __init__(self, t, free_shape, cell, name):
        self.t = t
        self.shape = list(free_shape)
        self.cell = cell
        self.strides = []
        s = 1
        for d in reversed(self.shape):
            self.strides.insert(0, s)
            s *= d
        self.size = s
        self.id = Buf._n
        Buf._n += 1
        self.name = name
        if len(self.shape) == 1:
            self._flat = None
        else:
            names = " ".join(f"d{i}" for i in range(len(self.shape)))
            self._flat = f"p {names} -> p ({names})"

    def __call__(self, *idx, p=None):
        idx = list(idx) + [None] * (len(self.shape) - len(idx))
        sl = []
        lo = 0
        hi = 0
        for i, n, st in zip(idx, self.shape, self.strides):
            if i is None:
                a, b = 0, n
                sl.append(slice(None))
            elif isinstance(i, tuple):
                a, b = i
                assert 0 <= a < b <= n, (self.name, idx, self.shape)
                sl.append(slice(a, b))
            else:
                a, b = i, i + 1
                assert 0 <= a < n, (self.name, idx, self.shape)
                sl.append(i)
            lo += a * st
            hi += (b - 1) * st
        hi += 1
        ps = slice(None) if p is None else slice(p[0], p[1])
        return Acc(self, self.t[tuple([ps] + sl)], lo, hi)

    def flat(self, lo, hi, p=None):
        ps = slice(None) if p is None else slice(p[0], p[1])
        full = self.t[tuple([slice(None)] * (len(self.shape) + 1))]
        if self._flat is not None:
            full = full.rearrange(self._flat)
        return Acc(self, full[ps, lo:hi], lo, hi)


class Acc:
    def __init__(self, buf, ap, lo, hi):
        self.buf = buf
        self.ap = ap
        self.lo = lo
        self.hi = hi

    def cells(self):
        c = self.buf.cell
        return range(self.lo // c, (self.hi - 1) // c + 1)

    def r(self, pattern, **kw):
        return Acc(self.buf, self.ap.rearrange(pattern, **kw), self.lo, self.hi)

    def s3(self):
        return self.r("p (s t) -> p s t", s=16)


class Op:
    __slots__ = ("eng", "fn", "kind", "chan", "chan_idx", "seq", "waits", "signal", "sigcount", "id")


class Sched:
    ENGS = ("pe", "act", "dve", "pool", "sp")

    def __init__(self):
        self.ops = []
        self.state = {}
        self.eng_ops = {e: [] for e in self.ENGS}
        self.waited = {e: {} for e in self.ENGS}
        self.waited_dma = {e: {} for e in self.ENGS}
        self.chan_count = {}

    def _add_dep(self, op, d):
        if d is None or d is op:
            return
        E = op.eng
        if d.kind == "dma":
            if self.waited_dma[E].get(d.chan, 0) >= d.chan_idx:
                return
            cur = op.waits.get(("dma", d.chan))
            if cur is None or cur.chan_idx < d.chan_idx:
                op.waits[("dma", d.chan)] = d
        else:
            if self.waited[E].get(d.eng, -1) >= d.seq:
                return
            cur = op.waits.get(("eng", d.eng))
            if cur is None or cur.seq < d.seq:
                op.waits[("eng", d.eng)] = d

    def op(self, eng, fn, reads=(), writes=(), dma=False, chan=None):
        o = Op()
        o.eng = eng
        o.fn = fn
        o.kind = "dma" if dma else "cmp"
        o.chan = chan
        o.waits = {}
        o.signal = False
        o.id = len(self.ops)
        o.seq = len(self.eng_ops[eng])
        if dma:
            n = self.chan_count.get(chan, 0) + 1
            self.chan_count[chan] = n
            o.chan_idx = n
            o.signal = True
        for a in reads:
            for c in a.cells():
                st = self.state.get((a.buf.id, c))
                if st is not None and st[0] is not None:
                    self._add_dep(o, st[0])
        for a in writes:
            for c in a.cells():
                st = self.state.get((a.buf.id, c))
                if st is None:
                    continue
                d = st[0]
                if d is not None and (d.kind == "dma" or dma or d.eng != eng):
                    self._add_dep(o, d)
                for r in st[1].values():
                    if r.kind == "dma" or dma or r.eng != eng:
                        self._add_dep(o, r)
        if dma and o.chan_idx > 1:
            cur = o.waits.get(("dma", chan))
            if self.waited_dma[eng].get(chan, 0) < o.chan_idx - 1 and (cur is None or cur.chan_idx < o.chan_idx - 1):
                prev = Op()
                prev.kind = "dma"
                prev.chan = chan
                prev.chan_idx = o.chan_idx - 1
                o.waits[("dma", chan)] = prev
        for key, d in o.waits.items():
            if key[0] == "dma":
                self.waited_dma[eng][d.chan] = max(self.waited_dma[eng].get(d.chan, 0), d.chan_idx)
            else:
                d.signal = True
                self.waited[eng][d.eng] = max(self.waited[eng].get(d.eng, -1), d.seq)
        for a in writes:
            for c in a.cells():
                self.state[(a.buf.id, c)] = [o, {}]
        for a in reads:
            for c in a.cells():
                st = self.state.get((a.buf.id, c))
                if st is None:
                    st = [None, {}]
                    self.state[(a.buf.id, c)] = st
                st[1][eng if not dma else ("dma", o.id)] = o
        self.ops.append(o)
        self.eng_ops[eng].append(o)
        return o

    def emit(self, nc, sems, chan_sems):
        for e, lst in self.eng_ops.items():
            n = 0
            for o in lst:
                if o.kind == "cmp" and o.signal:
                    n += 1
                o.sigcount = n
        finals = list(self.chan_count.items())

        class _First:
            def __init__(self, e):
                self._e = e
                self.first = None

            def __getattr__(self, name):
                attr = getattr(self._e, name)
                if not callable(attr):
                    return attr

                def w(*a, **k):
                    r = attr(*a, **k)
                    if self.first is None and r is not None:
                        self.first = r
                    return r
                return w

        def run(engobj, ename):
            proxy = _First(engobj)
            for o in self.eng_ops[ename]:
                waits = list(o.waits.items())
                fuse = None
                if o.kind == "cmp" and waits:
                    fuse = waits.pop()
                for key, d in waits:
                    if key[0] == "dma":
                        engobj.wait_ge(chan_sems[d.chan], 16 * d.chan_idx)
                    else:
                        engobj.wait_ge(sems[d.eng], d.sigcount)
                proxy.first = None
                ins = o.fn(proxy)
                if fuse is not None:
                    key, d = fuse
                    if key[0] == "dma":
                        proxy.first._wait_ge(chan_sems[d.chan], 16 * d.chan_idx)
                    else:
                        proxy.first._wait_ge(sems[d.eng], d.sigcount)
                if o.kind == "dma":
                    ins.then_inc(chan_sems[o.chan], 16)
                elif o.signal:
                    ins.then_inc(sems[o.eng], 1)
            if ename == "sp":
                for c, n in finals:
                    engobj.wait_ge(chan_sems[c], 16 * n)

        with nc.Block() as block:
            @block.tensor
            def _(t):
                run(t, "pe")

            @block.scalar
            def _(a):
                run(a, "act")

            @block.vector
            def _(v):
                run(v, "dve")

            @block.gpsimd
            def _(g):
                run(g, "pool")

            @block.sync
            def _(s):
                run(s, "sp")


class Blk:
    def __init__(self, c0, n, kind):
        self.c0 = c0
        self.n = n
        self.kind = kind
        self.cols = (c0, c0 + n)


class Builder:
    def __init__(self, depth=DEPTH, dbg=None):
        self.depth = depth
        self.dbg = dbg
        self.nc = bass.Bass("TRN2", target_bir_lowering=False)
        self.S = Sched()
        self.es = ExitStack()
        self.dry = False
        self.nchan = 0
        self.gen_chans = {}
        self.rot = {}

    def dram(self, name, shape, kind="ExternalInput", dt=F32):
        return self.nc.dram_tensor(name, list(shape), dt, kind=kind).ap()

    def sb(self, name, shape, dt=F32, cell=128):
        t = self.es.enter_context(self.nc.sbuf_tensor(name, [128] + list(shape), dt))
        return Buf(t, shape, cell, name)

    def psb(self, name):
        t = self.es.enter_context(self.nc.psum_tensor(name, [128, 512], F32))
        return Buf(t, [512], 512, name)

    def rotate(self, key, lst):
        i = self.rot.get(key, 0)
        self.rot[key] = i + 1
        return lst[i % len(lst)]

    def op(self, eng, fn, reads=(), writes=(), **kw):
        if self.dry:
            return None
        return self.S.op(eng, fn, reads=reads, writes=writes, **kw)

    def dma(self, eng, out, in_, reads=(), writes=(), chan=None):
        if self.dry:
            return
        if chan is None:
            chan = self.rotate("genchan_" + eng, self.gen_chans[eng])
        o_ap = out.ap if isinstance(out, Acc) else out
        i_ap = in_.ap if isinstance(in_, Acc) else in_
        rd = list(reads) + ([in_] if isinstance(in_, Acc) else [])
        wr = list(writes) + ([out] if isinstance(out, Acc) else [])
        self.S.op(eng, lambda e: e.dma_start(out=o_ap, in_=i_ap), reads=rd, writes=wr, dma=True, chan=chan)

    def ACT(self, func, out, in_, scale=None, bias=None):
        kw = {}
        rd = [in_]
        if scale is not None:
            if isinstance(scale, Acc):
                kw["scale"] = scale.ap
                rd.append(scale)
            else:
                kw["scale"] = float(scale)
        if bias is not None:
            if isinstance(bias, Acc):
                kw["bias"] = bias.ap
                rd.append(bias)
            else:
                kw["bias"] = float(bias)
        self.op("act", lambda e: e.activation(out=out.ap, in_=in_.ap, func=func, **kw), reads=rd, writes=[out])

    def TT(self, eng, out, a, b, op):
        self.op(eng, lambda e: e.tensor_tensor(out=out.ap, in0=a.ap, in1=b.ap, op=op), reads=[a, b], writes=[out])

    def TS(self, eng, out, a, s1, op0, s2=None, op1=None):
        rd = [a]
        v1 = s1
        v2 = s2
        if isinstance(s1, Acc):
            rd.append(s1)
            v1 = s1.ap
        if isinstance(s2, Acc):
            rd.append(s2)
            v2 = s2.ap
        if op1 is None:
            self.op(eng, lambda e: e.tensor_scalar(out=out.ap, in0=a.ap, scalar1=v1, scalar2=None, op0=op0), reads=rd, writes=[out])
        else:
            self.op(eng, lambda e: e.tensor_scalar(out=out.ap, in0=a.ap, scalar1=v1, scalar2=v2, op0=op0, op1=op1), reads=rd, writes=[out])

    def STT(self, eng, out, a, s, b, op0, op1):
        rd = [a, b]
        v = s
        if isinstance(s, Acc):
            rd.append(s)
            v = s.ap
        self.op(eng, lambda e: e.scalar_tensor_tensor(out=out.ap, in0=a.ap, scalar=v, in1=b.ap, op0=op0, op1=op1), reads=rd, writes=[out])

    def MM(self, out, pairs, extra_reads=()):
        rd = []
        for l, r in pairs:
            rd.append(l)
            rd.append(r)
        rd += list(extra_reads)
        n = len(pairs)

        def fn(e):
            ins = None
            for i, (l, r) in enumerate(pairs):
                ins = e.matmul(out.ap, lhsT=l.ap, rhs=r.ap, start=(i == 0), stop=(i == n - 1))
            return ins
        self.op("pe", fn, reads=rd, writes=[out])

    def MMS(self, items):
        rd = []
        wr = []
        for o, l, r in items:
            rd += [l, r]
            wr.append(o)

        def fn(e):
            ins = None
            for o, l, r in items:
                ins = e.matmul(o.ap, lhsT=l.ap, rhs=r.ap, start=True, stop=True)
            return ins
        self.op("pe", fn, reads=rd, writes=wr)

    def wreset(self):
        self.w_idx = 0
        self.w_issued = 0
        self.w_released = set()
        self.w_oldest = 0

    def wget(self, spec):
        i = self.w_idx
        self.w_idx += 1
        if self.dry:
            self.plan.append(spec)
            return i % NSLOT
        assert i < len(self.plan)
        while self.w_issued <= i:
            assert self.w_issued - self.w_oldest < NSLOT, "ring overflow"
            self._wissue()
        return i % NSLOT

    def _wissue(self):
        i = self.w_issued
        kind, src = self.plan[i]
        slot = i % NSLOT
        if kind == "in":
            dst = self.RING(slot).r("p (k f) -> p k f", k=KC)
        else:
            dst = self.RING(slot)
        self.dma("pool", dst, src, chan=("ring", slot))
        self.w_issued += 1

    def wrel(self, n=1):
        if self.dry:
            return
        self.w_oldest += n
        self.wprefetch()

    def wprefetch(self):
        if self.dry:
            return
        while self.w_issued < len(self.plan) and self.w_issued - self.w_oldest < NSLOT:
            self._wissue()

    def w_in(self, wd, f0):
        return ("in", wd[:, f0:f0 + 128].rearrange("(k p) f -> p k f", p=128))

    def wk(self, slot, k):
        return self.RING(slot, (k * 128, (k + 1) * 128))

    def build(self):
        nc = self.nc
        dram = self.dram
        self.xT = dram("xT", [D, 2176])
        self.yT = dram("yT", [D, 2176], "ExternalOutput")
        self.st_b = dram("st_b", [2, 128, 128])
        self.st_c = dram("st_c", [2, 128, 4, 480])
        self.st_d = dram("st_d", [2, 128, 4, 240])
        self.st_f = dram("st_f", [4, 128, NJ * 32])
        self.w_in_even = dram("w_in_even", [2, D, 2560])
        self.w_out_even = dram("w_out_even", [2, D, D])
        self.w_in_odd = dram("w_in_odd", [2, D, 1536])
        self.w_out_odd = dram("w_out_odd", [2, D, D])
        self.w_ffn_in = dram("w_ffn_in", [4, D, 2 * DFF])
        self.w_ffn_out = dram("w_ffn_out", [4, DFF, D])
        self.d_proj = dram("d_proj", [2, 4, 128, 128])
        self.vecs = dram("vecs", [128, NV])
        self.lng = dram("lng", [2, 512])
        self.bsb = dram("bsb", [2, 4, 256])
        self.wst = dram("wst", [2, 4, 2, 128, 128])
        self.masks = dram("masks", [2, 128, 128])
        self.ident = dram("ident", [128, 128])
        self.invc = dram("invc", [4, 16])
        self.o_av = dram("o_av", [2, 128, 512], "ExternalOutput")
        self.o_bp = dram("o_bp", [2, 128, 8], "ExternalOutput")
        self.o_bs = dram("o_bs", [2, 128, 128], "ExternalOutput")
        self.o_cp = dram("o_cp", [2, 128, 120], "ExternalOutput")
        self.o_cs = dram("o_cs", [2, 128, 4, 480], "ExternalOutput")
        self.o_dp = dram("o_dp", [2, 128, 60], "ExternalOutput")
        self.o_ds = dram("o_ds", [2, 128, 4, 240], "ExternalOutput")
        self.o_fp = dram("o_fp", [4, 128, NJ * 2], "ExternalOutput")
        self.o_fs = dram("o_fs", [4, 128, NJ * 32], "ExternalOutput")
        if self.dbg is not None:
            self.o_dbg = dram("o_dbg", [D, 2176], "ExternalOutput")

        sb = self.sb
        self.X = sb("X", [KC, NTM])
        self.XN = sb("XN", [KC, NTM], BF16)
        self.YH = sb("YH", [KC, NTM], BF16)
        self.R = [sb(f"R{i}", [HO + NPP]) for i in range(3)]
        self.RB = [sb(f"RB{i}", [HO + NPP], BF16) for i in range(2)]
        self.SR = [sb(f"SR{i}", [16, SW], cell=16 * SW) for i in range(3)]
        self.SRB = [sb(f"SRB{i}", [16, SW], BF16, cell=16 * SW) for i in range(2)]
        self.TA = [sb(f"TA{i}", [512], cell=512) for i in range(4)]
        self.TD = [sb(f"TD{i}", [512], cell=512) for i in range(3)]
        self.SQ = [sb(f"SQ{i}", [512], BF16, cell=512) for i in range(2)]
        self.SQ2 = [sb(f"SQ2_{i}", [512], BF16, cell=512) for i in range(2)]
        self.RING = sb("RING", [NSLOT, 1024], BF16, cell=1024)
        self.DIAG = [sb(f"DIAG{i}", [31, 128], BF16, cell=31 * 128) for i in range(2)]
        self.CC = sb("CC", [4, NTM])
        self.VEC = sb("VEC", [NV], cell=8)
        self.LNG = sb("LNG", [512], cell=512)
        self.BSB = sb("BSB", [4, 256], cell=256)
        self.WST = sb("WST", [2, 4, 2, 128], BF16, cell=128)
        self.MSK = sb("MSK", [2, 128], cell=128)
        self.IDENT = sb("IDENT", [128], cell=128)
        self.ONES = sb("ONES", [128], BF16, cell=128)
        self.NEGH = sb("NEGH", [8], cell=8)
        self.ONESF = sb("ONESF", [128], cell=128)
        self.R4 = [sb(f"R4_{i}", [8], cell=8) for i in range(8)]
        self.INVC = sb("INVC", [4, 16], cell=64)
        self.DPROJ = sb("DPROJ", [2, 4, 128], BF16, cell=128)
        self.CB = sb("CB", [2, 4, 2], cell=2)
        self.CC2 = sb("CC2", [2, 4, 30], BF16, cell=30)
        self.CD = sb("CD", [2, 4, 15], cell=15)
        self.CF = sb("CF", [4, NJ, 2], cell=2)
        self.GT = sb("GT", [2, 4, 30], cell=30)
        self.SI = sb("SI", [480], cell=480)
        self.SO = sb("SO", [480], cell=480)
        self.FSI = sb("FSI", [NJ, 32], cell=32)
        self.FSO = sb("FSO", [NJ, 32], cell=32)
        self.STT6 = [sb(f"ST6_{i}", [8], cell=8) for i in range(3)]
        self.MV = [sb(f"MV{i}", [4], cell=4) for i in range(3)]
        self.PS = [self.psb(f"PS{i}") for i in range(8)]
        self.PSIN = self.PS[0:5]
        self.PSAUX = self.PS[5:8]

        self.gen_chans = {"sp": [("g", i) for i in range(24)], "pool": [("q", i) for i in range(12)]}

        self.plan = []
        self.dry = True
        self.wreset()
        self.program()
        self.dry = False
        self.rot = {}
        self.wreset()
        self.program()

        sems = {e: self.es.enter_context(nc.semaphore("s_" + e)) for e in ("pe", "act", "dve", "pool")}
        chans = {}
        for c in list(self.S.chan_count.keys()):
            chans[c] = self.es.enter_context(nc.semaphore("c_%s_%s" % c))
        self.S.emit(nc, sems, chans)
        self.es.close()
        return nc

    def vec(self, c):
        return self.VEC((c, c + 1))

    def program(self):
        self.setup()
        for ps in range(2):
            self.run_pass(ps)

    def setup(self):
        dma = self.dma
        dma("sp", self.VEC(), self.vecs[:, :])
        dma("sp", self.MSK(), self.masks.rearrange("m p f -> p m f"))
        dma("sp", self.IDENT(), self.ident[:, :])
        dma("sp", self.INVC(), self.invc.partition_broadcast(128))
        dma("pool", self.DPROJ(), self.d_proj.rearrange("i g c d -> c i g d"))
        self.op("dve", lambda e: e.memset(self.ONES().ap, 1.0), writes=[self.ONES()])
        self.op("dve", lambda e: e.memset(self.NEGH().ap, -0.5), writes=[self.NEGH()])
        self.op("dve", lambda e: e.memset(self.ONESF().ap, 1.0), writes=[self.ONESF()])
        for b in (self.CB, self.CD, self.CF, self.CC2):
            self.op("dve", lambda e, b=b: e.memset(b().ap, 0.0), writes=[b()])
        g = self.VEC((V_GMIX, V_GFIN + 8))
        self.TS("dve", g, g, 32.0, ALU.mult)
        h = self.VEC((V_CLG, V_CLB + 8))
        self.TS("dve", h, h, 0.5, ALU.mult)
        stg = self.CC.flat(0, 2048)
        dma("sp", Acc(self.CC, stg.ap.rearrange("p (a t) -> p a t", a=16), 0, 2048), self.wst.rearrange("i h k s t -> s (i h k) t"))
        v5 = stg.ap.rearrange("p (i h k t) -> p i h k t", i=2, h=4, k=2)
        for i in range(2):
            for kd in range(2):
                src = Acc(self.CC, v5[:, i, :, kd, :], 0, 2048)
                m0 = self.MSK(kd)
                mb = Acc(self.MSK, m0.ap.rearrange("p (o t) -> p o t", o=1).broadcast_to([128, 4, 128]), m0.lo, m0.hi)
                self.TT("dve", self.WST(i, None, kd), src, mb, ALU.mult)

    def blocks(self, ps):
        b = [Blk(0, 512, "p"), Blk(512, 512, "p")]
        if ps == 1:
            b.append(Blk(1024, 128, "s"))
        return b

    def run_pass(self, ps):
        self.ps = ps
        blks = self.blocks(ps)
        self.blks = blks
        nt = blks[-1].c0 + blks[-1].n
        self.nt = nt
        for b in blks:
            for k in range(KC):
                if b.kind == "p":
                    g0 = ps * NPP + b.c0
                else:
                    g0 = 2048
                self.dma("sp" if ps == 0 else "pool", self.X(k, b.cols), self.xT[k * 128:(k + 1) * 128, g0:g0 + b.n])
        self.pending = {}
        self.lookahead = 1
        for bi, b in enumerate(blks):
            b.index = bi
            self.pending[bi] = (lambda b=b: self.norm_block(V_GMIX + 0, b))
        for l in range(self.depth):
            if l % 2 == 0:
                self.even_mixer(l)
            else:
                self.odd_mixer(l)
            self.ffn(l)
        if self.dbg is not None:
            for k in range(KC):
                self.dma("sp", self.o_dbg[k * 128:(k + 1) * 128, ps * NPP:(ps + 1) * NPP], self.X(k, (0, NPP)))
                if ps == 1:
                    self.dma("sp", self.o_dbg[k * 128:(k + 1) * 128, 2048:2176], self.X(k, (NPP, NPP + 128)))

    def bv(self, acc, b):
        return acc.s3() if b.kind == "s" else acc

    def tile(self, pool, key, b):
        t = self.rotate(key, pool)((0, b.n))
        return self.bv(t, b)

    def pst(self, lst, key, b):
        t = self.rotate(key, lst)((0, b.n))
        return self.bv(t, b)

    def xcols(self, buf, k, b):
        return self.bv(buf(k, b.cols), b)

    def row(self, R, SRb, b, shift):
        if b.kind == "p":
            return R((HO + b.c0 - shift, HO + b.c0 - shift + b.n))
        return SRb(None, (SH - shift, SH + 8 - shift))

    def bcast_rows(self, r4, ntt, n):
        rep = self.rotate("ta", self.TA)((0, n))
        for tt in range(ntt):
            self.ACT(AF.Identity, th_slice(rep, tt * 128, (tt + 1) * 128), self.ONESF(), scale=th_slice(r4, tt, tt + 1))
        pr = self.rotate("aux", self.PSAUX)((0, n))

        def fnt(e, pr=pr, rep=rep, ntt=ntt):
            ins = None
            for tt in range(ntt):
                ins = e.transpose(pr.ap[:, tt * 128:(tt + 1) * 128], rep.ap[:, tt * 128:(tt + 1) * 128], self.IDENT().ap)
            return ins
        self.op("pe", fnt, reads=[rep, self.IDENT()], writes=[pr])
        return pr

    def norm_block(self, gcol, b, final=False):
        n = b.n
        ntt = n // 128
        pbank = self.rotate("aux", self.PSAUX)
        pss = pbank((0, ntt))
        for k in range(KC):
            sq = self.rotate("sq", self.SQ)((0, n))
            self.ACT(AF.Square, sq, self.X(k, b.cols))

            def fn(e, k=k, sq=sq, pbank=pbank, ntt=ntt):
                ins = None
                for tt in range(ntt):
                    ins = e.matmul(pbank((tt, tt + 1)).ap, lhsT=sq.ap[:, tt * 128:(tt + 1) * 128], rhs=self.ONES((0, 1)).ap,
                                   start=(k == 0 and tt == 0), stop=(k == KC - 1), skip_group_check=True)
                return ins
            self.op("pe", fn, reads=[sq, self.ONES()], writes=[pss])
        r4 = self.rotate("r4", self.R4)((0, ntt))
        self.TS("dve", r4, pss, 1024.0 * EPS, ALU.add)
        self.TT("pool", r4, r4, self.NEGH((0, ntt)), ALU.pow)
        rs = self.bcast_rows(r4, ntt, n)
        for k in range(KC):
            if not final:
                self.STT("dve", self.XN(k, b.cols), self.X(k, b.cols), self.vec(gcol + k), rs, ALU.mult, ALU.mult)
            else:
                o = self.rotate("ta", self.TA)((0, n))
                self.STT("dve", o, self.X(k, b.cols), self.vec(gcol + k), rs, ALU.mult, ALU.mult)
                if b.kind == "p":
                    c = self.ps * NPP + b.c0
                else:
                    c = 2048
                self.dma("sp", self.yT[k * 128:(k + 1) * 128, c:c + n], o)

    def rotate_last(self, key, lst):
        i = self.rot.get(key, 0)
        return lst[(i - 1) % len(lst)]

    def need(self, b):
        for bi in sorted(self.pending.keys()):
            if bi <= b.index + self.lookahead:
                fn = self.pending.pop(bi)
                fn()

    def need_all(self):
        for bi in sorted(self.pending.keys()):
            self.pending.pop(bi)()

    def next_norm(self, l, b):
        if l == self.depth - 1:
            self.norm_block(V_GFIN, b, final=True)
        else:
            self.norm_block(V_GMIX + 8 * (l + 1), b)

    def out_proj(self, wd, after):
        slots = [self.wget(self.w_in(wd, m * 128)) for m in range(KC)]
        prev = None
        for b in self.blks:
            for m in range(KC):
                po = self.pst(self.PSAUX, "aux", b)
                self.MM(po, [(self.wk(slots[m], k), self.xcols(self.YH, k, b)) for k in range(KC)])
                xm = self.xcols(self.X, m, b)
                self.TT("dve", xm, xm, po, ALU.add)
            if prev is not None:
                after(prev)
            prev = b
        after(prev)
        self.wrel(KC)

    def even_mixer(self, l):
        i = l // 2
        ps = self.ps
        blks = self.blks
        wd = self.w_in_even[i]
        self.dma("pool", self.BSB(), self.bsb[i].partition_broadcast(128))
        self.dma("pool", self.LNG(), self.lng[i].partition_broadcast(128))
        if ps == 1:
            self.dma("pool", self.SI((0, 128)), self.st_b[i])
            self.dma("pool", self.FSI(), self.st_f[l].rearrange("p (j f) -> p j f", j=NJ))
        self.wprefetch()
        sv = [self.wget(self.w_in(wd, 512 + q * 128)) for q in range(4)]
        ntile = self.nt // 128
        VOFF = 4 * NTM

        def vacc(tt, c0, c1, p=None):
            lo = tt * 256 + c0 // 2
            hi = tt * 256 + c1 // 2
            a_ = self.CC.flat(lo, hi)
            return Acc(self.CC, a_.ap.bitcast(BF16), lo, hi)

        b_steps = []
        st_ = {"vrel": False, "defer": 0}

        def rel3():
            if st_["vrel"]:
                self.wrel(3)
            else:
                st_["defer"] += 3

        def mk_c(c):
            cx = {}

            def setup():
                cx["scg"] = self.wget(self.w_in(wd, 1536 + c * 128))
                cx["sh"] = self.wget(self.w_in(wd, 2048 + c * 128))
                cx["sbg"] = self.wget(self.w_in(wd, 1024 + c * 128))
                cx["R"] = self.rotate("r", self.R)
                cx["SRb"] = self.rotate("sr", self.SR)
                self.ACT(AF.Copy, cx["R"]((HO - 2, HO)), self.CB(i, c))
                if ps == 1:
                    self.ACT(AF.Copy, cx["SRb"](None, (SH - 2, SH)), self.SI((c * 32, (c + 1) * 32)).r("p (s r) -> p s r", s=16))

            def item(b):
                self.need(b)
                R, SRb = cx["R"], cx["SRb"]
                w0 = self.vec(V_BW + (i * 3 + 0) * 4 + c)
                w1 = self.vec(V_BW + (i * 3 + 1) * 4 + c)
                w2 = self.vec(V_BW + (i * 3 + 2) * 4 + c)
                pcg = self.pst(self.PSIN, "pin", b)
                self.MM(pcg, [(self.wk(cx["scg"], k), self.xcols(self.XN, k, b)) for k in range(KC)])
                ph = self.pst(self.PSIN, "pin", b)
                self.MM(ph, [(self.wk(cx["sh"], k), self.xcols(self.XN, k, b)) for k in range(KC)])
                pbg = self.pst(self.PSIN, "pin", b)
                self.MM(pbg, [(self.wk(cx["sbg"], k), self.xcols(self.XN, k, b)) for k in range(KC)])
                cg = self.tile(self.TA, "ta", b)
                self.ACT(AF.Copy, cg, pcg)
                self.TT("dve", self.row(R, SRb, b, 0), cg, ph, ALU.mult)
                t0 = self.tile(self.TA, "ta", b)
                self.ACT(AF.Identity, t0, self.row(R, SRb, b, 0), scale=w2)
                t1 = self.tile(self.TD, "td", b)
                self.STT("dve", t1, self.row(R, SRb, b, 1), w1, t0, ALU.mult, ALU.add)
                self.STT("dve", t1, self.row(R, SRb, b, 2), w0, t1, ALU.mult, ALU.add)
                self.TT("dve", self.xcols(self.YH, 4 + c, b), t1, pbg, ALU.mult)

            def finish():
                R, SRb = cx["R"], cx["SRb"]
                self.ACT(AF.Copy, self.CB(i, c), R((HO + NPP - 2, HO + NPP)))
                if ps == 1:
                    self.ACT(AF.Copy, self.SO((c * 32, (c + 1) * 32)).r("p (s r) -> p s r", s=16), SRb(None, (SH + 6, SH + 8)))
                rel3()
            b_steps.append(("setup", c, setup))
            for b in blks:
                b_steps.append(("item", c, (lambda b=b: item(b))))
            b_steps.append(("finish", c, finish))
        for c in range(4):
            mk_c(c)

        def b_advance():
            while b_steps:
                kind, c, fn = b_steps[0]
                if kind == "setup" and c >= 2 and not st_["vrel"]:
                    return
                b_steps.pop(0)
                fn()
                if kind == "item":
                    return
        for tt in range(ntile):
            self.need(blks[min(tt // 4, len(blks) - 1)])
            pv = self.rotate("pin", self.PSIN)
            runs = []
            q = 0
            while q < 4:
                L = 1
                while q + L < 4 and sv[q + L] == sv[q + L - 1] + 1:
                    L += 1
                runs.append((q, L))
                q += L
            for (q0, L) in runs:
                outv = pv((q0 * 128, (q0 + L) * 128)).r("p (s f) -> p s f", s=L)
                self.MM(outv, [(self.XN(k, (tt * 128, (tt + 1) * 128)),
                                self.RING((sv[q0], sv[q0] + L), (k * 128, (k + 1) * 128))) for k in range(KC)])
            gv = self.rotate("ta", self.TA)()
            self.ACT(AF.Gelu_apprx_tanh, gv, pv())
            st6 = self.rotate("st6", self.STT6)
            mv = self.rotate("mv", self.MV)
            self.op("dve", lambda e, st6=st6, gv=gv: e.bn_stats(out=st6((0, 6)).ap, in_=gv.ap), reads=[gv], writes=[st6()])
            self.op("dve", lambda e, st6=st6, mv=mv: e.bn_aggr(out=mv((0, 2)).ap, in_=st6((0, 6)).ap), reads=[st6()], writes=[mv()])
            self.TS("dve", mv((2, 3)), mv((1, 2)), EPS, ALU.add)
            self.TT("pool", mv((3, 4)), mv((2, 3)), self.NEGH((0, 1)), ALU.pow)
            is_s = (ps == 1 and tt == ntile - 1)
            if is_s:
                vn = self.rotate("td", self.TD)()
                self.TS("dve", vn, gv, mv((0, 1)), ALU.subtract, mv((3, 4)), ALU.mult)
                vf = self.rotate("td", self.TD)()
                self.TT("dve", vf, vn, self.LNG(), ALU.mult)
                self.dma("sp", self.o_av[i], vf)
                self.ACT(AF.Copy, vacc(tt, 0, 512), vn)
            else:
                self.TS("dve", vacc(tt, 0, 512), gv, mv((0, 1)), ALU.subtract, mv((3, 4)), ALU.mult)
            b_advance()
        self.wrel(4)
        st_["vrel"] = True
        if st_["defer"]:
            self.wrel(st_["defer"])
            st_["defer"] = 0
        self.lookahead = 0
        while b_steps:
            b_steps.pop(0)[2]()
        for h in range(4):
            su = self.wget(self.w_in(wd, h * 128))
            for b in blks:
                pu = self.pst(self.PSIN, "pin", b)
                self.MM(pu, [(self.wk(su, k), self.xcols(self.XN, k, b)) for k in range(KC)])
                u = self.tile(self.TA, "ta", b)
                self.ACT(AF.Gelu_apprx_tanh, u, pu)
                pmb = self.rotate("aux", self.PSAUX)
                kd = 1 if b.kind == "s" else 0
                items = []
                for q in range(b.n // 128):
                    tt = b.c0 // 128 + q
                    items.append((pmb((q * 128, (q + 1) * 128)), vacc(tt, h * 128, (h + 1) * 128), self.WST(i, h, kd)))
                self.MMS(items)
                pm = self.bv(pmb((0, b.n)), b)
                tmp = self.tile(self.TD, "td", b)
                if b.kind == "p":
                    b0 = self.BSB(h, (0, 128))
                    bs = Acc(self.BSB, b0.ap.rearrange("p (o t) -> p o t", o=1).broadcast_to([128, 4, 128]), b0.lo, b0.hi)
                    self.STT("dve", tmp.r("p (q t) -> p q t", q=4), pm.r("p (q t) -> p q t", q=4), self.vec(V_ALG + i * 4 + h), bs, ALU.mult, ALU.add)
                else:
                    bs = self.BSB(h, (128, 256)).s3()
                    self.STT("dve", tmp, pm, self.vec(V_ALG + i * 4 + h), bs, ALU.mult, ALU.add)
                self.TT("dve", self.xcols(self.YH, h, b), tmp, u, ALU.mult)
            self.wrel(1)
        if ps == 1:
            self.dma("sp", self.o_bp[i], self.CB(i).r("p c r -> p (c r)"))
            self.dma("sp", self.o_bs[i], self.SO((0, 128)))
        self.out_proj(self.w_out_even[i], lambda b: self.norm_block(V_GFFN + 8 * l, b))

    def odd_mixer(self, l):
        i = l // 2
        ps = self.ps
        blks = self.blks
        wd = self.w_in_odd[i]
        if ps == 1:
            self.dma("pool", self.FSI(), self.st_f[l].rearrange("p (j f) -> p j f", j=NJ))
        self.wprefetch()
        def d_in(g):
            sp_ = self.wget(self.w_in(wd, 1024 + g * 128))
            P = self.R[0]
            SP = self.SR[0]
            self.ACT(AF.Copy, P((HO - 15, HO)), self.CD(i, g))
            if ps == 1:
                self.dma("pool", self.SI((0, 240)), self.st_d[i, :, g])
                self.ACT(AF.Copy, SP(None, (SH - 15, SH)), self.SI((0, 240)).r("p (s r) -> p s r", s=16))
            for b in blks:
                pp = self.pst(self.PSIN, "pin", b)
                self.MM(pp, [(self.wk(sp_, k), self.xcols(self.XN, k, b)) for k in range(KC)])
                self.ACT(AF.Copy, self.row(P, SP, b, 0), pp)

        def d_rest(g):
            win = 2 << g
            P = self.R[0]
            Q1 = self.R[1]
            Q2 = self.R[2]
            SP = self.SR[0]
            SQ1 = self.SR[1]
            lo = HO - 15
            hi = HO + NPP
            src = P
            dsts = [Q1, Q2]
            sh = 1
            for m in range(g + 1):
                lo += sh
                dst = dsts[m % 2]
                self.TT("dve", dst((lo, hi)), src((lo, hi)), src((lo - sh, hi - sh)), ALU.add)
                src = dst
                sh *= 2
            RBr = self.rotate("rb", self.RB)
            SRBr = None
            self.STT("dve", RBr((HO, hi)), src((HO, hi)), 1.0 / win, P((HO, hi)), ALU.mult, ALU.subtract)
            if ps == 0:
                tmp = self.rotate("td", self.TD)((0, 16))
                self.TT("dve", tmp, src((HO, HO + 16)), self.INVC(g), ALU.mult)
                self.TT("dve", RBr((HO, HO + 16)), tmp, P((HO, HO + 16)), ALU.subtract)
            else:
                SRBr = self.rotate("srb", self.SRB)
                lo = SH - 15
                ssrc = SP
                sdsts = [SQ1, self.SR[2]]
                sh = 1
                for m in range(g + 1):
                    lo += sh
                    dst = sdsts[m % 2]
                    self.TT("dve", dst(None, (lo, SH + 8)), ssrc(None, (lo, SH + 8)), ssrc(None, (lo - sh, SH + 8 - sh)), ALU.add)
                    ssrc = dst
                    sh *= 2
                self.STT("dve", SRBr(None, (SH, SH + 8)), ssrc(None, (SH, SH + 8)), 1.0 / win, SP(None, (SH, SH + 8)), ALU.mult, ALU.subtract)
            for b in blks:
                pd = self.pst(self.PSAUX, "aux", b)
                rhs = RBr((HO + b.c0, HO + b.c0 + b.n)) if b.kind == "p" else SRBr(None, (SH, SH + 8))
                self.MM(pd, [(self.DPROJ(i, g), rhs)])
                self.ACT(AF.Identity, self.xcols(self.YH, 4 + g, b), pd, scale=self.vec(V_DSC + i * 4 + g))
            self.ACT(AF.Copy, self.CD(i, g), P((HO + NPP - 15, HO + NPP)))
            if ps == 1:
                self.ACT(AF.Copy, self.SO((0, 240)).r("p (s r) -> p s r", s=16), SP(None, (SH + 8 - 15, SH + 8)))
                self.dma("sp", self.o_ds[i, :, g], self.SO((0, 240)))
                if g == 3:
                    self.dma("sp", self.o_dp[i], self.CD(i).r("p c r -> p (c r)"))

        for c in range(4):
            sa = self.wget(self.w_in(wd, c * 128))
            sb_ = self.wget(self.w_in(wd, 512 + c * 128))
            DG = self.rotate("diag", self.DIAG)
            c0_ = V_CW + i * 124 + c
            cw = Acc(self.VEC, self.VEC.t[:, c0_:c0_ + 121:4].rearrange("p (k o) -> p k o", o=1).broadcast_to([128, 31, 128]), c0_, c0_ + 121)
            idb = Acc(self.IDENT, self.IDENT().ap.rearrange("p (o q) -> p o q", o=1).broadcast_to([128, 31, 128]), 0, 128)
            self.TT("dve", DG(), idb, cw, ALU.mult)
            RBr = self.rotate("rb", self.RB)
            SRf = self.rotate("sr", self.SR)
            SRBr = self.rotate("srb", self.SRB)
            self.ACT(AF.Copy, RBr((HO - 30, HO)), self.CC2(i, c))
            if ps == 1:
                self.dma("pool", self.SI((0, 480)), self.st_c[i, :, c])
                si3 = self.SI((0, 480)).r("p (s r) -> p s r", s=16)
                self.ACT(AF.Copy, SRf(None, (0, 30)), si3)
                self.ACT(AF.Identity, SRBr(None, (0, 30)), si3, scale=2.0)
            pend = None
            for bi, b in enumerate(blks):
                self.need(b)
                pa = self.pst(self.PSIN, "pin", b)
                self.MM(pa, [(self.wk(sa, k), self.xcols(self.XN, k, b)) for k in range(KC)])
                pb = self.pst(self.PSIN, "pin", b)
                self.MM(pb, [(self.wk(sb_, k), self.xcols(self.XN, k, b)) for k in range(KC)])
                th = self.tile(self.TA, "ta", b)
                self.ACT(AF.Tanh, th, pb, scale=0.5)
                self.STT("dve", self.row(RBr, SRBr, b, 0), th, 1.0, pa, ALU.add, ALU.mult)
                if b.kind == "s":
                    t2 = self.tile(self.TD, "td", b)
                    self.STT("dve", t2, th, 1.0, pa, ALU.add, ALU.mult)
                    self.ACT(AF.Identity, SRf(None, (SH, SH + 8)), t2, scale=0.5)
                elif ps == 1 and bi == 1:
                    t2 = self.rotate("td", self.TD)((0, 30))
                    self.STT("dve", t2, th_slice(th, 482, 512), 1.0, ps_slice(pa, 482, 512), ALU.add, ALU.mult)
                    self.ACT(AF.Identity, self.GT(i, c), t2, scale=0.5)
                if pend is not None:
                    pend()

                def conv(b=b, RBr=RBr, SRBr=SRBr, DG=DG, c=c):
                    pcv = self.pst(self.PSAUX, "aux", b)
                    pairs = []
                    for k in range(31):
                        if b.kind == "p":
                            rhs = RBr((HO + b.c0 - 30 + k, HO + b.c0 - 30 + k + b.n))
                        else:
                            rhs = SRBr(None, (k, k + 8))
                        pairs.append((DG(k), rhs))
                    self.MM(pcv, pairs)
                    self.ACT(AF.Identity, self.xcols(self.CC, c, b), pcv, scale=0.5, bias=self.vec(V_CB + i * 4 + c))
                pend = conv
            self.ACT(AF.Copy, self.CC2(i, c), RBr((HO + NPP - 30, HO + NPP)))
            if ps == 1:
                self.ACT(AF.Copy, self.SO((0, 480)).r("p (s r) -> p s r", s=16), SRf(None, (8, 38)))
                self.dma("sp", self.o_cs[i, :, c], self.SO((0, 480)))
                if c == 3:
                    self.dma("sp", self.o_cp[i], self.GT(i).r("p c r -> p (c r)"))
            d_in(c)
            pend()
            d_rest(c)
            self.wrel(3)
        ln_state = []
        for b in blks:
            n = b.n
            ntt = n // 128
            pbank = self.rotate("aux", self.PSAUX)
            pst_ = pbank((0, 8))
            for c in range(4):
                ccb = self.rotate("sq", self.SQ)((0, n))
                self.op("dve", lambda e, ccb=ccb, c=c, b=b: e.tensor_copy(out=ccb.ap, in_=self.CC(c, b.cols).ap), reads=[self.CC(c, b.cols)], writes=[ccb])
                sqb = self.rotate("sq2", self.SQ2)((0, n))
                self.ACT(AF.Square, sqb, self.CC(c, b.cols))

                def fn(e, c=c, ccb=ccb, sqb=sqb, pbank=pbank, ntt=ntt):
                    ins = None
                    for tt in range(ntt):
                        ins = e.matmul(pbank((tt, tt + 1)).ap, lhsT=ccb.ap[:, tt * 128:(tt + 1) * 128], rhs=self.ONES((0, 1)).ap,
                                       start=(c == 0 and tt == 0), stop=(c == 3), skip_group_check=True)
                    for tt in range(ntt):
                        ins = e.matmul(pbank((4 + tt, 5 + tt)).ap, lhsT=sqb.ap[:, tt * 128:(tt + 1) * 128], rhs=self.ONES((0, 1)).ap,
                                       start=False, stop=(c == 3), skip_group_check=True)
                    return ins
                self.op("pe", fn, reads=[ccb, sqb, self.ONES()], writes=[pst_])
            m4 = self.rotate("r4", self.R4)((0, ntt))
            self.TS("dve", m4, pbank((0, ntt)), 1.0 / 512, ALU.mult)
            v4 = self.rotate("r4", self.R4)((0, ntt))
            self.TT("dve", v4, m4, m4, ALU.mult)
            self.TS("dve", v4, v4, EPS, ALU.subtract)
            self.STT("dve", v4, pbank((4, 4 + ntt)), 1.0 / 512, v4, ALU.mult, ALU.subtract)
            self.TT("pool", v4, v4, self.NEGH((0, ntt)), ALU.pow)
            mr4 = self.rotate("r4", self.R4)((0, ntt))
            self.TT("dve", mr4, m4, v4, ALU.mult)
            ln_state.append((b, n, ntt, v4, mr4))
        for (b, n, ntt, v4, mr4) in ln_state:
            prR = self.bcast_rows(v4, ntt, n)
            prM = self.bcast_rows(mr4, ntt, n)
            pend = None
            for c in range(4):
                tn = self.rotate("td", self.TD)((0, n))
                self.TT("dve", tn, self.CC(c, b.cols), prR, ALU.mult)
                self.TT("dve", tn, tn, prM, ALU.subtract)
                zh = self.rotate("ta", self.TA)((0, n))
                self.ACT(AF.Identity, zh, tn, scale=self.vec(V_CLG + i * 4 + c), bias=self.vec(V_CLB + i * 4 + c))
                th = self.rotate("ta", self.TA)((0, n))
                self.ACT(AF.Tanh, th, tn, scale=self.vec(V_CLG + i * 4 + c), bias=self.vec(V_CLB + i * 4 + c))
                if pend is not None:
                    pend()
                pend = (lambda c=c, th=th, zh=zh, b=b: self.STT("dve", self.YH(c, b.cols), th, 1.0, zh, ALU.add, ALU.mult))
            pend()
        self.out_proj(self.w_out_odd[i], lambda b: self.norm_block(V_GFFN + 8 * l, b))

    def ffn(self, l):
        ps = self.ps
        blks = self.blks
        wd = self.w_ffn_in[l]
        wo = self.w_ffn_out[l]
        parts = [list(range(0, 4)), list(range(4, 8)), list(range(8, 12)), list(range(12, 16)),
                 list(range(16, 19)), list(range(19, 22))]
        self.wprefetch()

        def ffn_in(pi):
            part = parts[pi]
            pend = None
            for jj, j in enumerate(part):
                hs = (pi % 2) * 4 + jj
                sg = self.wget(self.w_in(wd, j * 128))
                su = self.wget(self.w_in(wd, DFF + j * 128))
                R = self.rotate("r", self.R)
                SRb = self.rotate("sr", self.SR)
                self.ACT(AF.Copy, R((HO - 2, HO)), self.CF(l, j))
                if ps == 1:
                    self.ACT(AF.Copy, SRb(None, (SH - 2, SH)), self.FSI(j).r("p (s r) -> p s r", s=16))
                w0 = self.vec(V_FW + (l * 3 + 0) * NJ + j)
                w1 = self.vec(V_FW + (l * 3 + 1) * NJ + j)
                w2 = self.vec(V_FW + (l * 3 + 2) * NJ + j)
                for b in blks:
                    self.need(b)
                    pg = self.pst(self.PSIN, "pin", b)
                    self.MM(pg, [(self.wk(sg, k), self.xcols(self.XN, k, b)) for k in range(KC)])
                    pu = self.pst(self.PSIN, "pin", b)
                    self.MM(pu, [(self.wk(su, k), self.xcols(self.XN, k, b)) for k in range(KC)])
                    self.ACT(AF.Copy, self.row(R, SRb, b, 0), pg)
                    t0 = self.tile(self.TA, "ta", b)
                    self.ACT(AF.Identity, t0, pg, scale=w2)
                    t1 = self.tile(self.TD, "td", b)
                    self.STT("dve", t1, self.row(R, SRb, b, 1), w1, t0, ALU.mult, ALU.add)
                    self.STT("dve", t1, self.row(R, SRb, b, 2), w0, t1, ALU.mult, ALU.add)
                    if pend is not None:
                        pend()

                    def fin(t1=t1, pu=pu, b=b, hs=hs):
                        gl = self.tile(self.TA, "ta", b)
                        self.ACT(AF.Gelu_apprx_tanh, gl, t1)
                        self.TT("dve", self.xcols(self.YH, hs, b), gl, pu, ALU.mult)
                    pend = fin
                self.ACT(AF.Copy, self.CF(l, j), R((HO + NPP - 2, HO + NPP)))
                if ps == 1:
                    self.ACT(AF.Copy, self.FSO(j).r("p (s r) -> p s r", s=16), SRb(None, (SH + 6, SH + 8)))
                self.wrel(2)
            if pend is not None:
                pend()

        def ffn_out(pi, last):
            part = parts[pi]
            slots = [self.wget(("out", wo[j * 128:(j + 1) * 128, :])) for j in part]
            prev = None
            for b in blks:
                for m in range(KC):
                    po = self.pst(self.PSAUX, "aux", b)
                    self.MM(po, [(self.RING(slots[jj], (m * 128, (m + 1) * 128)), self.xcols(self.YH, (pi % 2) * 4 + jj, b))
                                 for jj in range(len(part))])
                    xm = self.xcols(self.X, m, b)
                    self.TT("dve", xm, xm, po, ALU.add)
                if last and prev is not None:
                    self.next_norm(l, prev)
                prev = b
            if last:
                if l == self.depth - 1 or (l + 1) % 2 == 1:
                    self.next_norm(l, prev)
                else:
                    self.pending[prev.index] = (lambda prev=prev: self.next_norm(l, prev))
            self.wrel(len(part))

        ffn_in(0)
        for pi in range(len(parts)):
            if pi + 1 < len(parts):
                ffn_in(pi + 1)
            else:
                if ps == 1:
                    self.dma("sp", self.o_fp[l], self.CF(l).r("p c r -> p (c r)"))
                    self.dma("sp", self.o_fs[l], self.FSO().r("p j f -> p (j f)"))
            ffn_out(pi, pi == len(parts) - 1)


def th_slice(acc, a, b):
    return Acc(acc.buf, acc.ap[:, a:b], acc.lo + a, acc.lo + b)


def ps_slice(acc, a, b):
    return Acc(acc.buf, acc.ap[:, a:b], acc.lo + a, acc.lo + b)


def _consts():
    s = np.arange(128)
    maskp = (s[:, None] <= s[None, :]).astype(np.float32)
    masks = ((s[:, None] // 8 == s[None, :] // 8) & (s[:, None] % 8 <= s[None, :] % 8)).astype(np.float32)
    ident = np.eye(128, dtype=np.float32)
    invc = np.zeros((4, 16), np.float32)
    for g, win in enumerate((2, 4, 8, 16)):
        invc[g] = 1.0 / np.minimum(np.arange(16) + 1, win)
    return np.stack([maskp, masks]), ident, invc


def _pack_vecs(inp):
    def pk(a):
        a = np.asarray(a, np.float32)
        lead = a.shape[:-1]
        n = a.shape[-1] // 128
        a = a.reshape(lead + (n, 128))
        a = np.moveaxis(a, -1, 0)
        return a.reshape(128, -1)
    cols = [pk(inp["norm_mix_g"]), pk(inp["norm_ffn_g"]), pk(inp["norm_final_g"]),
            pk(inp["b_conv_w"]), pk(inp["c_conv_w"]), pk(inp["c_conv_b"]), pk(inp["c_ln_g"]),
            pk(inp["c_ln_b"]), pk(inp["d_scale"]), pk(inp["ffn_conv_w"]), pk(inp["a_ln_g"])]
    v = np.concatenate(cols, axis=1)
    assert v.shape == (128, NV), v.shape
    return np.ascontiguousarray(v)


_CACHE = {}


def kernel(**inp):
    inp = {k: np.asarray(v) for k, v in inp.items()}
    key = "nc"
    if key not in _CACHE:
        _CACHE[key] = Builder().build()
    nc = _CACHE[key]
    masks, ident, invc = _consts()
    vecs = _pack_vecs(inp)
    ws = inp["a_ws"].astype(np.float32)
    wst = np.zeros((2, 4, 2, 128, 128), np.float32)
    wst[:, :, 0] = np.swapaxes(ws, -1, -2)
    wst[:, :, 1] = np.tile(np.swapaxes(ws[:, :, :8, :8], -1, -2), (1, 1, 16, 16))
    bs = inp["a_bs"].astype(np.float32)
    bsb = np.concatenate([bs, np.tile(bs[:, :, :8], (1, 1, 16))], axis=-1)
    shared = dict(
        w_in_even=inp["w_in_even"], w_out_even=inp["w_out_even"], w_in_odd=inp["w_in_odd"],
        w_out_odd=inp["w_out_odd"], w_ffn_in=inp["w_ffn_in"], w_ffn_out=inp["w_ffn_out"],
        d_proj=inp["d_proj"], vecs=vecs, lng=inp["a_ln_g"], bsb=np.ascontiguousarray(bsb), wst=wst,
        masks=masks, ident=ident, invc=invc)
    shared = {k: np.ascontiguousarray(v, dtype=np.float32) for k, v in shared.items()}
    in_maps = []
    for c in range(NCORES):
        sl = slice(16 * c, 16 * c + 16)
        xs = inp["x_sample"][sl].reshape(128, D)
        xT = np.ascontiguousarray(np.concatenate([inp["x_prompt"][c], xs], axis=0).T, dtype=np.float32)
        m = dict(shared)
        m["xT"] = xT
        def st(a):
            n, _, r, ch = a.shape
            a = a.reshape(n, 16, r, ch // 128, 128)
            return np.ascontiguousarray(np.transpose(a, (0, 4, 3, 1, 2)), dtype=np.float32)
        m["st_b"] = st(inp["state_b_conv"][:, sl]).reshape(2, 128, 128)
        m["st_c"] = st(inp["state_c_conv"][:, sl]).reshape(2, 128, 4, 480)
        m["st_d"] = st(inp["state_d_pool"][:, sl]).reshape(2, 128, 4, 240)
        m["st_f"] = st(inp["state_ffn_conv"][:, sl]).reshape(4, 128, NJ * 32)
        in_maps.append(m)
    res = run_bass_kernel_spmd(nc, in_maps, core_ids=list(range(NCORES)))
    R = res.results
    y_prompt = np.stack([R[c]["yT"][:, :2048].T for c in range(NCORES)])
    y_sample = np.concatenate([R[c]["yT"][:, 2048:].T.reshape(16, 8, D) for c in range(NCORES)])
    a_s = np.concatenate([R[c]["o_av"].reshape(2, 16, 8, 512) for c in range(NCORES)], axis=1)

    def pr(name, C, r):
        outs_ = []
        for c in range(NCORES):
            a = R[c][name]
            n = a.shape[0]
            a = a.reshape(n, 128, C, r)
            outs_.append(np.transpose(a, (0, 3, 2, 1)).reshape(n, r, C * 128))
        return np.stack(outs_, axis=1)

    def sm(name, C, r):
        outs_ = []
        for c in range(NCORES):
            a = R[c][name]
            n = a.shape[0]
            a = a.reshape(n, 128, C, 16, r)
            outs_.append(np.transpose(a, (0, 3, 4, 2, 1)).reshape(n, 16, r, C * 128))
        return np.concatenate(outs_, axis=1)
    outs = (y_prompt, y_sample, a_s, pr("o_bp", 4, 2), sm("o_bs", 4, 2), pr("o_cp", 4, 30), sm("o_cs", 4, 30),
            pr("o_dp", 4, 15), sm("o_ds", 4, 15), pr("o_fp", NJ, 2), sm("o_fs", NJ, 2))
    return tuple(np.ascontiguousarray(o, dtype=np.float32) for o in outs)
```
